# Optimizing a Trainium2 kernel written in Bass

```python
import math
import jax, jax.numpy as jnp
from jax import lax
import numpy as np

D_MODEL = 1024
BATCH = 8
SEQ = 2048
DEPTH = 1

CHUNK = 64
Q_BLOCK = 128
MEM_LEN = 256
HEAD_DIM = 64
N_DIFF_HEADS = 4
DIFF_V_DIM = 2 * HEAD_DIM
DIFF_QK_WIDTH = N_DIFF_HEADS * 2 * HEAD_DIM
DIFF_WIDTH = N_DIFF_HEADS * DIFF_V_DIM
N_FOX_HEADS = 8
FOX_WIDTH = N_FOX_HEADS * HEAD_DIM
MIX_WIDTH = DIFF_WIDTH + FOX_WIDTH
ROPE_DIM = HEAD_DIM // 4
ROPE_THETA = 500000.0
N_MEM_HEADS = 4
MEM_HEAD_DIM = D_MODEL // N_MEM_HEADS
D_FF = 4 * D_MODEL
EPS = 1e-6
NEG_INF = -1e30

IN_SIZES = [DIFF_QK_WIDTH, DIFF_QK_WIDTH, DIFF_WIDTH, FOX_WIDTH, FOX_WIDTH, FOX_WIDTH, N_FOX_HEADS]
IN_OFFSETS = [int(o) for o in np.cumsum(IN_SIZES)[:-1]]
IN_COLS = int(sum(IN_SIZES))

kernel_name = "hybrid_diffattn_fox_memxattn_block"


def rms_norm(t, w):
    tf = t.astype(jnp.float32)
    tf = tf * lax.rsqrt(jnp.mean(tf * tf, axis=-1, keepdims=True) + EPS)
    return (tf * w.astype(jnp.float32)).astype(t.dtype)


def to_heads(t, n_heads):
    b, s, w = t.shape
    return t.reshape(b, s, n_heads, w // n_heads).transpose(0, 2, 1, 3)


def from_heads(t):
    b, n, s, d = t.shape
    return t.transpose(0, 2, 1, 3).reshape(b, s, n * d)


def rope_tables(positions):
    inv_freq = ROPE_THETA ** (-jnp.arange(0, ROPE_DIM, 2, dtype=jnp.float32) / ROPE_DIM)
    ang = positions.astype(jnp.float32)[..., None] * inv_freq
    return jnp.cos(ang)[:, None], jnp.sin(ang)[:, None]


def apply_partial_rope(t, cos, sin):
    tf = t.astype(jnp.float32)
    half = ROPE_DIM // 2
    t1, t2, rest = tf[..., :half], tf[..., half:ROPE_DIM], tf[..., ROPE_DIM:]
    out = jnp.concatenate([t1 * cos - t2 * sin, t2 * cos + t1 * sin, rest], axis=-1)
    return out.astype(t.dtype)


def chunk_causal_mask(q0, n_q, n_k):
    q_chunk = (q0 + jnp.arange(n_q)) // CHUNK
    k_chunk = jnp.arange(n_k) // CHUNK
    return k_chunk[None, :] <= q_chunk[:, None]


def frame_causal_mask(q0, n_q, n_k):
    return jnp.arange(n_k)[None, :] <= (q0 + jnp.arange(n_q))[:, None]


def masked_softmax(scores, mask):
    return jax.nn.softmax(jnp.where(mask, scores, NEG_INF), axis=-1)


def differential_attention(q1, q2, k1, k2, v, lam):
    seq = q1.shape[2]
    scale = HEAD_DIM ** -0.5
    outs = []
    for q0 in range(0, seq, Q_BLOCK):
        q_end = q0 + Q_BLOCK
        mask = chunk_causal_mask(q0, Q_BLOCK, q_end)
        s1 = jnp.einsum('bhqd,bhkd->bhqk', q1[:, :, q0:q_end], k1[:, :, :q_end]).astype(jnp.float32) * scale
        s2 = jnp.einsum('bhqd,bhkd->bhqk', q2[:, :, q0:q_end], k2[:, :, :q_end]).astype(jnp.float32) * scale
        a = masked_softmax(s1, mask) - lam * masked_softmax(s2, mask)
        outs.append(jnp.einsum('bhqk,bhkd->bhqd', a.astype(v.dtype), v[:, :, :q_end]))
    return jnp.concatenate(outs, axis=2)


def forgetting_attention(q, k, v, cum_logf):
    seq = q.shape[2]
    scale = HEAD_DIM ** -0.5
    outs = []
    for q0 in range(0, seq, Q_BLOCK):
        q_end = q0 + Q_BLOCK
        mask = frame_causal_mask(q0, Q_BLOCK, q_end)
        s = jnp.einsum('bhqd,bhkd->bhqk', q[:, :, q0:q_end], k[:, :, :q_end]).astype(jnp.float32) * scale
        s = s + (cum_logf[:, :, q0:q_end, None] - cum_logf[:, :, None, :q_end])
        p = masked_softmax(s, mask)
        outs.append(jnp.einsum('bhqk,bhkd->bhqd', p.astype(v.dtype), v[:, :, :q_end]))
    return jnp.concatenate(outs, axis=2)


def setup_inputs(seed: int = 0) -> dict:
    key = jax.random.key(seed)
    ks = jax.random.split(key, 32)
    f32 = jnp.float32

    def normal(k, shape, scale):
        return jax.random.normal(k, shape, f32) * scale

    def gain(k, shape):
        return 1.0 + 0.02 * jax.random.normal(k, shape, f32)

    D = D_MODEL
    return {
        "x": normal(ks[0], (BATCH, SEQ, D), 1.0),
        "mem": normal(ks[1], (BATCH, MEM_LEN, D), 1.0),
        "positions": jnp.arange(SEQ, dtype=jnp.int32)[None, :]
        + CHUNK * jax.random.randint(ks[2], (BATCH, 1), 0, 512, dtype=jnp.int32),
        "norm_mix_w": gain(ks[3], (DEPTH, D)),
        "w_in": normal(ks[4], (DEPTH, D, IN_COLS), D ** -0.5),
        "b_forget": jax.random.uniform(ks[5], (DEPTH, N_FOX_HEADS), f32, 2.0, 4.0),
        "diff_q_norm_w": gain(ks[6], (DEPTH, HEAD_DIM)),
        "diff_k_norm_w": gain(ks[7], (DEPTH, HEAD_DIM)),
        "lambda_q1": normal(ks[8], (DEPTH, HEAD_DIM), 0.1),
        "lambda_k1": normal(ks[9], (DEPTH, HEAD_DIM), 0.1),
        "lambda_q2": normal(ks[10], (DEPTH, HEAD_DIM), 0.1),
        "lambda_k2": normal(ks[11], (DEPTH, HEAD_DIM), 0.1),
        "diff_subln_w": gain(ks[12], (DEPTH, DIFF_V_DIM)),
        "fox_q_norm_w": gain(ks[13], (DEPTH, HEAD_DIM)),
        "fox_k_norm_w": gain(ks[14], (DEPTH, HEAD_DIM)),
        "w_out": normal(ks[15], (DEPTH, MIX_WIDTH, D), MIX_WIDTH ** -0.5),
        "norm_mem_q_w": gain(ks[16], (DEPTH, D)),
        "norm_mem_kv_w": gain(ks[17], (DEPTH, D)),
        "w_mem_q": normal(ks[18], (DEPTH, D, N_MEM_HEADS * MEM_HEAD_DIM), D ** -0.5),
        "w_mem_kv": normal(ks[19], (DEPTH, D, 2 * N_MEM_HEADS * MEM_HEAD_DIM), D ** -0.5),
        "mem_q_norm_w": gain(ks[20], (DEPTH, MEM_HEAD_DIM)),
        "mem_k_norm_w": gain(ks[21], (DEPTH, MEM_HEAD_DIM)),
        "w_mem_o": normal(ks[22], (DEPTH, N_MEM_HEADS * MEM_HEAD_DIM, D), (N_MEM_HEADS * MEM_HEAD_DIM) ** -0.5),
        "norm_mlp_w": gain(ks[23], (DEPTH, D)),
        "w_up": normal(ks[24], (DEPTH, D, D_FF), D ** -0.5),
        "w_down": normal(ks[25], (DEPTH, D_FF, D), D_FF ** -0.5),
    }


def reference(x, mem, positions, norm_mix_w, w_in, b_forget, diff_q_norm_w, diff_k_norm_w,
              lambda_q1, lambda_k1, lambda_q2, lambda_k2, diff_subln_w, fox_q_norm_w, fox_k_norm_w,
              w_out, norm_mem_q_w, norm_mem_kv_w, w_mem_q, w_mem_kv, mem_q_norm_w, mem_k_norm_w,
              w_mem_o, norm_mlp_w, w_up, w_down):
    cos, sin = rope_tables(positions)
    for l in range(DEPTH):
        lam_init = 0.8 - 0.6 * math.exp(-0.3 * l)

        h = rms_norm(x, norm_mix_w[l])
        proj = h @ w_in[l]
        dq, dk, dv, fq, fk, fv, f_logit = jnp.split(proj, IN_OFFSETS, axis=-1)

        dq = to_heads(dq, N_DIFF_HEADS)
        dk = to_heads(dk, N_DIFF_HEADS)
        dv = to_heads(dv, N_DIFF_HEADS)
        q1 = apply_partial_rope(rms_norm(dq[..., :HEAD_DIM], diff_q_norm_w[l]), cos, sin)
        q2 = apply_partial_rope(rms_norm(dq[..., HEAD_DIM:], diff_q_norm_w[l]), cos, sin)
        k1 = apply_partial_rope(rms_norm(dk[..., :HEAD_DIM], diff_k_norm_w[l]), cos, sin)
        k2 = apply_partial_rope(rms_norm(dk[..., HEAD_DIM:], diff_k_norm_w[l]), cos, sin)
        lam = (jnp.exp(jnp.sum(lambda_q1[l].astype(jnp.float32) * lambda_k1[l].astype(jnp.float32)))
               - jnp.exp(jnp.sum(lambda_q2[l].astype(jnp.float32) * lambda_k2[l].astype(jnp.float32)))
               + lam_init)
        o_diff = differential_attention(q1, q2, k1, k2, dv, lam)
        o_diff = rms_norm(o_diff, diff_subln_w[l]) * (1.0 - lam_init)

        log_f = jax.nn.log_sigmoid((f_logit + b_forget[l]).astype(jnp.float32))
        cum_logf = jnp.cumsum(log_f, axis=1).transpose(0, 2, 1)
        fq = rms_norm(to_heads(fq, N_FOX_HEADS), fox_q_norm_w[l])
        fk = rms_norm(to_heads(fk, N_FOX_HEADS), fox_k_norm_w[l])
        fv = to_heads(fv, N_FOX_HEADS)
        o_fox = forgetting_attention(fq, fk, fv, cum_logf)

        mixed = jnp.concatenate([from_heads(o_diff), from_heads(o_fox)], axis=-1)
        x = x + mixed @ w_out[l]

        hq = rms_norm(x, norm_mem_q_w[l])
        hm = rms_norm(mem, norm_mem_kv_w[l])
        mq = rms_norm(to_heads(hq @ w_mem_q[l], N_MEM_HEADS), mem_q_norm_w[l])
        mk, mv = jnp.split(hm @ w_mem_kv[l], 2, axis=-1)
        mk = rms_norm(to_heads(mk, N_MEM_HEADS), mem_k_norm_w[l])
        mv = to_heads(mv, N_MEM_HEADS)
        ms = jnp.einsum('bhqd,bhkd->bhqk', mq, mk).astype(jnp.float32) * (MEM_HEAD_DIM ** -0.5)
        mp = jax.nn.softmax(ms, axis=-1)
        mo = jnp.einsum('bhqk,bhkd->bhqd', mp.astype(mv.dtype), mv)
        x = x + from_heads(mo) @ w_mem_o[l]

        h = rms_norm(x, norm_mlp_w[l])
        x = x + jnp.square(jax.nn.relu(h @ w_up[l])) @ w_down[l]
    return x
```

```python
import math
import numpy as np
import concourse.bass as bass
import concourse.mybir as mybir
from concourse.bass_utils import run_bass_kernel_spmd

F32 = mybir.dt.float32
BF16 = mybir.dt.bfloat16
I32 = mybir.dt.int32
U8 = mybir.dt.uint8
AF = mybir.ActivationFunctionType
ALU = mybir.AluOpType
AX = mybir.AxisListType

S = 2048
D = 1024
NT = 16
KC = 8
MEM = 256
IN_COLS = 3080
EPS = 1e-6
LAM_INIT = 0.8 - 0.6 * math.exp(0.0)
N_CORES = 8


class Op:
    __slots__ = ("eng", "fn", "pos", "dma", "sig", "sem", "val", "vc", "waits", "id", "preds", "cost",
                 "nbytes", "is_out", "prio", "start", "fin", "succs", "npred", "opreds", "rdy")


class Prog:
    ENGS = ("pe", "act", "dve", "pool", "sp")
    SEM_LIMIT = 30000
    SCHEDULE = True

    def __init__(self, nc):
        self.nc = nc
        self.allops = []
        self.ops = {e: [] for e in self.ENGS}
        self.acc = {}
        self.dma_sems = {}
        self.out_dmas = []

    @staticmethod
    def _region(ap):
        sp = str(ap.space)
        if sp not in ("SB", "PSUM"):
            return None
        aps = ap.ap
        esz = mybir.dt.size(ap.dtype)
        pstride, npart = aps[0]
        off = ap.offset
        if pstride == 0:
            p0, f0 = 0, off
        else:
            p0, f0 = off // pstride, off % pstride
        dims = sorted([(abs(s_), c) for s_, c in aps[1:] if c > 1])
        ivs = [(0, 1)]
        for s_, c in dims:
            if s_ == 0:
                continue
            span = ivs[-1][1] - ivs[0][0]
            if s_ <= span or len(ivs) * c > 64:
                ivs = [(ivs[0][0], ivs[-1][1] + (c - 1) * s_)]
            else:
                ivs = [(lo + i * s_, hi + i * s_) for i in range(c) for lo, hi in ivs]
                ivs.sort()
        out = []
        for lo, hi in ivs:
            lo, hi = (f0 + lo) * esz, (f0 + hi) * esz
            if sp == "PSUM":
                lo = (lo // 2048) * 2048
                hi = ((hi + 2047) // 2048) * 2048
            if out and lo <= out[-1][1]:
                if hi > out[-1][1]:
                    out[-1] = (out[-1][0], hi)
            else:
                out.append((lo, hi))
        if sp == "PSUM":
            return (ap.tensor.name, True, 0, 128, tuple(out))
        return (ap.tensor.name, False, p0, p0 + npart, tuple(out))

    @staticmethod
    def _ov(a, b):
        if a[0][0] >= b[-1][1] or b[0][0] >= a[-1][1]:
            return False
        for lo, hi in a:
            for lo2, hi2 in b:
                if lo < hi2 and lo2 < hi:
                    return True
        return False

    @staticmethod
    def _cov(new, old):
        for lo, hi in old:
            ok = False
            for lo2, hi2 in new:
                if lo2 <= lo and hi <= hi2:
                    ok = True
                    break
            if not ok:
                return False
        return True

    def _access(self, op, ap, is_write, deps):
        r = self._region(ap)
        if r is None:
            return
        name, psum, p0, p1, ivs = r
        lst = self.acc.get(name)
        if lst is None:
            lst = []
            self.acc[name] = lst
        conflict_w = is_write or psum
        new = []
        mine = []
        for e in lst:
            eivs, ep0, ep1, eop, ew, ecw, more = e
            if eop is op:
                new.append(e)
                continue
            pov = ep0 < p1 and p0 < ep1
            ov = pov and self._ov(ivs, eivs)
            same = (eop.eng == op.eng) and not eop.dma and not op.dma
            if ov:
                if same:
                    if ew or is_write:
                        deps.append(eop)
                        deps.extend(more)
                elif conflict_w or ecw:
                    deps.append(eop)
                    deps.extend(more)
            cover = ov and p0 <= ep0 and ep1 <= p1 and self._cov(ivs, eivs)
            if cover and is_write:
                continue
            if cover and same and psum:
                op.opreds.append(eop)
                continue
            if cover and same and (not ew) and (not is_write):
                mine.append(eop)
                mine.extend(more)
                continue
            new.append(e)
        new.append((ivs, p0, p1, op, is_write, conflict_w, mine))
        self.acc[name] = new

    def add(self, eng, fn, reads, writes, dma=False, is_out=False, cost=100.0, nbytes=0):
        op = Op()
        op.eng, op.fn, op.dma, op.sig = eng, fn, dma, False
        op.id = len(self.allops)
        op.sem = None
        op.val = 0
        op.cost = cost
        op.nbytes = nbytes
        op.is_out = is_out
        op.opreds = []
        deps = []
        for ap in reads:
            if ap is not None and not isinstance(ap, (int, float)):
                self._access(op, ap, False, deps)
        for ap in writes:
            if ap is not None:
                self._access(op, ap, True, deps)
        seen = set()
        preds = []
        for a in deps:
            if a.id not in seen:
                seen.add(a.id)
                preds.append(a)
        op.preds = preds
        self.allops.append(op)
        return op

    def _schedule(self):
        import heapq
        ops = self.allops
        for op in ops:
            op.succs = []
        for op in ops:
            allp = {a.id: a for a in op.preds}
            for a in op.opreds:
                allp[a.id] = a
            op.npred = len(allp)
            for a in allp.values():
                a.succs.append(op)
        for op in reversed(ops):
            m = 0.0
            for s_ in op.succs:
                if s_.prio > m:
                    m = s_.prio
            lat = op.cost + (2000.0 + op.nbytes / 360.0 if op.dma else 0.0)
            op.prio = m + lat
        LAT = 400.0
        ready = {e: [] for e in self.ENGS}
        for op in ops:
            if op.npred == 0:
                heapq.heappush(ready[op.eng], (-op.prio, op.id, op))
        free = {e: 0.0 for e in self.ENGS}
        events = []
        pending = []
        for op in ops:
            op.rdy = 0.0
        order = {e: [] for e in self.ENGS}
        pipe_free = 0.0
        t = 0.0
        ndone = 0
        n = len(ops)
        while ndone < n:
            progressed = False
            for e in self.ENGS:
                if free[e] <= t and ready[e]:
                    _, _, op = heapq.heappop(ready[e])
                    op.start = t
                    if op.dma:
                        free[e] = t + op.cost
                        xs = max(t + op.cost, pipe_free)
                        pipe_free = xs + op.nbytes / 360.0
                        op.fin = pipe_free + 2000.0
                    else:
                        free[e] = t + op.cost
                        op.fin = free[e]
                    heapq.heappush(events, (op.fin, op.id, op))
                    order[e].append(op)
                    progressed = True
            cand = []
            if events:
                cand.append(events[0][0])
            if pending:
                cand.append(pending[0][0])
            for e in self.ENGS:
                if ready[e] and free[e] > t:
                    cand.append(free[e])
            if not progressed and not cand:
                raise RuntimeError("scheduler deadlock")
            if cand:
                nt = min(cand)
                if nt > t:
                    t = nt
            while events and events[0][0] <= t:
                _, _, op = heapq.heappop(events)
                ndone += 1
                for s_ in op.succs:
                    s_.npred -= 1
                    rt = op.fin if (s_.eng == op.eng and not op.dma) else op.fin + LAT
                    if rt > s_.rdy:
                        s_.rdy = rt
                    if s_.npred == 0:
                        if s_.rdy <= t:
                            heapq.heappush(ready[s_.eng], (-s_.prio, s_.id, s_))
                        else:
                            heapq.heappush(pending, (s_.rdy, s_.id, s_))
            while pending and pending[0][0] <= t:
                _, _, s_ = heapq.heappop(pending)
                heapq.heappush(ready[s_.eng], (-s_.prio, s_.id, s_))
        self.est_ns = t
        return order

    def _finalize(self):
        if self.SCHEDULE:
            order = self._schedule()
            glob = sorted(self.allops, key=lambda o: (o.start, o.id))
        else:
            order = {e: [o for o in self.allops if o.eng == e] for e in self.ENGS}
            glob = list(self.allops)
        self.ops = order
        for e in self.ENGS:
            for i, op in enumerate(order[e]):
                op.pos = i
        dma_last = {}
        dma_cnt = {}
        extra = {}
        for e in self.ENGS:
            pool = self.dma_sems.get(e)
            i = 0
            for op in order[e]:
                if not op.dma:
                    continue
                sem = pool[i % len(pool)]
                i += 1
                prev = dma_last.get(sem)
                if prev is not None:
                    extra[op.id] = prev
                dma_last[sem] = op
                dma_cnt[sem] = dma_cnt.get(sem, 0) + 16
                op.sem, op.val, op.sig = sem, dma_cnt[sem], True
                if op.is_out:
                    self.out_dmas.append(op)
        known = {e: {} for e in self.ENGS}
        for op in glob:
            kn = known[op.eng]
            deps = list(op.preds)
            if op.id in extra:
                deps.append(extra[op.id])
            deps.sort(key=lambda a: -a.pos)
            waits = []
            for a in deps:
                same = (a.eng == op.eng) and not a.dma and not op.dma
                if same and op.eng == "pe":
                    continue
                key = ("d", a.id) if a.dma else a.eng
                need = 1 if a.dma else a.pos
                if kn.get(key, -1) >= need:
                    continue
                waits.append(a)
                a.sig = True
                for k, v in a.vc.items():
                    if kn.get(k, -1) < v:
                        kn[k] = v
            op.waits = waits
            vc = dict(kn)
            if op.dma:
                vc[("d", op.id)] = 1
            else:
                vc[op.eng] = op.pos
            op.vc = vc

    def emit(self, block, sems_for_engine):
        self._finalize()
        for e in self.ENGS:
            pool = sems_for_engine[e]
            cnt, ep = 0, 0
            for op in self.ops[e]:
                if op.dma or not op.sig:
                    continue
                cnt += 1
                op.sem, op.val = pool[ep], cnt
                if cnt >= self.SEM_LIMIT:
                    cnt, ep = 0, ep + 1

        def run(engname, eng):
            for op in self.ops[engname]:
                best = {}
                for a in op.waits:
                    if best.get(a.sem, 0) < a.val:
                        best[a.sem] = a.val
                for sem, val in best.items():
                    eng.wait_ge(sem, val)
                ins = op.fn(eng)
                if op.sig:
                    ins.then_inc(op.sem, 16 if op.dma else 1)
            if engname == "sp":
                best = {}
                for a in self.out_dmas:
                    if best.get(a.sem, 0) < a.val:
                        best[a.sem] = a.val
                for sem, val in best.items():
                    eng.wait_ge(sem, val)

        @block.tensor
        def _(t):
            run("pe", t)

        @block.scalar
        def _(s):
            run("act", s)

        @block.vector
        def _(v):
            run("dve", v)

        @block.gpsimd
        def _(g):
            run("pool", g)

        @block.sync
        def _(sy):
            run("sp", sy)

    @staticmethod
    def _fs(ap):
        n = 1
        for s_ in ap.shape[1:]:
            n *= s_
        return n

    @staticmethod
    def _is_psum(ap):
        return str(ap.space) == "PSUM"

    def _vcost(self, eng, out, ins):
        n = self._fs(out)
        ps = any(self._is_psum(a) for a in ins if a is not None and not isinstance(a, (int, float))) or self._is_psum(out)
        if eng == "pool":
            return 150.0 + n * 1.9
        return (125.0 if ps else 65.0) + n * 1.04

    def mm(self, out, lhsT, rhs, start=True, stop=True, skip=False):
        n = self._fs(rhs)
        mult = 4.0 if rhs.dtype == F32 else 1.0
        return self.add("pe", lambda e: e.matmul(out, lhsT, rhs, start=start, stop=stop,
                                                 skip_group_check=skip), [lhsT, rhs], [out],
                        cost=mult * max(64, n) / 2.4 + 8.0)

    def tr(self, out, in_, ident):
        return self.add("pe", lambda e: e.transpose(out, in_, ident), [in_, ident], [out], cost=75.0)

    def act(self, out, in_, func, bias=0.0, scale=1.0, accum_out=None):
        rd = [in_]
        if not isinstance(bias, (int, float)):
            rd.append(bias)
        if not isinstance(scale, (int, float)):
            rd.append(scale)
        kw = {}
        if accum_out is not None:
            kw["accum_out"] = accum_out
        return self.add("act", lambda e: e.activation(out=out, in_=in_, func=func, bias=bias,
                                                      scale=scale, **kw), rd, [out, accum_out],
                        cost=200.0 + 0.8 * self._fs(in_))

    def tt(self, eng, out, in0, in1, op):
        return self.add(eng, lambda e: e.tensor_tensor(out=out, in0=in0, in1=in1, op=op),
                        [in0, in1], [out], cost=self._vcost(eng, out, [in0, in1]))

    def ts(self, eng, out, in0, s1, op0, s2=None, op1=None):
        rd = [in0]
        if not isinstance(s1, (int, float)):
            rd.append(s1)
        if s2 is not None and not isinstance(s2, (int, float)):
            rd.append(s2)
        c = self._vcost(eng, out, [in0])
        if op1 is None:
            return self.add(eng, lambda e: e.tensor_scalar(out=out, in0=in0, scalar1=s1, scalar2=None,
                                                           op0=op0), rd, [out], cost=c)
        return self.add(eng, lambda e: e.tensor_scalar(out=out, in0=in0, scalar1=s1, scalar2=s2,
                                                       op0=op0, op1=op1), rd, [out], cost=c)

    def stt(self, eng, out, in0, scalar, in1, op0, op1):
        rd = [in0, in1]
        if not isinstance(scalar, (int, float)):
            rd.append(scalar)
        return self.add(eng, lambda e: e.scalar_tensor_tensor(out=out, in0=in0, scalar=scalar, in1=in1,
                                                              op0=op0, op1=op1), rd, [out],
                        cost=self._vcost(eng, out, [in0, in1]))

    def copy(self, eng, out, in_):
        if eng == "act":
            return self.add("act", lambda e: e.activation(out=out, in_=in_, func=AF.Copy), [in_], [out],
                            cost=200.0 + 0.8 * self._fs(in_))
        return self.add(eng, lambda e: e.tensor_copy(out=out, in_=in_), [in_], [out],
                        cost=self._vcost(eng, out, [in_]))

    def memset(self, eng, ap, val):
        return self.add(eng, lambda e: e.memset(ap, val), [], [ap], cost=60.0 + 0.3 * self._fs(ap))

    def reduce(self, eng, out, in_, op=ALU.add):
        return self.add(eng, lambda e: e.tensor_reduce(out=out, in_=in_, axis=AX.X, op=op), [in_], [out],
                        cost=65.0 + 1.04 * self._fs(in_))

    def recip(self, out, in_):
        return self.add("dve", lambda e: e.reciprocal(out=out, in_=in_), [in_], [out],
                        cost=self._vcost("dve", out, [in_]))

    def affsel(self, out, in_, pattern, cmp, fill, base, cm):
        return self.add("pool", lambda e: e.affine_select(out=out, in_=in_, pattern=pattern, compare_op=cmp,
                                                          fill=fill, base=base, channel_multiplier=cm),
                        [in_], [out], cost=150.0 + 1.0 * self._fs(out))

    def dma(self, q, out, in_, is_out=False, slow=False):
        if slow:
            fn = lambda e: e.dma_start(out=out, in_=in_, allow_slow_non_contiguous=True)
        else:
            fn = lambda e: e.dma_start(out=out, in_=in_)
        nb = self._fs(out) * mybir.dt.size(out.dtype) * out.shape[0]
        return self.add(q, fn, [in_], [out], dma=True, is_out=is_out,
                        cost=(1000.0 if q == "pool" else 60.0), nbytes=nb)


def _view(base, off, shape, dt):
    esz = mybir.dt.size(dt)
    n = 1
    for s in shape[1:]:
        n *= s
    v = base[0:shape[0], off:off + n * esz].bitcast(dt)
    if len(shape) == 2:
        return v
    names = [f"d{i}" for i in range(1, len(shape))]
    pat = "p (" + " ".join(names) + ") -> p " + " ".join(names)
    kw = {names[i]: shape[i + 1] for i in range(len(names) - 1)}
    return v.rearrange(pat, **kw)


def _bc(ap, axis, n):
    shp = list(ap.shape)
    shp.insert(axis, n)
    return ap.unsqueeze(axis).broadcast_to(shp)


def build_program():
    nc = bass.Bass("TRN2", target_bir_lowering=False)

    def din(name, shape, dt=F32):
        return nc.dram_tensor(name, list(shape), dt, kind="ExternalInput").ap()

    x = din("x", [S, D])
    mem = din("mem", [MEM, D])
    positions = din("positions", [1, S], I32)
    norm_mix_w = din("norm_mix_w", [1, D])
    w_in = din("w_in", [D, IN_COLS])
    b_forget = din("b_forget", [1, 8])
    diff_q_norm_w = din("diff_q_norm_w", [1, 64])
    diff_k_norm_w = din("diff_k_norm_w", [1, 64])
    lambda_q1 = din("lambda_q1", [1, 64])
    lambda_k1 = din("lambda_k1", [1, 64])
    lambda_q2 = din("lambda_q2", [1, 64])
    lambda_k2 = din("lambda_k2", [1, 64])
    diff_subln_w = din("diff_subln_w", [1, 128])
    fox_q_norm_w = din("fox_q_norm_w", [1, 64])
    fox_k_norm_w = din("fox_k_norm_w", [1, 64])
    w_out = din("w_out", [D, D])
    norm_mem_q_w = din("norm_mem_q_w", [1, D])
    norm_mem_kv_w = din("norm_mem_kv_w", [1, D])
    w_mem_q = din("w_mem_q", [D, D])
    w_mem_kv = din("w_mem_kv", [D, 2 * D])
    mem_q_norm_w = din("mem_q_norm_w", [1, 256])
    mem_k_norm_w = din("mem_k_norm_w", [1, 256])
    w_mem_o = din("w_mem_o", [D, D])
    norm_mlp_w = din("norm_mlp_w", [1, D])
    w_up = din("w_up", [D, 4 * D])
    w_down = din("w_down", [4 * D, D])
    y = nc.dram_tensor("y", [S, D], F32, kind="ExternalOutput").ap()

    from contextlib import ExitStack
    with ExitStack() as es:
        def sb(name, shape, dt):
            return es.enter_context(nc.sbuf_tensor(name, list(shape), dt))

        RX = sb("RX", [128, 65536], U8)
        R2 = sb("R2", [128, 131072], U8)
        PS = es.enter_context(nc.psum_tensor("PS", [128, 4096], F32))
        ident = sb("ident", [128, 128], BF16)
        tri = sb("tri", [128, 128], F32)
        ones = sb("ones", [128, 128], F32)
        wcol_mix = sb("wcol_mix", [128, 8], F32)
        wcol_memq = sb("wcol_memq", [128, 8], F32)
        wcol_memkv = sb("wcol_memkv", [128, 8], F32)
        wcol_mlp = sb("wcol_mlp", [128, 8], F32)
        wq_adj = sb("wq_adj", [128, 1], F32)
        wk_adj = sb("wk_adj", [128, 1], F32)
        wfq = sb("wfq", [128, 1], F32)
        wfk = sb("wfk", [128, 1], F32)
        w16q = sb("w16q", [128, 16], F32)
        w16k = sb("w16k", [128, 16], F32)
        subln = sb("subln", [128, 128], F32)
        wcol_mq = sb("wcol_mq", [128, 2], F32)
        wcol_mk = sb("wcol_mk", [128, 2], F32)
        bfg = sb("bfg", [128, 8], F32)
        lamv = sb("lamv", [128, 4, 64], F32)
        lamt = sb("lamt", [128, 8], F32)
        posi = sb("posi", [128, 16], I32)
        posf = sb("posf", [128, 16], F32)
        invf = sb("invf", [128, 8], F32)
        ang = sb("ang", [128, 16, 8], F32)
        angk = sb("angk", [128, 16, 8], F32)
        angi = sb("angi", [128, 16, 8], I32)
        angr = sb("angr", [128, 16, 8], F32)
        angc = sb("angc", [128, 16, 8], F32)
        cosT = sb("cosT", [128, 16, 8], F32)
        sinT = sb("sinT", [128, 16, 8], F32)
        st1 = sb("st1", [128, 64], F32)
        st2 = sb("st2", [128, 64], F32)
        st3 = sb("st3", [128, 64], F32)
        nls = sb("nls", [128, 16, 8], F32)
        Tb = sb("Tb", [128, 16, 8], F32)
        Pinc = sb("Pinc", [128, 16, 8], F32)
        gcol = sb("gcol", [128, 16, 8], F32)
        gtmp = sb("gtmp", [128, 16, 8], F32)
        biasT = sb("biasT", [128, 16, 8, 8], F32)
        zt = sb("zt", [128, 8], F32)
        et = sb("et", [128, 8], F32)

        n_eng_sems = 1
        sem_ctx = {}
        for e in Prog.ENGS:
            sem_ctx[e] = [es.enter_context(nc.semaphore(f"s_{e}{i}")) for i in range(n_eng_sems)]
        P = Prog(nc)
        P.dma_sems["sp"] = [es.enter_context(nc.semaphore(f"d_sp{i}")) for i in range(12)]
        P.dma_sems["pool"] = [es.enter_context(nc.semaphore(f"d_pool{i}")) for i in range(12)]

        def bank(b, dt=F32):
            v = PS[:, b * 512:(b + 1) * 512]
            return v if dt == F32 else v.bitcast(dt)

        P.memset("pool", ident[:], 1.0)
        P.affsel(ident[:], ident[:], [[-1, 128]], ALU.is_equal, 0.0, 0, 1)
        P.memset("pool", tri[:], 1.0)
        P.affsel(tri[:], tri[:], [[1, 128]], ALU.is_ge, 0.0, 0, -1)
        P.memset("pool", ones[:], 1.0)

        identf = sb("identf", [128, 128], F32)
        P.memset("pool", identf[:], 1.0)
        P.affsel(identf[:], identf[:], [[-1, 128]], ALU.is_equal, 0.0, 0, 1)
        w8 = _view(RX, 0, [8, 6, 128], F32)
        w128 = _view(RX, 3072, [1, 4, 128], F32)
        posi16 = _view(RX, 5120, [16, 128], I32)
        posf16 = _view(RX, 5632, [16, 128], F32)
        pcol = [0]

        def col_load(dst, src, nk, slot):
            P.dma("sp", w8[0:nk, slot, :], src.rearrange("o (k q) -> (o k) q", q=128))
            c0 = pcol[0]
            pcol[0] += nk
            P.mm(bank(7)[:, c0:c0 + nk], w8[0:nk, slot, :], identf[0:nk, 0:nk], True, True)
            P.copy("dve", dst[:], bank(7)[:, c0:c0 + nk])

        def col64(dst, src, slot):
            P.dma("sp", w128[0:1, slot, 0:64], src)
            P.dma("sp", w128[0:1, slot, 64:128], src)
            c0 = pcol[0]
            pcol[0] += 1
            P.mm(bank(7)[:, c0:c0 + 1], w128[0:1, slot, :], ones[0:1, 0:1], True, True)
            P.copy("dve", dst[:, 0:1], bank(7)[:, c0:c0 + 1])

        col_load(wcol_mix, norm_mix_w, 8, 0)
        col_load(wcol_memq, norm_mem_q_w, 8, 1)
        col_load(wcol_memkv, norm_mem_kv_w, 8, 2)
        col_load(wcol_mlp, norm_mlp_w, 8, 3)
        col_load(wcol_mq, mem_q_norm_w, 2, 4)
        col_load(wcol_mk, mem_k_norm_w, 2, 5)
        for i, (dst, src) in enumerate(((wq_adj, diff_q_norm_w), (wk_adj, diff_k_norm_w), (wfq, fox_q_norm_w),
                                        (wfk, fox_k_norm_w))):
            col64(dst, src, i)
        for dst in (wq_adj, wk_adj):
            P.memset("pool", dst[0:16, :], 1.0)
            P.memset("pool", dst[64:80, :], 1.0)
        P.dma("sp", w16q[:], diff_q_norm_w[:, 0:16].partition_broadcast(128))
        P.dma("sp", w16k[:], diff_k_norm_w[:, 0:16].partition_broadcast(128))
        P.dma("sp", subln[:], diff_subln_w.partition_broadcast(128))
        P.dma("sp", bfg[:], b_forget.partition_broadcast(128))
        for i, src in enumerate((lambda_q1, lambda_k1, lambda_q2, lambda_k2)):
            P.dma("sp", lamv[:, i, :], src.partition_broadcast(128))
        P.dma("sp", posi16, positions.rearrange("o (t q) -> (o t) q", q=128))
        P.copy("dve", posf16, posi16)
        P.mm(bank(7)[:, 64:80], posf16, identf[0:16, 0:16], True, True)

        P.tt("dve", lamv[:, 0, :], lamv[:, 0, :], lamv[:, 1, :], ALU.mult)
        P.tt("dve", lamv[:, 2, :], lamv[:, 2, :], lamv[:, 3, :], ALU.mult)
        P.reduce("dve", lamt[:, 0:1], lamv[:, 0, :])
        P.reduce("dve", lamt[:, 1:2], lamv[:, 2, :])
        P.act(lamt[:, 2:4], lamt[:, 0:2], AF.Exp)
        P.tt("dve", lamt[:, 4:5], lamt[:, 3:4], lamt[:, 2:3], ALU.subtract)
        P.ts("dve", lamt[:, 4:5], lamt[:, 4:5], -LAM_INIT, ALU.add)
        P.ts("dve", subln[:], subln[:], 1.0 - LAM_INIT, ALU.mult)

        inv64 = 500000.0 ** (-np.arange(0, 16, 2, dtype=np.float64) / 16.0)
        inv_hi = inv64.astype(np.float32)
        inv_lo = (inv64 - inv_hi.astype(np.float64)).astype(np.float32)
        invf_lo = zt
        ang2 = gtmp
        for j in range(8):
            P.memset("pool", invf[:, j:j + 1], float(inv_hi[j]))
            P.memset("pool", invf_lo[:, j:j + 1], float(inv_lo[j]))
        P.copy("dve", posf[:], bank(7)[:, 64:80])
        P.tt("dve", ang[:], _bc(posf[:], 2, 8), _bc(invf[:], 1, 16), ALU.mult)
        P.tt("dve", ang2[:], _bc(posf[:], 2, 8), _bc(invf_lo[:], 1, 16), ALU.mult)
        C1 = 6.28125
        C2 = float(np.float32(2 * math.pi - C1))
        C3 = float(np.float32(2 * math.pi - C1 - C2))
        PI_SAFE = 3.1415925
        P.tt("dve", angk[:], ang[:], ang2[:], ALU.add)
        P.ts("dve", angk[:], angk[:], float(np.float32(1.0 / (2 * math.pi))), ALU.mult)
        P.copy("dve", angi[:], angk[:])
        P.copy("dve", angk[:], angi[:])
        P.stt("dve", angr[:], angk[:], -C1, ang[:], ALU.mult, ALU.add)
        P.tt("dve", angr[:], angr[:], ang2[:], ALU.add)
        P.stt("dve", angr[:], angk[:], -C2, angr[:], ALU.mult, ALU.add)
        P.stt("dve", angr[:], angk[:], -C3, angr[:], ALU.mult, ALU.add)
        P.ts("dve", angc[:], angr[:], math.pi / 2, ALU.is_gt)
        P.ts("dve", angk[:], angr[:], math.pi / 2, ALU.add)
        P.stt("dve", angc[:], angc[:], -2 * math.pi, angk[:], ALU.mult, ALU.add)
        P.ts("dve", angr[:], angr[:], -PI_SAFE, ALU.max, PI_SAFE, ALU.min)
        P.ts("dve", angc[:], angc[:], -PI_SAFE, ALU.max, PI_SAFE, ALU.min)
        P.act(sinT[:], angr[:], AF.Sin)
        P.act(cosT[:], angc[:], AF.Sin)

        QTd = _view(RX, 0, [128, 4, S], BF16)
        KTd = _view(RX, 16384, [128, 4, S], BF16)
        QTf = _view(RX, 32768, [128, 4, S], BF16)
        KTf = _view(RX, 49152, [128, 4, S], BF16)
        xres = _view(RX, 0, [128, NT, D], F32)

        w_in_sb = _view(R2, 0, [128, KC, IN_COLS], BF16)
        Vd = _view(R2, 50176, [128, NT, 4, 130], BF16)
        Vf = _view(R2, 66816, [128, NT, 8, 66], BF16)
        o1 = 83712
        XT = [_view(R2, o1 + i * 4096, [128, D], F32) for i in range(2)]
        XN = [_view(R2, o1 + 8192 + i * 2048, [128, D], BF16) for i in range(2)]
        HTt = [_view(R2, o1 + 12288 + i * 2048, [128, KC, 128], BF16) for i in range(2)]
        SQ = [_view(R2, o1 + 16384 + i * 2048, [128, 512], F32) for i in range(2)]
        QN = [_view(R2, o1 + 20480 + i * 4096, [128, 2048], BF16) for i in range(2)]
        ropeA = _view(R2, o1 + 28672, [128, 16, 16], F32)
        ropeT = [_view(R2, o1 + 29696 + i * 512, [128, 16, 8], F32) for i in range(4)]

        prev_grp = []
        for (c0, c1) in ((0, 1024), (1536, 2560), (1024, 1536), (2560, IN_COLS)):
            grp = []
            for kc in range(KC):
                o = P.dma("pool", w_in_sb[:, kc, c0:c1], w_in[kc * 128:(kc + 1) * 128, c0:c1])
                o.preds.extend(prev_grp)
                grp.append(o)
            prev_grp = grp
        P.memset("pool", Vd[:, :, :, 128:130], 1.0)
        P.memset("pool", Vf[:, :, :, 64:66], 1.0)

        rs_rr = [0]

        def rms_stats(src_ss, n, dst_rs, ncol):
            o = 16 * (rs_rr[0] % 4)
            rs_rr[0] += 1
            P.act(st3[:, o:o + ncol], src_ss, AF.Ln, bias=EPS, scale=1.0 / n)
            P.act(dst_rs, st3[:, o:o + ncol], AF.Exp, scale=-0.5)

        nrm_rr = [0]

        def norm_prep(src, xn):
            k = 4 * (nrm_rr[0] % 2)
            nrm_rr[0] += 1
            P.memset("dve", st1[:, k:k + 1], 0.0)
            P.act(xn, src, AF.Square, accum_out=st1[:, k:k + 1])
            P.act(st1[:, k + 2:k + 3], st1[:, k:k + 1], AF.Ln, bias=EPS, scale=1.0 / D)
            P.act(st1[:, k + 1:k + 2], st1[:, k + 2:k + 3], AF.Exp, scale=-0.5)
            P.ts("dve", xn, src, st1[:, k + 1:k + 2], ALU.mult)

        def norm_tr(xn, wcol, dst, pbank):
            pb = bank(pbank, BF16)
            for kc in range(KC):
                P.tr(pb[:, kc * 128:(kc + 1) * 128], xn[:, kc * 128:(kc + 1) * 128], ident[:])
            P.tt("dve", dst, pb.rearrange("p (k t) -> p k t", t=128), _bc(wcol[:], 2, 128), ALU.mult)

        def norm_transpose(src, xn, wcol, dst, pbank):
            norm_prep(src, xn)
            pb = bank(pbank, BF16)
            for kc in range(KC):
                P.tr(pb[:, kc * 128:(kc + 1) * 128], xn[:, kc * 128:(kc + 1) * 128], ident[:])
            P.tt("dve", dst, pb.rearrange("p (k t) -> p k t", t=128), _bc(wcol[:], 2, 128), ALU.mult)

        pb7 = bank(7, BF16)

        def stA0(t):
            P.dma("sp", XT[t % 2], x[t * 128:(t + 1) * 128, :])
            norm_prep(XT[t % 2], XN[t % 2])

        def stA1(t):
            norm_tr(XN[t % 2], wcol_mix, HTt[t % 2], 6)

        def proj(t, b, c0, n=512, col0=0):
            htt = HTt[t % 2]
            for kc in range(KC):
                P.mm(bank(b)[:, col0:col0 + n], htt[:, kc, :], w_in_sb[:, kc, c0:c0 + n],
                     start=(kc == 0), stop=(kc == KC - 1))

        def stB1(t):
            proj(t, 0, 0)
            proj(t, 1, 512)

        def stC1(t):
            qn = QN[t % 2]
            for i, b in enumerate((0, 1)):
                P.act(SQ[i], bank(b), AF.Square)
                P.reduce("dve", st2[:, i * 8:(i + 1) * 8], SQ[i].rearrange("p (g d) -> p g d", d=64))
            rms_stats(st2[:, 0:16], 64.0, st2[:, 16:32], 16)
            for i, b in enumerate((0, 1)):
                P.tt("dve", qn[:, i * 512:(i + 1) * 512].rearrange("p (g d) -> p g d", d=64),
                     bank(b).rearrange("p (g d) -> p g d", d=64),
                     _bc(st2[:, 16 + i * 8:24 + i * 8], 2, 64), ALU.mult)
                P.tt("dve", ropeA[:, i * 8:(i + 1) * 8, :],
                     bank(b).rearrange("p (g d) -> p g d", d=64)[:, :, 0:16],
                     _bc(st2[:, 16 + i * 8:24 + i * 8], 2, 16), ALU.mult)
            P.tt("pool", ropeA[:, 0:8, :], ropeA[:, 0:8, :], _bc(w16q[:], 1, 8), ALU.mult)
            P.tt("pool", ropeA[:, 8:16, :], ropeA[:, 8:16, :], _bc(w16k[:], 1, 8), ALU.mult)
            cs = _bc(cosT[:, t, :], 1, 16)
            sn = _bc(sinT[:, t, :], 1, 16)
            qv = qn[:, 0:1024].rearrange("p (g d) -> p g d", d=64)
            P.tt("pool", ropeT[0], ropeA[:, :, 0:8], cs, ALU.mult)
            P.tt("pool", ropeT[1], ropeA[:, :, 8:16], sn, ALU.mult)
            P.tt("pool", ropeT[2], ropeA[:, :, 8:16], cs, ALU.mult)
            P.tt("pool", ropeT[3], ropeA[:, :, 0:8], sn, ALU.mult)
            P.tt("pool", qv[:, :, 0:8], ropeT[0], ropeT[1], ALU.subtract)
            P.tt("pool", qv[:, :, 8:16], ropeT[2], ropeT[3], ALU.add)

        def stB2(t):
            proj(t, 2, 1536)
            proj(t, 3, 2048)

        def stC2(t):
            qn = QN[t % 2]
            for i, b in enumerate((2, 3)):
                P.act(SQ[i], bank(b), AF.Square)
                P.reduce("dve", st2[:, 32 + i * 8:40 + i * 8], SQ[i].rearrange("p (g d) -> p g d", d=64))
            rms_stats(st2[:, 32:48], 64.0, st2[:, 48:64], 16)
            for i, b in enumerate((2, 3)):
                P.tt("dve", qn[:, 1024 + i * 512:1536 + i * 512].rearrange("p (g d) -> p g d", d=64),
                     bank(b).rearrange("p (g d) -> p g d", d=64),
                     _bc(st2[:, 48 + i * 8:56 + i * 8], 2, 64), ALU.mult)

        def stD(t):
            qn = QN[t % 2]
            tcols = slice(t * 128, (t + 1) * 128)
            for (c0, dst, wc) in ((0, QTd, wq_adj), (512, KTd, wk_adj), (1024, QTf, wfq), (1536, KTf, wfk)):
                for j in range(4):
                    P.tr(pb7[:, j * 128:(j + 1) * 128], qn[:, c0 + j * 128:c0 + (j + 1) * 128], ident[:])
                P.ts("dve", dst[:, :, tcols], pb7[:, 0:512].rearrange("p (h t) -> p h t", t=128),
                     wc[:, 0:1], ALU.mult)

        def stB3(t):
            proj(t, 4, 1024)
            proj(t, 5, 2560)
            proj(t, 7, 3072, n=8, col0=256)

        def stC3(t):
            P.copy("act", Vd[:, t, :, 0:128], bank(4).rearrange("p (h d) -> p h d", d=128))
            P.copy("act", Vf[:, t, :, 0:64], bank(5).rearrange("p (h d) -> p h d", d=64))
            P.tt("dve", zt[:], bank(7)[:, 256:264], bfg[:], ALU.add)
            P.act(et[:], zt[:], AF.Exp, scale=-1.0)
            P.act(nls[:, t, :], et[:], AF.Ln, bias=1.0)

        stA0(0)
        stA1(0)
        for t in range(NT):
            if t + 1 < NT:
                stA0(t + 1)
            stB1(t)
            if t + 1 < NT:
                stA1(t + 1)
            stC1(t)
            stB2(t)
            stC2(t)
            if t >= 1:
                stD(t - 1)
            stB3(t)
            stC3(t)
        stD(NT - 1)

        nls2 = nls[:].rearrange("p t h -> p (t h)")
        P.mm(bank(7)[:, 0:128], tri[:], nls2, True, True)
        P.mm(bank(6)[:, 0:128], ones[:], nls2, True, True)
        P.copy("dve", Tb[:].rearrange("p t h -> p (t h)"), bank(6)[:, 0:128])
        P.copy("dve", Pinc[:, 0, :], Tb[:, 0, :])
        for j in range(1, NT):
            P.tt("dve", Pinc[:, j, :], Pinc[:, j - 1, :], Tb[:, j, :], ALU.add)
        P.tt("dve", gtmp[:], Pinc[:], Tb[:], ALU.subtract)
        P.tt("dve", gcol[:].rearrange("p t h -> p (t h)"), bank(7)[:, 0:128],
             gtmp[:].rearrange("p t h -> p (t h)"), ALU.add)
        Gmid = Pinc[:].rearrange("p (c four) h -> p c four h", four=4)[:, :, 1, :]
        for kb in range(NT):
            P.tt("dve", biasT[:, kb, 0:4, :], _bc(gcol[:, kb, :], 1, 4), Gmid, ALU.subtract)

        mixedT = _view(R2, 0, [128, KC, S], BF16)
        MIXTOK = [_view(R2, 32768, [128, 4, D], BF16), _view(R2, 122880, [128, 4, D], BF16)]
        PT = [_view(R2, 40960 + i * 1024, [128, 512], BF16) for i in range(3)]
        w_out_sb = _view(R2, 83712, [128, KC, D], BF16)
        e0 = 83712 + 16384
        XT4 = [_view(R2, e0 + i * 4096, [128, D], F32) for i in range(2)]
        QDP = _view(R2, e0, [128, 4, 2, 512], BF16)
        QFP = _view(R2, e0 + 8192, [128, 4, 2, 512], BF16)
        e1 = e0 + 16384
        EPB = [[_view(R2, e1 + k * 2048, [128, 4, 128], F32) for k in range(3)] for i in range(2)]
        assert e1 + 3 * 2048 <= 122880
        P.memset("pool", QDP[:], 0.0)
        P.memset("pool", QFP[:], 0.0)

        def mk_qpad(c, fox):
            def fn():
                for m in range(2):
                    rows = slice(m * 64, (m + 1) * 64)
                    if fox:
                        P.copy("dve", QFP[rows, :, m, :], QTf[rows, :, c * 512:(c + 1) * 512])
                    else:
                        P.copy("dve", QDP[rows, :, m, :], QTd[rows, :, c * 512:(c + 1) * 512])
            return fn

        mk_qpad(0, False)()
        mk_qpad(0, True)()

        for kc in range(KC):
            P.dma("pool", w_out_sb[:, kc, :], w_out[kc * 128:(kc + 1) * 128, :])

        steps = []
        deferred = {}

        def defer(idx, fn):
            deferred.setdefault(idx, []).append(fn)

        def mk_diff_step(i, c, h, m, kb, accv):
            rows = slice(m * 64, (m + 1) * 64)
            qlo = max(kb, 4 * c)
            j0 = qlo - 4 * c
            ncol = (4 - j0) * 128
            sb_ = 4 + i % 3
            pt = PT[i % 3]

            def st_fn():
                P.mm(bank(sb_)[:, 0:ncol], KTd[:, h, kb * 128:(kb + 1) * 128],
                     QDP[:, h, m, j0 * 128:512], True, True)

            def rest_fn():
                P.act(pt[:, 0:ncol], bank(sb_)[:, 0:ncol], AF.Exp, scale=0.125)
                if kb >= 4 * c:
                    P.memset("pool", pt[64:128, 0:64], 0.0)
                for j in range(j0, 4):
                    P.mm(accv[:, j, 0:129], pt[:, (j - j0) * 128:(j - j0 + 1) * 128],
                         Vd[:, kb, h, 0:129], start=(kb == 0 and j in (0, 2)),
                         stop=(kb == 4 * c + j), skip=True)
            return st_fn, rest_fn

        def mk_fox_step(i, c, h, kb, accv):
            pr = h // 2
            qlo = max(kb, 4 * c)
            j0 = qlo - 4 * c
            ncol = (4 - j0) * 128
            sb_ = 4 + i % 3
            pt = PT[i % 3]

            def st_fn():
                P.mm(bank(sb_)[:, 0:ncol], KTf[:, pr, kb * 128:(kb + 1) * 128],
                     QFP[:, pr, h % 2, j0 * 128:512], True, True)

            def rest_fn():
                P.act(pt[:, 0:ncol], bank(sb_)[:, 0:ncol], AF.Exp, scale=0.125,
                      bias=biasT[:, kb, c, h:h + 1])
                if kb >= 4 * c:
                    P.affsel(pt[:, 0:128], pt[:, 0:128], [[1, 128]], ALU.is_ge, 0.0, 0, -1)
                for j in range(j0, 4):
                    P.mm(accv[:, j, 0:65], pt[:, (j - j0) * 128:(j - j0 + 1) * 128],
                         Vf[:, kb, h, 0:65], start=(kb == 0 and j == 0),
                         stop=(kb == 4 * c + j), skip=True)
            return st_fn, rest_fn

        def mk_diff_ep(c, h, accs, par):
            epT, epU, epS = EPB[par]
            mixtok = MIXTOK[c % 2]
            so = 8 + 16 * (h % 2)

            def ep0():
                P.recip(st1[:, so:so + 4], accs[0][:, :, 128])
                P.tt("dve", epU[:], accs[0][:, :, 0:128], _bc(st1[:, so:so + 4], 2, 128), ALU.mult)

            def ep1():
                P.recip(st1[:, so + 4:so + 8], accs[1][:, :, 128])
                P.ts("dve", st1[:, so + 4:so + 8], st1[:, so + 4:so + 8], lamt[:, 4:5], ALU.mult)
                P.tt("dve", epT[:], accs[1][:, :, 0:128], _bc(st1[:, so + 4:so + 8], 2, 128), ALU.mult)
                P.tt("dve", epU[:], epU[:], epT[:], ALU.add)
                P.tt("pool", epS[:], epU[:], epU[:], ALU.mult)
                P.reduce("dve", st1[:, so + 8:so + 12], epS[:])

            def ep2():
                rms_stats(st1[:, so + 8:so + 12], 128.0, st1[:, so + 12:so + 16], 4)
                P.tt("dve", epU[:], epU[:], _bc(st1[:, so + 12:so + 16], 2, 128), ALU.mult)
                P.tt("pool", mixtok[:, :, h * 128:(h + 1) * 128], epU[:], _bc(subln[:], 1, 4), ALU.mult)
            return ep0, ep1, ep2

        def mk_fox_ep(c, h, accv, par):
            mixtok = MIXTOK[c % 2]
            so = 40 + 4 * par

            def ep():
                P.recip(st1[:, so:so + 4], accv[:, :, 64])
                P.tt("dve", mixtok[:, :, 512 + h * 64:512 + (h + 1) * 64],
                     accv[:, :, 0:64], _bc(st1[:, so:so + 4], 2, 64), ALU.mult)
            return ep

        def mk_mix_tr(c):
            mixtok = MIXTOK[c % 2]

            def fn():
                for tl in range(4):
                    for kc in range(KC):
                        P.tr(pb7[:, kc * 128:(kc + 1) * 128], mixtok[:, tl, kc * 128:(kc + 1) * 128], ident[:])
                    tg = c * 4 + tl
                    P.copy("dve", mixedT[:, :, tg * 128:(tg + 1) * 128], pb7.rearrange("p (k t) -> p k t", t=128))
            return fn

        nfox = 0
        for c in range(4):
            for h in range(4):
                accs = [PS[:, m * 1024:(m + 1) * 1024].rearrange("p (q n) -> p q n", n=256) for m in range(2)]
                ep0, ep1, ep2 = mk_diff_ep(c, h, accs, 0)
                for m in range(2):
                    for kb in range(4 * c + 4):
                        steps.append(mk_diff_step(len(steps), c, h, m, kb, accs[m]))
                    if m == 0:
                        defer(len(steps) - 1, ep0)
                defer(len(steps) - 1, ep1)
                defer(len(steps) - 1, ep2)
            if c + 1 < 4:
                defer(len(steps) - 1, mk_qpad(c + 1, False))
            for h in range(8):
                accv = bank(nfox % 4).rearrange("p (q n) -> p q n", n=128)
                for kb in range(4 * c + 4):
                    steps.append(mk_fox_step(len(steps), c, h, kb, accv))
                defer(len(steps) - 1, mk_fox_ep(c, h, accv, nfox % 2))
                nfox += 1
            if c + 1 < 4:
                defer(len(steps) - 1, mk_qpad(c + 1, True))
            defer(len(steps) - 1 + 8, mk_mix_tr(c))

        LA = 2
        nst = len(steps)
        endi = max(nst, max(deferred) + 1)
        for i in range(endi + LA):
            if i < nst:
                steps[i][0]()
            j = i - LA
            if j >= 0:
                if j < nst:
                    steps[j][1]()
                for fn in deferred.get(j, []):
                    fn()

        o5 = 83712
        w_kv_sb = _view(R2, 50176, [128, KC, 2 * D], BF16)
        w_mq_sb = _view(R2, 32768, [128, KC, D], BF16)
        w_mo_sb = _view(R2, 0, [128, KC, D], BF16)
        for kc in range(KC):
            P.dma("pool", w_kv_sb[:, kc, :], w_mem_kv[kc * 128:(kc + 1) * 128, :])
        for kc in range(KC):
            P.dma("pool", w_mq_sb[:, kc, :], w_mem_q[kc * 128:(kc + 1) * 128, :])

        for t in range(NT):
            xt = XT4[t % 2]
            P.dma("sp", xt, x[t * 128:(t + 1) * 128, :])
            for hf in range(2):
                b = 2 * (t % 2) + hf
                for kc in range(KC):
                    P.mm(bank(b), mixedT[:, kc, t * 128:(t + 1) * 128], w_out_sb[:, kc, hf * 512:(hf + 1) * 512],
                         start=(kc == 0), stop=(kc == KC - 1))
                P.tt("dve", xres[:, t, hf * 512:(hf + 1) * 512], bank(b), xt[:, hf * 512:(hf + 1) * 512], ALU.add)

        for kc in range(KC):
            P.dma("pool", w_mo_sb[:, kc, :], w_mem_o[kc * 128:(kc + 1) * 128, :])
        hmT = _view(R2, o5, [128, KC, MEM], BF16)
        mkT = _view(R2, o5 + 4096, [128, 8, MEM], BF16)
        mv = _view(R2, o5 + 8192, [128, 2, 4, 258], BF16)
        o5b = o5 + 8192 + 4128
        MT = [_view(R2, o5b + i * 4096, [128, D], F32) for i in range(2)]
        SQ5 = _view(R2, o5b + 8192, [128, D], F32)
        PT5 = [_view(R2, o5b + 12288 + i * 1024, [128, 512], BF16) for i in range(2)]
        motok = _view(R2, o5b + 14336, [128, 4, D], BF16)
        XN5 = [_view(R2, o5b + 22528 + i * 2048, [128, D], BF16) for i in range(2)]
        MQN = [_view(R2, o5b + 26624 + i * 2048, [128, D], BF16) for i in range(2)]
        MQT = [_view(R2, 16384 + i * 8192, [128, 8, 512], BF16) for i in range(2)]
        moT = _view(R2, o5b, [128, KC, 512], BF16)
        assert o5b + 30720 <= 131072

        P.memset("pool", mv[:, :, :, 256:258], 1.0)

        def head_norm(src_banks, sq, dst_bf, stcol):
            for i, b in enumerate(src_banks):
                P.act(sq[:, i * 512:(i + 1) * 512], bank(b), AF.Square)
            P.reduce("dve", st2[:, stcol:stcol + 4], sq.rearrange("p (g d) -> p g d", d=256))
            rms_stats(st2[:, stcol:stcol + 4], 256.0, st2[:, stcol + 4:stcol + 8], 4)
            for i, b in enumerate(src_banks):
                P.tt("dve", dst_bf[:, i * 512:(i + 1) * 512].rearrange("p (g d) -> p g d", d=256),
                     bank(b).rearrange("p (g d) -> p g d", d=256),
                     _bc(st2[:, stcol + 4 + 2 * i:stcol + 6 + 2 * i], 2, 256), ALU.mult)

        for mt in range(2):
            mtile = MT[mt]
            P.dma("sp", mtile, mem[mt * 128:(mt + 1) * 128, :])
            norm_transpose(mtile, XN5[mt], wcol_memkv, hmT[:, :, mt * 128:(mt + 1) * 128], 6)
        for mt in range(2):
            for g in range(4):
                for kc in range(KC):
                    P.mm(bank(g), hmT[:, kc, mt * 128:(mt + 1) * 128], w_kv_sb[:, kc, g * 512:(g + 1) * 512],
                         start=(kc == 0), stop=(kc == KC - 1))
            head_norm((0, 1), SQ5, MQN[mt], 0)
            P.copy("act", mv[:, mt, 0:2, 0:256], bank(2).rearrange("p (h d) -> p h d", d=256))
            P.copy("act", mv[:, mt, 2:4, 0:256], bank(3).rearrange("p (h d) -> p h d", d=256))
            pb7 = bank(7, BF16)
            for j in range(8):
                P.tr(pb7[:, j * 128:(j + 1) * 128], MQN[mt][:, j * 128:(j + 1) * 128], ident[:])
            pv = pb7.rearrange("p (h f t) -> p h f t", f=2, t=128)
            dv_ = mkT[:, :, mt * 128:(mt + 1) * 128].rearrange("p (h f) t -> p h f t", f=2)
            for f in range(2):
                P.ts("dve", dv_[:, :, f, :], pv[:, :, f, :], wcol_mk[:, f:f + 1], ALU.mult)

        SQH = [SQ5[:, 0:512], SQ5[:, 512:1024]]
        XN5b = [_view(R2, 50176 + i * 2048, [128, D], BF16) for i in range(3)]
        HQTb = [_view(R2, 50176 + 6144 + i * 2048, [128, KC, 128], BF16) for i in range(3)]
        MQNb = [_view(R2, 50176 + 12288 + i * 2048, [128, D], BF16) for i in range(3)]
        QF = [_view(R2, 50176 + 18432 + i * 2048, [128, 512], F32) for i in range(3)]
        MOTOK = [motok, _view(R2, 50176 + 24576, [128, 4, D], BF16)]
        MOT = [moT, _view(R2, o5b + 22528, [128, KC, 512], BF16)]
        accv5 = PS[:, 4 * 512:6 * 512].rearrange("p (q n) -> p q n", n=256)
        def q_stage(c):
            mq = MQT[c % 2]
            for tl in range(4):
                t = c * 4 + tl
                xn, hq, mqn = XN5b[t % 3], HQTb[t % 3], MQNb[t % 3]
                norm_prep(xres[:, t, :], xn)
                norm_tr(xn, wcol_memq, hq, 6)
                for g in range(2):
                    qf = QF[(2 * t + g) % 3]
                    for kc in range(KC):
                        P.mm(bank(g), hq[:, kc, :], w_mq_sb[:, kc, g * 512:(g + 1) * 512],
                             start=(kc == 0), stop=(kc == KC - 1))
                    s0 = 16 * (t % 2) + 4 * g
                    P.memset("dve", st2[:, s0:s0 + 2], 0.0)
                    for hh in range(2):
                        P.act(SQH[g][:, hh * 256:(hh + 1) * 256], bank(g)[:, hh * 256:(hh + 1) * 256], AF.Square,
                              accum_out=st2[:, s0 + hh:s0 + hh + 1])
                    P.copy("act", qf, bank(g))
                    rms_stats(st2[:, s0:s0 + 2], 256.0, st2[:, s0 + 2:s0 + 4], 2)
                    P.tt("pool", mqn[:, g * 512:(g + 1) * 512].rearrange("p (g d) -> p g d", d=256),
                         qf.rearrange("p (g d) -> p g d", d=256),
                         _bc(st2[:, s0 + 2:s0 + 4], 2, 256), ALU.mult)
                pbq = bank(7, BF16)
                for j in range(8):
                    P.tr(pbq[:, j * 128:(j + 1) * 128], mqn[:, j * 128:(j + 1) * 128], ident[:])
                pv = pbq.rearrange("p (h f t) -> p h f t", f=2, t=128)
                dv_ = mq[:, :, tl * 128:(tl + 1) * 128].rearrange("p (h f) t -> p h f t", f=2)
                for f in range(2):
                    P.ts("dve", dv_[:, :, f, :], pv[:, :, f, :], wcol_mq[:, f:f + 1], ALU.mult)
        def heads_stage(c):
            mq = MQT[c % 2]
            motok = MOTOK[c % 2]
            moT = MOT[c % 2]
            for h in range(4):
                sumv = bank(2)[:, 8 * h:8 * h + 4]
                for mt in range(2):
                    pt = PT5[mt]
                    for f in range(2):
                        P.mm(bank(7), mkT[:, 2 * h + f, mt * 128:(mt + 1) * 128], mq[:, 2 * h + f, :],
                             start=(f == 0), stop=(f == 1))
                    P.act(pt[:], bank(7), AF.Exp, scale=1.0 / 16.0)
                    for tl in range(4):
                        P.mm(accv5[:, tl, :], pt[:, tl * 128:(tl + 1) * 128], mv[:, mt, h, 0:256],
                             start=(mt == 0 and tl in (0, 2)), stop=(mt == 1), skip=True)
                        P.mm(sumv[:, tl:tl + 1], pt[:, tl * 128:(tl + 1) * 128], mv[:, mt, h, 256:257],
                             start=(mt == 0 and tl == 0 and h == 0), stop=(mt == 1), skip=True)
                so = 32 + 4 * (h % 2)
                P.recip(st1[:, so:so + 4], sumv)
                P.tt("dve", motok[:, :, h * 256:(h + 1) * 256], accv5, _bc(st1[:, so:so + 4], 2, 256), ALU.mult)
            pb6 = bank(6, BF16)
            for tl in range(4):
                for kc in range(KC):
                    P.tr(pb6[:, kc * 128:(kc + 1) * 128], motok[:, tl, kc * 128:(kc + 1) * 128], ident[:])
                P.copy("act", moT[:, :, tl * 128:(tl + 1) * 128], pb6.rearrange("p (k t) -> p k t", t=128))
            for tl in range(4):
                t = c * 4 + tl
                for hf in range(2):
                    for kc in range(KC):
                        P.mm(bank(3), moT[:, kc, tl * 128:(tl + 1) * 128], w_mo_sb[:, kc, hf * 512:(hf + 1) * 512],
                             start=(kc == 0), stop=(kc == KC - 1))
                    P.tt("dve", xres[:, t, hf * 512:(hf + 1) * 512], bank(3),
                         xres[:, t, hf * 512:(hf + 1) * 512], ALU.add)

        q_stage(0)
        for c in range(4):
            if c + 1 < 4:
                q_stage(c + 1)
            heads_stage(c)

        hT = _view(R2, 0, [128, KC, S], BF16)
        WU = [_view(R2, 32768 + i * 16384, [128, KC, 1024], BF16) for i in range(2)]
        WD = [_view(R2, 65536 + i * 16384, [128, 8, 1024], BF16) for i in range(2)]
        AT = [_view(R2, 98304 + i * 8192, [128, 8, 512], BF16) for i in range(2)]
        OUTS = [_view(R2, 114688 + i * 4096, [128, D], F32) for i in range(2)]
        RL = [_view(R2, 122880 + i * 2048, [128, 512], F32) for i in range(2)]
        XN6 = [_view(R2, 126976 + i * 2048, [128, D], BF16) for i in range(2)]

        def load_mlp_w(qf):
            for kc in range(KC):
                P.dma("pool", WU[qf % 2][:, kc, :], w_up[kc * 128:(kc + 1) * 128, qf * 1024:(qf + 1) * 1024])
            for fc in range(8):
                r0 = qf * 1024 + fc * 128
                P.dma("pool", WD[qf % 2][:, fc, :], w_down[r0:r0 + 128, :])

        load_mlp_w(0)
        for t in range(NT):
            norm_transpose(xres[:, t, :], XN6[t % 2], wcol_mlp, hT[:, :, t * 128:(t + 1) * 128], 6 + t % 2)
        load_mlp_w(1)
        cnt = 0
        for qf in range(4):
            wu, wd = WU[qf % 2], WD[qf % 2]
            if qf in (1, 2):
                pass
            for c in range(4):
                at = AT[c % 2]
                for fcl in range(8):
                    b = 4 + cnt % 2
                    rl = RL[cnt % 2]
                    cnt += 1
                    for kc in range(KC):
                        P.mm(bank(b), wu[:, kc, fcl * 128:(fcl + 1) * 128], hT[:, kc, c * 512:(c + 1) * 512],
                             start=(kc == 0), stop=(kc == KC - 1))
                    P.act(rl[:], bank(b), AF.Relu)
                    P.tt("pool" if fcl % 2 else "dve", at[:, fcl, :], rl[:], rl[:], ALU.mult)
                for tl in range(4):
                    t = c * 4 + tl
                    for hf in range(2):
                        b = 2 * (tl % 2) + hf
                        for fcl in range(8):
                            P.mm(bank(b), at[:, fcl, tl * 128:(tl + 1) * 128], wd[:, fcl, hf * 512:(hf + 1) * 512],
                                 start=(fcl == 0), stop=(fcl == 7))
                        if qf < 3:
                            P.tt("dve", xres[:, t, hf * 512:(hf + 1) * 512], bank(b),
                                 xres[:, t, hf * 512:(hf + 1) * 512], ALU.add)
                        else:
                            P.tt("dve", OUTS[t % 2][:, hf * 512:(hf + 1) * 512], bank(b),
                                 xres[:, t, hf * 512:(hf + 1) * 512], ALU.add)
                    if qf == 3:
                        P.dma("sp", y[t * 128:(t + 1) * 128, :], OUTS[t % 2], is_out=True)
            if qf + 2 < 4:
                load_mlp_w(qf + 2)

        with nc.Block() as block:
            P.emit(block, sem_ctx)
    return nc


_NC_CACHE = {}


def kernel(**inputs):
    if "nc" not in _NC_CACHE:
        _NC_CACHE["nc"] = build_program()
    nc = _NC_CACHE["nc"]
    f32 = lambda a: np.ascontiguousarray(np.asarray(a, dtype=np.float32))
    shared = {}
    for name in ("norm_mix_w", "w_in", "b_forget", "diff_q_norm_w", "diff_k_norm_w", "lambda_q1", "lambda_k1",
                 "lambda_q2", "lambda_k2", "diff_subln_w", "fox_q_norm_w", "fox_k_norm_w", "w_out",
                 "norm_mem_q_w", "norm_mem_kv_w", "w_mem_q", "w_mem_kv", "mem_q_norm_w", "mem_k_norm_w",
                 "w_mem_o", "norm_mlp_w", "w_up", "w_down"):
        a = f32(inputs[name])
        if name in ("w_in", "w_out", "w_mem_q", "w_mem_kv", "w_mem_o", "w_up", "w_down"):
            shared[name] = np.ascontiguousarray(a[0])
        else:
            shared[name] = np.ascontiguousarray(a.reshape(1, -1))
    x = f32(inputs["x"])
    mem = f32(inputs["mem"])
    pos = np.ascontiguousarray(np.asarray(inputs["positions"], dtype=np.int32))
    in_maps = []
    for b in range(N_CORES):
        m = dict(shared)
        m["x"] = np.ascontiguousarray(x[b])
        m["mem"] = np.ascontiguousarray(mem[b])
        m["positions"] = np.ascontiguousarray(pos[b].reshape(1, S))
        in_maps.append(m)
    res = run_bass_kernel_spmd(nc, in_maps, core_ids=list(range(N_CORES)))
    out = np.stack([np.asarray(r["y"], dtype=np.float32) for r in res.results], axis=0)
    return out
```

```python
import math
import numpy as np
import concourse.bass as bass
import concourse.mybir as mybir
from concourse.bass_utils import run_bass_kernel_spmd

F32 = mybir.dt.float32
BF16 = mybir.dt.bfloat16
I32 = mybir.dt.int32
U8 = mybir.dt.uint8
AF = mybir.ActivationFunctionType
ALU = mybir.AluOpType
AX = mybir.AxisListType

S = 2048
D = 1024
NT = 16
KC = 8
MEM = 256
IN_COLS = 3080
EPS = 1e-6
LAM_INIT = 0.8 - 0.6 * math.exp(0.0)
N_CORES = 8


class Op:
    __slots__ = ("eng", "fn", "pos", "dma", "sig", "sem", "val", "vc", "waits", "id", "preds", "cost",
                 "nbytes", "is_out", "prio", "start", "fin", "succs", "npred", "opreds", "rdy")


class Prog:
    ENGS = ("pe", "act", "dve", "pool", "sp")
    SEM_LIMIT = 30000
    SCHEDULE = True

    def __init__(self, nc):
        self.nc = nc
        self.allops = []
        self.ops = {e: [] for e in self.ENGS}
        self.acc = {}
        self.dma_sems = {}
        self.out_dmas = []

    @staticmethod
    def _region(ap):
        sp = str(ap.space)
        if sp not in ("SB", "PSUM"):
            return None
        aps = ap.ap
        esz = mybir.dt.size(ap.dtype)
        pstride, npart = aps[0]
        off = ap.offset
        if pstride == 0:
            p0, f0 = 0, off
        else:
            p0, f0 = off // pstride, off % pstride
        dims = sorted([(abs(s_), c) for s_, c in aps[1:] if c > 1])
        ivs = [(0, 1)]
        for s_, c in dims:
            if s_ == 0:
                continue
            span = ivs[-1][1] - ivs[0][0]
            if s_ <= span or len(ivs) * c > 64:
                ivs = [(ivs[0][0], ivs[-1][1] + (c - 1) * s_)]
            else:
                ivs = [(lo + i * s_, hi + i * s_) for i in range(c) for lo, hi in ivs]
                ivs.sort()
        out = []
        for lo, hi in ivs:
            lo, hi = (f0 + lo) * esz, (f0 + hi) * esz
            if sp == "PSUM":
                lo = (lo // 2048) * 2048
                hi = ((hi + 2047) // 2048) * 2048
            if out and lo <= out[-1][1]:
                if hi > out[-1][1]:
                    out[-1] = (out[-1][0], hi)
            else:
                out.append((lo, hi))
        if sp == "PSUM":
            return (ap.tensor.name, True, 0, 128, tuple(out))
        return (ap.tensor.name, False, p0, p0 + npart, tuple(out))

    @staticmethod
    def _ov(a, b):
        if a[0][0] >= b[-1][1] or b[0][0] >= a[-1][1]:
            return False
        for lo, hi in a:
            for lo2, hi2 in b:
                if lo < hi2 and lo2 < hi:
                    return True
        return False

    @staticmethod
    def _cov(new, old):
        for lo, hi in old:
            ok = False
            for lo2, hi2 in new:
                if lo2 <= lo and hi <= hi2:
                    ok = True
                    break
            if not ok:
                return False
        return True

    def _access(self, op, ap, is_write, deps):
        r = self._region(ap)
        if r is None:
            return
        name, psum, p0, p1, ivs = r
        lst = self.acc.get(name)
        if lst is None:
            lst = []
            self.acc[name] = lst
        conflict_w = is_write or psum
        new = []
        mine = []
        for e in lst:
            eivs, ep0, ep1, eop, ew, ecw, more = e
            if eop is op:
                new.append(e)
                continue
            pov = ep0 < p1 and p0 < ep1
            ov = pov and self._ov(ivs, eivs)
            same = (eop.eng == op.eng) and not eop.dma and not op.dma
            if ov:
                if same:
                    if ew or is_write:
                        deps.append(eop)
                        deps.extend(more)
                elif conflict_w or ecw:
                    deps.append(eop)
                    deps.extend(more)
            cover = ov and p0 <= ep0 and ep1 <= p1 and self._cov(ivs, eivs)
            if cover and is_write:
                continue
            if cover and same and psum:
                op.opreds.append(eop)
                continue
            if cover and same and (not ew) and (not is_write):
                mine.append(eop)
                mine.extend(more)
                continue
            new.append(e)
        new.append((ivs, p0, p1, op, is_write, conflict_w, mine))
        self.acc[name] = new

    def add(self, eng, fn, reads, writes, dma=False, is_out=False, cost=100.0, nbytes=0):
        op = Op()
        op.eng, op.fn, op.dma, op.sig = eng, fn, dma, False
        op.id = len(self.allops)
        op.sem = None
        op.val = 0
        op.cost = cost
        op.nbytes = nbytes
        op.is_out = is_out
        op.opreds = []
        deps = []
        for ap in reads:
            if ap is not None and not isinstance(ap, (int, float)):
                self._access(op, ap, False, deps)
        for ap in writes:
            if ap is not None:
                self._access(op, ap, True, deps)
        seen = set()
        preds = []
        for a in deps:
            if a.id not in seen:
                seen.add(a.id)
                preds.append(a)
        op.preds = preds
        self.allops.append(op)
        return op

    def _schedule(self):
        import heapq
        ops = self.allops
        for op in ops:
            op.succs = []
        for op in ops:
            allp = {a.id: a for a in op.preds}
            for a in op.opreds:
                allp[a.id] = a
            op.npred = len(allp)
            for a in allp.values():
                a.succs.append(op)
        for op in reversed(ops):
            m = 0.0
            for s_ in op.succs:
                if s_.prio > m:
                    m = s_.prio
            lat = op.cost + (2000.0 + op.nbytes / 360.0 if op.dma else 0.0)
            op.prio = m + lat
        LAT = 250.0
        ready = {e: [] for e in self.ENGS}
        for op in ops:
            if op.npred == 0:
                heapq.heappush(ready[op.eng], (-op.prio, op.id, op))
        free = {e: 0.0 for e in self.ENGS}
        events = []
        pending = []
        for op in ops:
            op.rdy = 0.0
        order = {e: [] for e in self.ENGS}
        pipe_free = 0.0
        t = 0.0
        ndone = 0
        n = len(ops)
        while ndone < n:
            progressed = False
            for e in self.ENGS:
                if free[e] <= t and ready[e]:
                    _, _, op = heapq.heappop(ready[e])
                    op.start = t
                    if op.dma:
                        free[e] = t + op.cost
                        xs = max(t + op.cost, pipe_free)
                        pipe_free = xs + op.nbytes / 360.0
                        op.fin = pipe_free + 2000.0
                    else:
                        free[e] = t + op.cost
                        op.fin = free[e]
                    heapq.heappush(events, (op.fin, op.id, op))
                    order[e].append(op)
                    progressed = True
            cand = []
            if events:
                cand.append(events[0][0])
            if pending:
                cand.append(pending[0][0])
            for e in self.ENGS:
                if ready[e] and free[e] > t:
                    cand.append(free[e])
            if not progressed and not cand:
                raise RuntimeError("scheduler deadlock")
            if cand:
                nt = min(cand)
                if nt > t:
                    t = nt
            while events and events[0][0] <= t:
                _, _, op = heapq.heappop(events)
                ndone += 1
                for s_ in op.succs:
                    s_.npred -= 1
                    rt = op.fin if (s_.eng == op.eng and not op.dma) else op.fin + LAT
                    if rt > s_.rdy:
                        s_.rdy = rt
                    if s_.npred == 0:
                        if s_.rdy <= t:
                            heapq.heappush(ready[s_.eng], (-s_.prio, s_.id, s_))
                        else:
                            heapq.heappush(pending, (s_.rdy, s_.id, s_))
            while pending and pending[0][0] <= t:
                _, _, s_ = heapq.heappop(pending)
                heapq.heappush(ready[s_.eng], (-s_.prio, s_.id, s_))
        self.est_ns = t
        return order

    def _finalize(self):
        if self.SCHEDULE:
            order = self._schedule()
            glob = sorted(self.allops, key=lambda o: (o.start, o.id))
        else:
            order = {e: [o for o in self.allops if o.eng == e] for e in self.ENGS}
            glob = list(self.allops)
        self.ops = order
        for e in self.ENGS:
            for i, op in enumerate(order[e]):
                op.pos = i
        dma_last = {}
        dma_cnt = {}
        extra = {}
        for e in self.ENGS:
            pool = self.dma_sems.get(e)
            i = 0
            for op in order[e]:
                if not op.dma:
                    continue
                sem = pool[i % len(pool)]
                i += 1
                prev = dma_last.get(sem)
                if prev is not None:
                    extra[op.id] = prev
                dma_last[sem] = op
                dma_cnt[sem] = dma_cnt.get(sem, 0) + 16
                op.sem, op.val, op.sig = sem, dma_cnt[sem], True
                if op.is_out:
                    self.out_dmas.append(op)
        known = {e: {} for e in self.ENGS}
        for op in glob:
            kn = known[op.eng]
            deps = list(op.preds)
            if op.id in extra:
                deps.append(extra[op.id])
            deps.sort(key=lambda a: -a.pos)
            waits = []
            for a in deps:
                same = (a.eng == op.eng) and not a.dma and not op.dma
                if same and op.eng == "pe":
                    continue
                key = ("d", a.id) if a.dma else a.eng
                need = 1 if a.dma else a.pos
                if kn.get(key, -1) >= need:
                    continue
                waits.append(a)
                a.sig = True
                for k, v in a.vc.items():
                    if kn.get(k, -1) < v:
                        kn[k] = v
            op.waits = waits
            vc = dict(kn)
            if op.dma:
                vc[("d", op.id)] = 1
            else:
                vc[op.eng] = op.pos
            op.vc = vc

    def emit(self, block, sems_for_engine):
        self._finalize()
        for e in self.ENGS:
            pool = sems_for_engine[e]
            cnt, ep = 0, 0
            for op in self.ops[e]:
                if op.dma or not op.sig:
                    continue
                cnt += 1
                op.sem, op.val = pool[ep], cnt
                if cnt >= self.SEM_LIMIT:
                    cnt, ep = 0, ep + 1

        def run(engname, eng):
            for op in self.ops[engname]:
                best = {}
                for a in op.waits:
                    if best.get(a.sem, 0) < a.val:
                        best[a.sem] = a.val
                for sem, val in best.items():
                    eng.wait_ge(sem, val)
                ins = op.fn(eng)
                if op.sig:
                    ins.then_inc(op.sem, 16 if op.dma else 1)
            if engname == "sp":
                best = {}
                for a in self.out_dmas:
                    if best.get(a.sem, 0) < a.val:
                        best[a.sem] = a.val
                for sem, val in best.items():
                    eng.wait_ge(sem, val)

        @block.tensor
        def _(t):
            run("pe", t)

        @block.scalar
        def _(s):
            run("act", s)

        @block.vector
        def _(v):
            run("dve", v)

        @block.gpsimd
        def _(g):
            run("pool", g)

        @block.sync
        def _(sy):
            run("sp", sy)

    @staticmethod
    def _fs(ap):
        n = 1
        for s_ in ap.shape[1:]:
            n *= s_
        return n

    @staticmethod
    def _is_psum(ap):
        return str(ap.space) == "PSUM"

    def _vcost(self, eng, out, ins):
        n = self._fs(out)
        ps = any(self._is_psum(a) for a in ins if a is not None and not isinstance(a, (int, float))) or self._is_psum(out)
        if eng == "pool":
            return 150.0 + n * 1.9
        return (125.0 if ps else 65.0) + n * 1.04

    def mm(self, out, lhsT, rhs, start=True, stop=True, skip=False):
        n = self._fs(rhs)
        mult = 4.0 if rhs.dtype == F32 else 1.0
        return self.add("pe", lambda e: e.matmul(out, lhsT, rhs, start=start, stop=stop,
                                                 skip_group_check=skip), [lhsT, rhs], [out],
                        cost=mult * max(64, n) / 2.4 + 8.0)

    def tr(self, out, in_, ident):
        return self.add("pe", lambda e: e.transpose(out, in_, ident), [in_, ident], [out], cost=75.0)

    def act(self, out, in_, func, bias=0.0, scale=1.0, accum_out=None):
        rd = [in_]
        if not isinstance(bias, (int, float)):
            rd.append(bias)
        if not isinstance(scale, (int, float)):
            rd.append(scale)
        kw = {}
        if accum_out is not None:
            kw["accum_out"] = accum_out
        return self.add("act", lambda e: e.activation(out=out, in_=in_, func=func, bias=bias,
                                                      scale=scale, **kw), rd, [out, accum_out],
                        cost=200.0 + 0.8 * self._fs(in_))

    def tt(self, eng, out, in0, in1, op):
        return self.add(eng, lambda e: e.tensor_tensor(out=out, in0=in0, in1=in1, op=op),
                        [in0, in1], [out], cost=self._vcost(eng, out, [in0, in1]))

    def ts(self, eng, out, in0, s1, op0, s2=None, op1=None):
        rd = [in0]
        if not isinstance(s1, (int, float)):
            rd.append(s1)
        if s2 is not None and not isinstance(s2, (int, float)):
            rd.append(s2)
        c = self._vcost(eng, out, [in0])
        if op1 is None:
            return self.add(eng, lambda e: e.tensor_scalar(out=out, in0=in0, scalar1=s1, scalar2=None,
                                                           op0=op0), rd, [out], cost=c)
        return self.add(eng, lambda e: e.tensor_scalar(out=out, in0=in0, scalar1=s1, scalar2=s2,
                                                       op0=op0, op1=op1), rd, [out], cost=c)

    def stt(self, eng, out, in0, scalar, in1, op0, op1):
        rd = [in0, in1]
        if not isinstance(scalar, (int, float)):
            rd.append(scalar)
        return self.add(eng, lambda e: e.scalar_tensor_tensor(out=out, in0=in0, scalar=scalar, in1=in1,
                                                              op0=op0, op1=op1), rd, [out],
                        cost=self._vcost(eng, out, [in0, in1]))

    def copy(self, eng, out, in_):
        if eng == "act":
            return self.add("act", lambda e: e.activation(out=out, in_=in_, func=AF.Copy), [in_], [out],
                            cost=200.0 + 0.8 * self._fs(in_))
        return self.add(eng, lambda e: e.tensor_copy(out=out, in_=in_), [in_], [out],
                        cost=self._vcost(eng, out, [in_]))

    def memset(self, eng, ap, val):
        return self.add(eng, lambda e: e.memset(ap, val), [], [ap], cost=60.0 + 0.3 * self._fs(ap))

    def reduce(self, eng, out, in_, op=ALU.add):
        return self.add(eng, lambda e: e.tensor_reduce(out=out, in_=in_, axis=AX.X, op=op), [in_], [out],
                        cost=65.0 + 1.04 * self._fs(in_))

    def recip(self, out, in_):
        return self.add("dve", lambda e: e.reciprocal(out=out, in_=in_), [in_], [out],
                        cost=self._vcost("dve", out, [in_]))

    def affsel(self, out, in_, pattern, cmp, fill, base, cm):
        return self.add("pool", lambda e: e.affine_select(out=out, in_=in_, pattern=pattern, compare_op=cmp,
                                                          fill=fill, base=base, channel_multiplier=cm),
                        [in_], [out], cost=150.0 + 1.0 * self._fs(out))

    def dma(self, q, out, in_, is_out=False, slow=False):
        if slow:
            fn = lambda e: e.dma_start(out=out, in_=in_, allow_slow_non_contiguous=True)
        else:
            fn = lambda e: e.dma_start(out=out, in_=in_)
        nb = self._fs(out) * mybir.dt.size(out.dtype) * out.shape[0]
        return self.add(q, fn, [in_], [out], dma=True, is_out=is_out,
                        cost=(1000.0 if q == "pool" else 60.0), nbytes=nb)


def _view(base, off, shape, dt):
    esz = mybir.dt.size(dt)
    n = 1
    for s in shape[1:]:
        n *= s
    v = base[0:shape[0], off:off + n * esz].bitcast(dt)
    if len(shape) == 2:
        return v
    names = [f"d{i}" for i in range(1, len(shape))]
    pat = "p (" + " ".join(names) + ") -> p " + " ".join(names)
    kw = {names[i]: shape[i + 1] for i in range(len(names) - 1)}
    return v.rearrange(pat, **kw)


def _bc(ap, axis, n):
    shp = list(ap.shape)
    shp.insert(axis, n)
    return ap.unsqueeze(axis).broadcast_to(shp)


def build_program():
    nc = bass.Bass("TRN2", target_bir_lowering=False)

    def din(name, shape, dt=F32):
        return nc.dram_tensor(name, list(shape), dt, kind="ExternalInput").ap()

    x = din("x", [S, D])
    mem = din("mem", [MEM, D])
    positions = din("positions", [1, S], I32)
    norm_mix_w = din("norm_mix_w", [1, D])
    w_in = din("w_in", [D, IN_COLS])
    b_forget = din("b_forget", [1, 8])
    diff_q_norm_w = din("diff_q_norm_w", [1, 64])
    diff_k_norm_w = din("diff_k_norm_w", [1, 64])
    lambda_q1 = din("lambda_q1", [1, 64])
    lambda_k1 = din("lambda_k1", [1, 64])
    lambda_q2 = din("lambda_q2", [1, 64])
    lambda_k2 = din("lambda_k2", [1, 64])
    diff_subln_w = din("diff_subln_w", [1, 128])
    fox_q_norm_w = din("fox_q_norm_w", [1, 64])
    fox_k_norm_w = din("fox_k_norm_w", [1, 64])
    w_out = din("w_out", [D, D])
    norm_mem_q_w = din("norm_mem_q_w", [1, D])
    norm_mem_kv_w = din("norm_mem_kv_w", [1, D])
    w_mem_q = din("w_mem_q", [D, D])
    w_mem_kv = din("w_mem_kv", [D, 2 * D])
    mem_q_norm_w = din("mem_q_norm_w", [1, 256])
    mem_k_norm_w = din("mem_k_norm_w", [1, 256])
    w_mem_o = din("w_mem_o", [D, D])
    norm_mlp_w = din("norm_mlp_w", [1, D])
    w_up = din("w_up", [D, 4 * D])
    w_down = din("w_down", [4 * D, D])
    y = nc.dram_tensor("y", [S, D], F32, kind="ExternalOutput").ap()

    from contextlib import ExitStack
    with ExitStack() as es:
        def sb(name, shape, dt):
            return es.enter_context(nc.sbuf_tensor(name, list(shape), dt))

        RX = sb("RX", [128, 65536], U8)
        R2 = sb("R2", [128, 131072], U8)
        PS = es.enter_context(nc.psum_tensor("PS", [128, 4096], F32))
        ident = sb("ident", [128, 128], BF16)
        tri = sb("tri", [128, 128], F32)
        ones = sb("ones", [128, 128], F32)
        wcol_mix = sb("wcol_mix", [128, 8], F32)
        wcol_memq = sb("wcol_memq", [128, 8], F32)
        wcol_memkv = sb("wcol_memkv", [128, 8], F32)
        wcol_mlp = sb("wcol_mlp", [128, 8], F32)
        wq_adj = sb("wq_adj", [128, 1], F32)
        wk_adj = sb("wk_adj", [128, 1], F32)
        wfq = sb("wfq", [128, 1], F32)
        wfk = sb("wfk", [128, 1], F32)
        w16q = sb("w16q", [128, 16], F32)
        w16k = sb("w16k", [128, 16], F32)
        subln = sb("subln", [128, 128], F32)
        wcol_mq = sb("wcol_mq", [128, 2], F32)
        wcol_mk = sb("wcol_mk", [128, 2], F32)
        bfg = sb("bfg", [128, 8], F32)
        lamv = sb("lamv", [128, 4, 64], F32)
        lamt = sb("lamt", [128, 8], F32)
        posi = sb("posi", [128, 16], I32)
        posf = sb("posf", [128, 16], F32)
        invf = sb("invf", [128, 8], F32)
        ang = sb("ang", [128, 16, 8], F32)
        angk = sb("angk", [128, 16, 8], F32)
        angi = sb("angi", [128, 16, 8], I32)
        angr = sb("angr", [128, 16, 8], F32)
        angc = sb("angc", [128, 16, 8], F32)
        cosT = sb("cosT", [128, 16, 8], F32)
        sinT = sb("sinT", [128, 16, 8], F32)
        st1 = sb("st1", [128, 64], F32)
        st2 = sb("st2", [128, 64], F32)
        st3 = sb("st3", [128, 64], F32)
        nls = sb("nls", [128, 16, 8], F32)
        Tb = sb("Tb", [128, 16, 8], F32)
        Pinc = sb("Pinc", [128, 16, 8], F32)
        gcol = sb("gcol", [128, 16, 8], F32)
        gtmp = sb("gtmp", [128, 16, 8], F32)
        biasT = sb("biasT", [128, 16, 8, 8], F32)
        zt = sb("zt", [128, 8], F32)
        et = sb("et", [128, 8], F32)

        n_eng_sems = 1
        sem_ctx = {}
        for e in Prog.ENGS:
            sem_ctx[e] = [es.enter_context(nc.semaphore(f"s_{e}{i}")) for i in range(n_eng_sems)]
        P = Prog(nc)
        P.dma_sems["sp"] = [es.enter_context(nc.semaphore(f"d_sp{i}")) for i in range(16)]
        P.dma_sems["pool"] = [es.enter_context(nc.semaphore(f"d_pool{i}")) for i in range(16)]

        def bank(b, dt=F32):
            v = PS[:, b * 512:(b + 1) * 512]
            return v if dt == F32 else v.bitcast(dt)

        P.memset("pool", ident[:], 1.0)
        P.affsel(ident[:], ident[:], [[-1, 128]], ALU.is_equal, 0.0, 0, 1)
        P.memset("pool", tri[:], 1.0)
        P.affsel(tri[:], tri[:], [[1, 128]], ALU.is_ge, 0.0, 0, -1)
        P.memset("pool", ones[:], 1.0)

        identf = sb("identf", [128, 128], F32)
        P.memset("pool", identf[:], 1.0)
        P.affsel(identf[:], identf[:], [[-1, 128]], ALU.is_equal, 0.0, 0, 1)
        w8 = _view(RX, 0, [8, 6, 128], F32)
        w128 = _view(RX, 3072, [1, 4, 128], F32)
        posi16 = _view(RX, 5120, [16, 128], I32)
        posf16 = _view(RX, 5632, [16, 128], F32)
        pcol = [0]

        def col_load(dst, src, nk, slot):
            P.dma("sp", w8[0:nk, slot, :], src.rearrange("o (k q) -> (o k) q", q=128))
            c0 = pcol[0]
            pcol[0] += nk
            P.mm(bank(7)[:, c0:c0 + nk], w8[0:nk, slot, :], identf[0:nk, 0:nk], True, True)
            P.copy("dve", dst[:], bank(7)[:, c0:c0 + nk])

        def col64(dst, src, slot):
            P.dma("sp", w128[0:1, slot, 0:64], src)
            P.dma("sp", w128[0:1, slot, 64:128], src)
            c0 = pcol[0]
            pcol[0] += 1
            P.mm(bank(7)[:, c0:c0 + 1], w128[0:1, slot, :], ones[0:1, 0:1], True, True)
            P.copy("dve", dst[:, 0:1], bank(7)[:, c0:c0 + 1])

        col_load(wcol_mix, norm_mix_w, 8, 0)
        col_load(wcol_memq, norm_mem_q_w, 8, 1)
        col_load(wcol_memkv, norm_mem_kv_w, 8, 2)
        col_load(wcol_mlp, norm_mlp_w, 8, 3)
        col_load(wcol_mq, mem_q_norm_w, 2, 4)
        col_load(wcol_mk, mem_k_norm_w, 2, 5)
        for i, (dst, src) in enumerate(((wq_adj, diff_q_norm_w), (wk_adj, diff_k_norm_w), (wfq, fox_q_norm_w),
                                        (wfk, fox_k_norm_w))):
            col64(dst, src, i)
        for dst in (wq_adj, wk_adj):
            P.memset("pool", dst[0:16, :], 1.0)
            P.memset("pool", dst[64:80, :], 1.0)
        P.dma("sp", w16q[:], diff_q_norm_w[:, 0:16].partition_broadcast(128))
        P.dma("sp", w16k[:], diff_k_norm_w[:, 0:16].partition_broadcast(128))
        P.dma("sp", subln[:], diff_subln_w.partition_broadcast(128))
        P.dma("sp", bfg[:], b_forget.partition_broadcast(128))
        for i, src in enumerate((lambda_q1, lambda_k1, lambda_q2, lambda_k2)):
            P.dma("sp", lamv[:, i, :], src.partition_broadcast(128))
        P.dma("sp", posi16, positions.rearrange("o (t q) -> (o t) q", q=128))
        P.copy("dve", posf16, posi16)
        P.mm(bank(7)[:, 64:80], posf16, identf[0:16, 0:16], True, True)

        P.tt("dve", lamv[:, 0, :], lamv[:, 0, :], lamv[:, 1, :], ALU.mult)
        P.tt("dve", lamv[:, 2, :], lamv[:, 2, :], lamv[:, 3, :], ALU.mult)
        P.reduce("dve", lamt[:, 0:1], lamv[:, 0, :])
        P.reduce("dve", lamt[:, 1:2], lamv[:, 2, :])
        P.act(lamt[:, 2:4], lamt[:, 0:2], AF.Exp)
        P.tt("dve", lamt[:, 4:5], lamt[:, 3:4], lamt[:, 2:3], ALU.subtract)
        P.ts("dve", lamt[:, 4:5], lamt[:, 4:5], -LAM_INIT, ALU.add)
        P.ts("dve", subln[:], subln[:], 1.0 - LAM_INIT, ALU.mult)

        inv64 = 500000.0 ** (-np.arange(0, 16, 2, dtype=np.float64) / 16.0)
        inv_hi = inv64.astype(np.float32)
        inv_lo = (inv64 - inv_hi.astype(np.float64)).astype(np.float32)
        invf_lo = zt
        ang2 = gtmp
        for j in range(8):
            P.memset("pool", invf[:, j:j + 1], float(inv_hi[j]))
            P.memset("pool", invf_lo[:, j:j + 1], float(inv_lo[j]))
        P.copy("dve", posf[:], bank(7)[:, 64:80])
        P.tt("dve", ang[:], _bc(posf[:], 2, 8), _bc(invf[:], 1, 16), ALU.mult)
        P.tt("dve", ang2[:], _bc(posf[:], 2, 8), _bc(invf_lo[:], 1, 16), ALU.mult)
        C1 = 6.28125
        C2 = float(np.float32(2 * math.pi - C1))
        C3 = float(np.float32(2 * math.pi - C1 - C2))
        PI_SAFE = 3.1415925
        P.tt("dve", angk[:], ang[:], ang2[:], ALU.add)
        P.ts("dve", angk[:], angk[:], float(np.float32(1.0 / (2 * math.pi))), ALU.mult)
        P.copy("dve", angi[:], angk[:])
        P.copy("dve", angk[:], angi[:])
        P.stt("dve", angr[:], angk[:], -C1, ang[:], ALU.mult, ALU.add)
        P.tt("dve", angr[:], angr[:], ang2[:], ALU.add)
        P.stt("dve", angr[:], angk[:], -C2, angr[:], ALU.mult, ALU.add)
        P.stt("dve", angr[:], angk[:], -C3, angr[:], ALU.mult, ALU.add)
        P.ts("dve", angc[:], angr[:], math.pi / 2, ALU.is_gt)
        P.ts("dve", angk[:], angr[:], math.pi / 2, ALU.add)
        P.stt("dve", angc[:], angc[:], -2 * math.pi, angk[:], ALU.mult, ALU.add)
        P.ts("dve", angr[:], angr[:], -PI_SAFE, ALU.max, PI_SAFE, ALU.min)
        P.ts("dve", angc[:], angc[:], -PI_SAFE, ALU.max, PI_SAFE, ALU.min)
        P.act(sinT[:], angr[:], AF.Sin)
        P.act(cosT[:], angc[:], AF.Sin)

        QTd = _view(RX, 0, [128, 4, S], BF16)
        KTd = _view(RX, 16384, [128, 4, S], BF16)
        QTf = _view(RX, 32768, [128, 4, S], BF16)
        KTf = _view(RX, 49152, [128, 4, S], BF16)
        xres = _view(RX, 0, [128, NT, D], F32)

        w_in_sb = _view(R2, 0, [128, KC, IN_COLS], BF16)
        Vd = _view(R2, 50176, [128, NT, 4, 130], BF16)
        Vf = _view(R2, 66816, [128, NT, 8, 66], BF16)
        o1 = 83712
        XT = [_view(R2, o1 + i * 4096, [128, D], F32) for i in range(2)]
        XN = [_view(R2, o1 + 8192 + i * 2048, [128, D], BF16) for i in range(2)]
        HTt = [_view(R2, o1 + 12288 + i * 2048, [128, KC, 128], BF16) for i in range(2)]
        SQ = [_view(R2, o1 + 16384 + i * 2048, [128, 512], F32) for i in range(2)]
        QN = [_view(R2, o1 + 20480 + i * 4096, [128, 2048], BF16) for i in range(2)]
        ropeA = _view(R2, o1 + 28672, [128, 16, 16], F32)
        ropeT = [_view(R2, o1 + 29696 + i * 512, [128, 16, 8], F32) for i in range(4)]

        prev_grp = []
        for (c0, c1) in ((0, 1024), (1536, 2560), (1024, 1536), (2560, IN_COLS)):
            grp = []
            for kc in range(KC):
                o = P.dma("pool", w_in_sb[:, kc, c0:c1], w_in[kc * 128:(kc + 1) * 128, c0:c1])
                o.preds.extend(prev_grp)
                grp.append(o)
            prev_grp = grp
        P.memset("pool", Vd[:, :, :, 128:130], 1.0)
        P.memset("pool", Vf[:, :, :, 64:66], 1.0)

        rs_rr = [0]

        def rms_stats(src_ss, n, dst_rs, ncol):
            o = 16 * (rs_rr[0] % 4)
            rs_rr[0] += 1
            P.act(st3[:, o:o + ncol], src_ss, AF.Ln, bias=EPS, scale=1.0 / n)
            P.act(dst_rs, st3[:, o:o + ncol], AF.Exp, scale=-0.5)

        nrm_rr = [0]

        def norm_prep(src, xn):
            k = 4 * (nrm_rr[0] % 2)
            nrm_rr[0] += 1
            P.memset("dve", st1[:, k:k + 1], 0.0)
            P.act(xn, src, AF.Square, accum_out=st1[:, k:k + 1])
            P.act(st1[:, k + 2:k + 3], st1[:, k:k + 1], AF.Ln, bias=EPS, scale=1.0 / D)
            P.act(st1[:, k + 1:k + 2], st1[:, k + 2:k + 3], AF.Exp, scale=-0.5)
            P.ts("dve", xn, src, st1[:, k + 1:k + 2], ALU.mult)

        def norm_tr(xn, wcol, dst, pbank):
            pb = bank(pbank, BF16)
            for kc in range(KC):
                P.tr(pb[:, kc * 128:(kc + 1) * 128], xn[:, kc * 128:(kc + 1) * 128], ident[:])
            P.tt("dve", dst, pb.rearrange("p (k t) -> p k t", t=128), _bc(wcol[:], 2, 128), ALU.mult)

        def norm_transpose(src, xn, wcol, dst, pbank):
            norm_prep(src, xn)
            pb = bank(pbank, BF16)
            for kc in range(KC):
                P.tr(pb[:, kc * 128:(kc + 1) * 128], xn[:, kc * 128:(kc + 1) * 128], ident[:])
            P.tt("dve", dst, pb.rearrange("p (k t) -> p k t", t=128), _bc(wcol[:], 2, 128), ALU.mult)

        pb7 = bank(7, BF16)

        def stA0(t):
            P.dma("sp", XT[t % 2], x[t * 128:(t + 1) * 128, :])
            norm_prep(XT[t % 2], XN[t % 2])

        def stA1(t):
            norm_tr(XN[t % 2], wcol_mix, HTt[t % 2], 6)

        def proj(t, b, c0, n=512, col0=0):
            htt = HTt[t % 2]
            for kc in range(KC):
                P.mm(bank(b)[:, col0:col0 + n], htt[:, kc, :], w_in_sb[:, kc, c0:c0 + n],
                     start=(kc == 0), stop=(kc == KC - 1))

        def stB1(t):
            proj(t, 0, 0)
            proj(t, 1, 512)

        def stC1(t):
            qn = QN[t % 2]
            for i, b in enumerate((0, 1)):
                P.act(SQ[i], bank(b), AF.Square)
                P.reduce("dve", st2[:, i * 8:(i + 1) * 8], SQ[i].rearrange("p (g d) -> p g d", d=64))
            rms_stats(st2[:, 0:16], 64.0, st2[:, 16:32], 16)
            for i, b in enumerate((0, 1)):
                P.tt("dve", qn[:, i * 512:(i + 1) * 512].rearrange("p (g d) -> p g d", d=64),
                     bank(b).rearrange("p (g d) -> p g d", d=64),
                     _bc(st2[:, 16 + i * 8:24 + i * 8], 2, 64), ALU.mult)
                P.tt("dve", ropeA[:, i * 8:(i + 1) * 8, :],
                     bank(b).rearrange("p (g d) -> p g d", d=64)[:, :, 0:16],
                     _bc(st2[:, 16 + i * 8:24 + i * 8], 2, 16), ALU.mult)
            P.tt("pool", ropeA[:, 0:8, :], ropeA[:, 0:8, :], _bc(w16q[:], 1, 8), ALU.mult)
            P.tt("pool", ropeA[:, 8:16, :], ropeA[:, 8:16, :], _bc(w16k[:], 1, 8), ALU.mult)
            cs = _bc(cosT[:, t, :], 1, 16)
            sn = _bc(sinT[:, t, :], 1, 16)
            qv = qn[:, 0:1024].rearrange("p (g d) -> p g d", d=64)
            P.tt("pool", ropeT[0], ropeA[:, :, 0:8], cs, ALU.mult)
            P.tt("pool", ropeT[1], ropeA[:, :, 8:16], sn, ALU.mult)
            P.tt("pool", ropeT[2], ropeA[:, :, 8:16], cs, ALU.mult)
            P.tt("pool", ropeT[3], ropeA[:, :, 0:8], sn, ALU.mult)
            P.tt("pool", qv[:, :, 0:8], ropeT[0], ropeT[1], ALU.subtract)
            P.tt("pool", qv[:, :, 8:16], ropeT[2], ropeT[3], ALU.add)

        def stB2(t):
            proj(t, 2, 1536)
            proj(t, 3, 2048)

        def stC2(t):
            qn = QN[t % 2]
            for i, b in enumerate((2, 3)):
                P.act(SQ[i], bank(b), AF.Square)
                P.reduce("dve", st2[:, 32 + i * 8:40 + i * 8], SQ[i].rearrange("p (g d) -> p g d", d=64))
            rms_stats(st2[:, 32:48], 64.0, st2[:, 48:64], 16)
            for i, b in enumerate((2, 3)):
                P.tt("dve", qn[:, 1024 + i * 512:1536 + i * 512].rearrange("p (g d) -> p g d", d=64),
                     bank(b).rearrange("p (g d) -> p g d", d=64),
                     _bc(st2[:, 48 + i * 8:56 + i * 8], 2, 64), ALU.mult)

        def stD(t):
            qn = QN[t % 2]
            tcols = slice(t * 128, (t + 1) * 128)
            for (c0, dst, wc) in ((0, QTd, wq_adj), (512, KTd, wk_adj), (1024, QTf, wfq), (1536, KTf, wfk)):
                for j in range(4):
                    P.tr(pb7[:, j * 128:(j + 1) * 128], qn[:, c0 + j * 128:c0 + (j + 1) * 128], ident[:])
                P.ts("dve", dst[:, :, tcols], pb7[:, 0:512].rearrange("p (h t) -> p h t", t=128),
                     wc[:, 0:1], ALU.mult)

        def stB3(t):
            proj(t, 4, 1024)
            proj(t, 5, 2560)
            proj(t, 7, 3072, n=8, col0=256)

        def stC3(t):
            P.copy("act", Vd[:, t, :, 0:128], bank(4).rearrange("p (h d) -> p h d", d=128))
            P.copy("act", Vf[:, t, :, 0:64], bank(5).rearrange("p (h d) -> p h d", d=64))
            P.tt("dve", zt[:], bank(7)[:, 256:264], bfg[:], ALU.add)
            P.act(et[:], zt[:], AF.Exp, scale=-1.0)
            P.act(nls[:, t, :], et[:], AF.Ln, bias=1.0)

        stA0(0)
        stA1(0)
        for t in range(NT):
            if t + 1 < NT:
                stA0(t + 1)
            stB1(t)
            if t + 1 < NT:
                stA1(t + 1)
            stC1(t)
            stB2(t)
            stC2(t)
            if t >= 1:
                stD(t - 1)
            stB3(t)
            stC3(t)
        stD(NT - 1)

        nls2 = nls[:].rearrange("p t h -> p (t h)")
        P.mm(bank(7)[:, 0:128], tri[:], nls2, True, True)
        P.mm(bank(6)[:, 0:128], ones[:], nls2, True, True)
        P.copy("dve", Tb[:].rearrange("p t h -> p (t h)"), bank(6)[:, 0:128])
        P.copy("dve", Pinc[:, 0, :], Tb[:, 0, :])
        for j in range(1, NT):
            P.tt("dve", Pinc[:, j, :], Pinc[:, j - 1, :], Tb[:, j, :], ALU.add)
        P.tt("dve", gtmp[:], Pinc[:], Tb[:], ALU.subtract)
        P.tt("dve", gcol[:].rearrange("p t h -> p (t h)"), bank(7)[:, 0:128],
             gtmp[:].rearrange("p t h -> p (t h)"), ALU.add)
        Gmid = Pinc[:].rearrange("p (c four) h -> p c four h", four=4)[:, :, 1, :]
        for kb in range(NT):
            P.tt("dve", biasT[:, kb, 0:4, :], _bc(gcol[:, kb, :], 1, 4), Gmid, ALU.subtract)

        mixedT = _view(R2, 0, [128, KC, S], BF16)
        MIXTOK = [_view(R2, 32768, [128, 4, D], BF16), _view(R2, 122880, [128, 4, D], BF16)]
        PT = [_view(R2, 40960 + i * 1024, [128, 512], BF16) for i in range(3)]
        w_out_sb = _view(R2, 83712, [128, KC, D], BF16)
        e0 = 83712 + 16384
        XT4 = [_view(R2, e0 + i * 4096, [128, D], F32) for i in range(2)]
        QDP = _view(R2, e0, [128, 4, 2, 512], BF16)
        QFP = _view(R2, e0 + 8192, [128, 4, 2, 512], BF16)
        e1 = e0 + 16384
        EPB = [[_view(R2, e1 + k * 2048, [128, 4, 128], F32) for k in range(3)] for i in range(2)]
        assert e1 + 3 * 2048 <= 122880
        P.memset("pool", QDP[:], 0.0)
        P.memset("pool", QFP[:], 0.0)

        def mk_qpad(c, fox):
            def fn():
                for m in range(2):
                    rows = slice(m * 64, (m + 1) * 64)
                    if fox:
                        P.copy("dve", QFP[rows, :, m, :], QTf[rows, :, c * 512:(c + 1) * 512])
                    else:
                        P.copy("dve", QDP[rows, :, m, :], QTd[rows, :, c * 512:(c + 1) * 512])
            return fn

        mk_qpad(0, False)()
        mk_qpad(0, True)()

        for kc in range(KC):
            P.dma("pool", w_out_sb[:, kc, :], w_out[kc * 128:(kc + 1) * 128, :])

        steps = []
        deferred = {}

        def defer(idx, fn):
            deferred.setdefault(idx, []).append(fn)

        def mk_diff_step(i, c, h, m, kb, accv):
            rows = slice(m * 64, (m + 1) * 64)
            qlo = max(kb, 4 * c)
            j0 = qlo - 4 * c
            ncol = (4 - j0) * 128
            sb_ = 4 + i % 3
            pt = PT[i % 3]

            def st_fn():
                P.mm(bank(sb_)[:, 0:ncol], KTd[:, h, kb * 128:(kb + 1) * 128],
                     QDP[:, h, m, j0 * 128:512], True, True)

            def rest_fn():
                P.act(pt[:, 0:ncol], bank(sb_)[:, 0:ncol], AF.Exp, scale=0.125)
                if kb >= 4 * c:
                    P.memset("pool", pt[64:128, 0:64], 0.0)
                for j in range(j0, 4):
                    P.mm(accv[:, j, 0:129], pt[:, (j - j0) * 128:(j - j0 + 1) * 128],
                         Vd[:, kb, h, 0:129], start=(kb == 0 and j in (0, 2)),
                         stop=(kb == 4 * c + j), skip=True)
            return st_fn, rest_fn

        def mk_fox_step(i, c, h, kb, accv):
            pr = h // 2
            qlo = max(kb, 4 * c)
            j0 = qlo - 4 * c
            ncol = (4 - j0) * 128
            sb_ = 4 + i % 3
            pt = PT[i % 3]

            def st_fn():
                P.mm(bank(sb_)[:, 0:ncol], KTf[:, pr, kb * 128:(kb + 1) * 128],
                     QFP[:, pr, h % 2, j0 * 128:512], True, True)

            def rest_fn():
                P.act(pt[:, 0:ncol], bank(sb_)[:, 0:ncol], AF.Exp, scale=0.125,
                      bias=biasT[:, kb, c, h:h + 1])
                if kb >= 4 * c:
                    P.affsel(pt[:, 0:128], pt[:, 0:128], [[1, 128]], ALU.is_ge, 0.0, 0, -1)
                for j in range(j0, 4):
                    P.mm(accv[:, j, 0:65], pt[:, (j - j0) * 128:(j - j0 + 1) * 128],
                         Vf[:, kb, h, 0:65], start=(kb == 0 and j == 0),
                         stop=(kb == 4 * c + j), skip=True)
            return st_fn, rest_fn

        def mk_diff_ep(c, h, accs, par):
            epT, epU, epS = EPB[par]
            mixtok = MIXTOK[c % 2]
            so = 8 + 16 * (h % 2)

            def ep0():
                P.recip(st1[:, so:so + 4], accs[0][:, :, 128])
                P.tt("dve", epU[:], accs[0][:, :, 0:128], _bc(st1[:, so:so + 4], 2, 128), ALU.mult)

            def ep1():
                P.recip(st1[:, so + 4:so + 8], accs[1][:, :, 128])
                P.ts("dve", st1[:, so + 4:so + 8], st1[:, so + 4:so + 8], lamt[:, 4:5], ALU.mult)
                P.tt("dve", epT[:], accs[1][:, :, 0:128], _bc(st1[:, so + 4:so + 8], 2, 128), ALU.mult)
                P.tt("dve", epU[:], epU[:], epT[:], ALU.add)
                P.tt("pool", epS[:], epU[:], epU[:], ALU.mult)
                P.reduce("dve", st1[:, so + 8:so + 12], epS[:])

            def ep2():
                rms_stats(st1[:, so + 8:so + 12], 128.0, st1[:, so + 12:so + 16], 4)
                P.tt("dve", epU[:], epU[:], _bc(st1[:, so + 12:so + 16], 2, 128), ALU.mult)
                P.tt("pool", mixtok[:, :, h * 128:(h + 1) * 128], epU[:], _bc(subln[:], 1, 4), ALU.mult)
            return ep0, ep1, ep2

        def mk_fox_ep(c, h, accv, par):
            mixtok = MIXTOK[c % 2]
            so = 40 + 4 * par

            def ep():
                P.recip(st1[:, so:so + 4], accv[:, :, 64])
                P.tt("dve", mixtok[:, :, 512 + h * 64:512 + (h + 1) * 64],
                     accv[:, :, 0:64], _bc(st1[:, so:so + 4], 2, 64), ALU.mult)
            return ep

        def mk_mix_tr(c):
            mixtok = MIXTOK[c % 2]

            def fn():
                for tl in range(4):
                    for kc in range(KC):
                        P.tr(pb7[:, kc * 128:(kc + 1) * 128], mixtok[:, tl, kc * 128:(kc + 1) * 128], ident[:])
                    tg = c * 4 + tl
                    P.copy("dve", mixedT[:, :, tg * 128:(tg + 1) * 128], pb7.rearrange("p (k t) -> p k t", t=128))
            return fn

        nfox = 0
        for c in range(4):
            for h in range(4):
                accs = [PS[:, m * 1024:(m + 1) * 1024].rearrange("p (q n) -> p q n", n=256) for m in range(2)]
                ep0, ep1, ep2 = mk_diff_ep(c, h, accs, 0)
                for m in range(2):
                    for kb in range(4 * c + 4):
                        steps.append(mk_diff_step(len(steps), c, h, m, kb, accs[m]))
                    if m == 0:
                        defer(len(steps) - 1, ep0)
                defer(len(steps) - 1, ep1)
                defer(len(steps) - 1, ep2)
            if c + 1 < 4:
                defer(len(steps) - 1, mk_qpad(c + 1, False))
            for h in range(8):
                accv = bank(nfox % 4).rearrange("p (q n) -> p q n", n=128)
                for kb in range(4 * c + 4):
                    steps.append(mk_fox_step(len(steps), c, h, kb, accv))
                defer(len(steps) - 1, mk_fox_ep(c, h, accv, nfox % 2))
                nfox += 1
            if c + 1 < 4:
                defer(len(steps) - 1, mk_qpad(c + 1, True))
            defer(len(steps) - 1 + 8, mk_mix_tr(c))

        LA = 2
        nst = len(steps)
        endi = max(nst, max(deferred) + 1)
        for i in range(endi + LA):
            if i < nst:
                steps[i][0]()
            j = i - LA
            if j >= 0:
                if j < nst:
                    steps[j][1]()
                for fn in deferred.get(j, []):
                    fn()

        o5 = 83712
        w_kv_sb = _view(R2, 50176, [128, KC, 2 * D], BF16)
        w_mq_sb = _view(R2, 32768, [128, KC, D], BF16)
        w_mo_sb = _view(R2, 0, [128, KC, D], BF16)
        for kc in range(KC):
            P.dma("pool", w_kv_sb[:, kc, :], w_mem_kv[kc * 128:(kc + 1) * 128, :])
        for kc in range(KC):
            P.dma("pool", w_mq_sb[:, kc, :], w_mem_q[kc * 128:(kc + 1) * 128, :])

        for t in range(NT):
            xt = XT4[t % 2]
            P.dma("sp", xt, x[t * 128:(t + 1) * 128, :])
            for hf in range(2):
                b = 2 * (t % 2) + hf
                for kc in range(KC):
                    P.mm(bank(b), mixedT[:, kc, t * 128:(t + 1) * 128], w_out_sb[:, kc, hf * 512:(hf + 1) * 512],
                         start=(kc == 0), stop=(kc == KC - 1))
                P.tt("dve", xres[:, t, hf * 512:(hf + 1) * 512], bank(b), xt[:, hf * 512:(hf + 1) * 512], ALU.add)

        for kc in range(KC):
            P.dma("pool", w_mo_sb[:, kc, :], w_mem_o[kc * 128:(kc + 1) * 128, :])
        hmT = _view(R2, o5, [128, KC, MEM], BF16)
        mkT = _view(R2, o5 + 4096, [128, 8, MEM], BF16)
        mv = _view(R2, o5 + 8192, [128, 2, 4, 258], BF16)
        o5b = o5 + 8192 + 4128
        MT = [_view(R2, o5b + i * 4096, [128, D], F32) for i in range(2)]
        SQ5 = _view(R2, o5b + 8192, [128, D], F32)
        PT5 = [_view(R2, o5b + 12288 + i * 1024, [128, 512], BF16) for i in range(2)]
        motok = _view(R2, o5b + 14336, [128, 4, D], BF16)
        XN5 = [_view(R2, o5b + 22528 + i * 2048, [128, D], BF16) for i in range(2)]
        MQN = [_view(R2, o5b + 26624 + i * 2048, [128, D], BF16) for i in range(2)]
        MQT = [_view(R2, 16384 + i * 8192, [128, 8, 512], BF16) for i in range(2)]
        moT = _view(R2, o5b, [128, KC, 512], BF16)
        assert o5b + 30720 <= 131072

        P.memset("pool", mv[:, :, :, 256:258], 1.0)

        def head_norm(src_banks, sq, dst_bf, stcol):
            for i, b in enumerate(src_banks):
                P.act(sq[:, i * 512:(i + 1) * 512], bank(b), AF.Square)
            P.reduce("dve", st2[:, stcol:stcol + 4], sq.rearrange("p (g d) -> p g d", d=256))
            rms_stats(st2[:, stcol:stcol + 4], 256.0, st2[:, stcol + 4:stcol + 8], 4)
            for i, b in enumerate(src_banks):
                P.tt("dve", dst_bf[:, i * 512:(i + 1) * 512].rearrange("p (g d) -> p g d", d=256),
                     bank(b).rearrange("p (g d) -> p g d", d=256),
                     _bc(st2[:, stcol + 4 + 2 * i:stcol + 6 + 2 * i], 2, 256), ALU.mult)

        for mt in range(2):
            mtile = MT[mt]
            P.dma("sp", mtile, mem[mt * 128:(mt + 1) * 128, :])
            norm_transpose(mtile, XN5[mt], wcol_memkv, hmT[:, :, mt * 128:(mt + 1) * 128], 6)
        for mt in range(2):
            for g in range(4):
                for kc in range(KC):
                    P.mm(bank(g), hmT[:, kc, mt * 128:(mt + 1) * 128], w_kv_sb[:, kc, g * 512:(g + 1) * 512],
                         start=(kc == 0), stop=(kc == KC - 1))
            head_norm((0, 1), SQ5, MQN[mt], 0)
            P.copy("act", mv[:, mt, 0:2, 0:256], bank(2).rearrange("p (h d) -> p h d", d=256))
            P.copy("act", mv[:, mt, 2:4, 0:256], bank(3).rearrange("p (h d) -> p h d", d=256))
            pb7 = bank(7, BF16)
            for j in range(8):
                P.tr(pb7[:, j * 128:(j + 1) * 128], MQN[mt][:, j * 128:(j + 1) * 128], ident[:])
            pv = pb7.rearrange("p (h f t) -> p h f t", f=2, t=128)
            dv_ = mkT[:, :, mt * 128:(mt + 1) * 128].rearrange("p (h f) t -> p h f t", f=2)
            for f in range(2):
                P.ts("dve", dv_[:, :, f, :], pv[:, :, f, :], wcol_mk[:, f:f + 1], ALU.mult)

        SQH = [SQ5[:, 0:512], SQ5[:, 512:1024]]
        XN5b = [_view(R2, 50176 + i * 2048, [128, D], BF16) for i in range(3)]
        HQTb = [_view(R2, 50176 + 6144 + i * 2048, [128, KC, 128], BF16) for i in range(3)]
        MQNb = [_view(R2, 50176 + 12288 + i * 2048, [128, D], BF16) for i in range(3)]
        QF = [_view(R2, 50176 + 18432 + i * 2048, [128, 512], F32) for i in range(3)]
        MOTOK = [motok, _view(R2, 50176 + 24576, [128, 4, D], BF16)]
        MOT = [moT, _view(R2, o5b + 22528, [128, KC, 512], BF16)]
        accv5 = PS[:, 4 * 512:6 * 512].rearrange("p (q n) -> p q n", n=256)
        def q_stage(c):
            mq = MQT[c % 2]
            for tl in range(4):
                t = c * 4 + tl
                xn, hq, mqn = XN5b[t % 3], HQTb[t % 3], MQNb[t % 3]
                norm_prep(xres[:, t, :], xn)
                norm_tr(xn, wcol_memq, hq, 6)
                for g in range(2):
                    qf = QF[(2 * t + g) % 3]
                    for kc in range(KC):
                        P.mm(bank(g), hq[:, kc, :], w_mq_sb[:, kc, g * 512:(g + 1) * 512],
                             start=(kc == 0), stop=(kc == KC - 1))
                    s0 = 16 * (t % 2) + 4 * g
                    P.memset("dve", st2[:, s0:s0 + 2], 0.0)
                    for hh in range(2):
                        P.act(SQH[g][:, hh * 256:(hh + 1) * 256], bank(g)[:, hh * 256:(hh + 1) * 256], AF.Square,
                              accum_out=st2[:, s0 + hh:s0 + hh + 1])
                    P.copy("act", qf, bank(g))
                    rms_stats(st2[:, s0:s0 + 2], 256.0, st2[:, s0 + 2:s0 + 4], 2)
                    P.tt("pool", mqn[:, g * 512:(g + 1) * 512].rearrange("p (g d) -> p g d", d=256),
                         qf.rearrange("p (g d) -> p g d", d=256),
                         _bc(st2[:, s0 + 2:s0 + 4], 2, 256), ALU.mult)
                pbq = bank(7, BF16)
                for j in range(8):
                    P.tr(pbq[:, j * 128:(j + 1) * 128], mqn[:, j * 128:(j + 1) * 128], ident[:])
                pv = pbq.rearrange("p (h f t) -> p h f t", f=2, t=128)
                dv_ = mq[:, :, tl * 128:(tl + 1) * 128].rearrange("p (h f) t -> p h f t", f=2)
                for f in range(2):
                    P.ts("dve", dv_[:, :, f, :], pv[:, :, f, :], wcol_mq[:, f:f + 1], ALU.mult)
        def heads_stage(c):
            mq = MQT[c % 2]
            motok = MOTOK[c % 2]
            moT = MOT[c % 2]
            for h in range(4):
                sumv = bank(2)[:, 8 * h:8 * h + 4]
                for mt in range(2):
                    pt = PT5[mt]
                    for f in range(2):
                        P.mm(bank(7), mkT[:, 2 * h + f, mt * 128:(mt + 1) * 128], mq[:, 2 * h + f, :],
                             start=(f == 0), stop=(f == 1))
                    P.act(pt[:], bank(7), AF.Exp, scale=1.0 / 16.0)
                    for tl in range(4):
                        P.mm(accv5[:, tl, :], pt[:, tl * 128:(tl + 1) * 128], mv[:, mt, h, 0:256],
                             start=(mt == 0 and tl in (0, 2)), stop=(mt == 1), skip=True)
                        P.mm(sumv[:, tl:tl + 1], pt[:, tl * 128:(tl + 1) * 128], mv[:, mt, h, 256:257],
                             start=(mt == 0 and tl == 0 and h == 0), stop=(mt == 1), skip=True)
                so = 32 + 4 * (h % 2)
                P.recip(st1[:, so:so + 4], sumv)
                P.tt("dve", motok[:, :, h * 256:(h + 1) * 256], accv5, _bc(st1[:, so:so + 4], 2, 256), ALU.mult)
            pb6 = bank(6, BF16)
            for tl in range(4):
                for kc in range(KC):
                    P.tr(pb6[:, kc * 128:(kc + 1) * 128], motok[:, tl, kc * 128:(kc + 1) * 128], ident[:])
                P.copy("act", moT[:, :, tl * 128:(tl + 1) * 128], pb6.rearrange("p (k t) -> p k t", t=128))
            for tl in range(4):
                t = c * 4 + tl
                for hf in range(2):
                    for kc in range(KC):
                        P.mm(bank(3), moT[:, kc, tl * 128:(tl + 1) * 128], w_mo_sb[:, kc, hf * 512:(hf + 1) * 512],
                             start=(kc == 0), stop=(kc == KC - 1))
                    P.tt("dve", xres[:, t, hf * 512:(hf + 1) * 512], bank(3),
                         xres[:, t, hf * 512:(hf + 1) * 512], ALU.add)

        q_stage(0)
        for c in range(4):
            if c + 1 < 4:
                q_stage(c + 1)
            heads_stage(c)

        hT = _view(R2, 0, [128, KC, S], BF16)
        WU = [_view(R2, 32768 + i * 16384, [128, KC, 1024], BF16) for i in range(2)]
        WD = [_view(R2, 65536 + i * 16384, [128, 8, 1024], BF16) for i in range(2)]
        AT = [_view(R2, 98304 + i * 8192, [128, 8, 512], BF16) for i in range(2)]
        OUTS = [_view(R2, 114688 + i * 4096, [128, D], F32) for i in range(2)]
        RL = [_view(R2, 122880 + i * 2048, [128, 512], F32) for i in range(2)]
        XN6 = [_view(R2, 126976 + i * 2048, [128, D], BF16) for i in range(2)]

        def load_mlp_w(qf):
            for kc in range(KC):
                P.dma("pool", WU[qf % 2][:, kc, :], w_up[kc * 128:(kc + 1) * 128, qf * 1024:(qf + 1) * 1024])
            for fc in range(8):
                r0 = qf * 1024 + fc * 128
                P.dma("pool", WD[qf % 2][:, fc, :], w_down[r0:r0 + 128, :])

        load_mlp_w(0)
        for t in range(NT):
            norm_transpose(xres[:, t, :], XN6[t % 2], wcol_mlp, hT[:, :, t * 128:(t + 1) * 128], 6 + t % 2)
        load_mlp_w(1)
        cnt = 0
        for qf in range(4):
            wu, wd = WU[qf % 2], WD[qf % 2]
            if qf in (1, 2):
                pass
            for c in range(4):
                at = AT[c % 2]
                for fcl in range(8):
                    b = 4 + cnt % 2
                    rl = RL[cnt % 2]
                    cnt += 1
                    for kc in range(KC):
                        P.mm(bank(b), wu[:, kc, fcl * 128:(fcl + 1) * 128], hT[:, kc, c * 512:(c + 1) * 512],
                             start=(kc == 0), stop=(kc == KC - 1))
                    P.act(rl[:], bank(b), AF.Relu)
                    P.tt("pool" if fcl % 2 else "dve", at[:, fcl, :], rl[:], rl[:], ALU.mult)
                for tl in range(4):
                    t = c * 4 + tl
                    for hf in range(2):
                        b = 2 * (tl % 2) + hf
                        for fcl in range(8):
                            P.mm(bank(b), at[:, fcl, tl * 128:(tl + 1) * 128], wd[:, fcl, hf * 512:(hf + 1) * 512],
                                 start=(fcl == 0), stop=(fcl == 7))
                        if qf < 3:
                            P.tt("dve", xres[:, t, hf * 512:(hf + 1) * 512], bank(b),
                                 xres[:, t, hf * 512:(hf + 1) * 512], ALU.add)
                        else:
                            P.tt("dve", OUTS[t % 2][:, hf * 512:(hf + 1) * 512], bank(b),
                                 xres[:, t, hf * 512:(hf + 1) * 512], ALU.add)
                    if qf == 3:
                        P.dma("sp", y[t * 128:(t + 1) * 128, :], OUTS[t % 2], is_out=True)
            if qf + 2 < 4:
                load_mlp_w(qf + 2)

        with nc.Block() as block:
            P.emit(block, sem_ctx)
    return nc


_NC_CACHE = {}


def kernel(**inputs):
    if "nc" not in _NC_CACHE:
        _NC_CACHE["nc"] = build_program()
    nc = _NC_CACHE["nc"]
    f32 = lambda a: np.ascontiguousarray(np.asarray(a, dtype=np.float32))
    shared = {}
    for name in ("norm_mix_w", "w_in", "b_forget", "diff_q_norm_w", "diff_k_norm_w", "lambda_q1", "lambda_k1",
                 "lambda_q2", "lambda_k2", "diff_subln_w", "fox_q_norm_w", "fox_k_norm_w", "w_out",
                 "norm_mem_q_w", "norm_mem_kv_w", "w_mem_q", "w_mem_kv", "mem_q_norm_w", "mem_k_norm_w",
                 "w_mem_o", "norm_mlp_w", "w_up", "w_down"):
        a = f32(inputs[name])
        if name in ("w_in", "w_out", "w_mem_q", "w_mem_kv", "w_mem_o", "w_up", "w_down"):
            shared[name] = np.ascontiguousarray(a[0])
        else:
            shared[name] = np.ascontiguousarray(a.reshape(1, -1))
    x = f32(inputs["x"])
    mem = f32(inputs["mem"])
    pos = np.ascontiguousarray(np.asarray(inputs["positions"], dtype=np.int32))
    in_maps = []
    for b in range(N_CORES):
        m = dict(shared)
        m["x"] = np.ascontiguousarray(x[b])
        m["mem"] = np.ascontiguousarray(mem[b])
        m["positions"] = np.ascontiguousarray(pos[b].reshape(1, S))
        in_maps.append(m)
    res = run_bass_kernel_spmd(nc, in_maps, core_ids=list(range(N_CORES)))
    out = np.stack([np.asarray(r["y"], dtype=np.float32) for r in res.results], axis=0)
    return out
```

```python
import math
import numpy as np
import concourse.bass as bass
import concourse.mybir as mybir
from concourse.bass_utils import run_bass_kernel_spmd

F32 = mybir.dt.float32
BF16 = mybir.dt.bfloat16
I32 = mybir.dt.int32
U8 = mybir.dt.uint8
AF = mybir.ActivationFunctionType
ALU = mybir.AluOpType
AX = mybir.AxisListType

S = 2048
D = 1024
NT = 16
KC = 8
MEM = 256
IN_COLS = 3080
EPS = 1e-6
LAM_INIT = 0.8 - 0.6 * math.exp(0.0)
N_CORES = 8


class Op:
    __slots__ = ("eng", "fn", "pos", "dma", "sig", "sem", "val", "vc", "waits", "id", "preds", "cost",
                 "nbytes", "is_out", "prio", "start", "fin", "succs", "npred", "opreds", "rdy")


class Prog:
    ENGS = ("pe", "act", "dve", "pool", "sp")
    SEM_LIMIT = 30000
    SCHEDULE = True

    def __init__(self, nc):
        self.nc = nc
        self.allops = []
        self.ops = {e: [] for e in self.ENGS}
        self.acc = {}
        self.dma_sems = {}
        self.out_dmas = []

    @staticmethod
    def _region(ap):
        sp = str(ap.space)
        if sp not in ("SB", "PSUM"):
            return None
        aps = ap.ap
        esz = mybir.dt.size(ap.dtype)
        pstride, npart = aps[0]
        off = ap.offset
        if pstride == 0:
            p0, f0 = 0, off
        else:
            p0, f0 = off // pstride, off % pstride
        dims = sorted([(abs(s_), c) for s_, c in aps[1:] if c > 1])
        ivs = [(0, 1)]
        for s_, c in dims:
            if s_ == 0:
                continue
            span = ivs[-1][1] - ivs[0][0]
            if s_ <= span or len(ivs) * c > 64:
                ivs = [(ivs[0][0], ivs[-1][1] + (c - 1) * s_)]
            else:
                ivs = [(lo + i * s_, hi + i * s_) for i in range(c) for lo, hi in ivs]
                ivs.sort()
        out = []
        for lo, hi in ivs:
            lo, hi = (f0 + lo) * esz, (f0 + hi) * esz
            if sp == "PSUM":
                lo = (lo // 2048) * 2048
                hi = ((hi + 2047) // 2048) * 2048
            if out and lo <= out[-1][1]:
                if hi > out[-1][1]:
                    out[-1] = (out[-1][0], hi)
            else:
                out.append((lo, hi))
        if sp == "PSUM":
            return (ap.tensor.name, True, 0, 128, tuple(out))
        return (ap.tensor.name, False, p0, p0 + npart, tuple(out))

    @staticmethod
    def _ov(a, b):
        if a[0][0] >= b[-1][1] or b[0][0] >= a[-1][1]:
            return False
        for lo, hi in a:
            for lo2, hi2 in b:
                if lo < hi2 and lo2 < hi:
                    return True
        return False

    @staticmethod
    def _cov(new, old):
        for lo, hi in old:
            ok = False
            for lo2, hi2 in new:
                if lo2 <= lo and hi <= hi2:
                    ok = True
                    break
            if not ok:
                return False
        return True

    def _access(self, op, ap, is_write, deps):
        r = self._region(ap)
        if r is None:
            return
        name, psum, p0, p1, ivs = r
        lst = self.acc.get(name)
        if lst is None:
            lst = []
            self.acc[name] = lst
        conflict_w = is_write or psum
        new = []
        mine = []
        for e in lst:
            eivs, ep0, ep1, eop, ew, ecw, more = e
            if eop is op:
                new.append(e)
                continue
            pov = ep0 < p1 and p0 < ep1
            ov = pov and self._ov(ivs, eivs)
            same = (eop.eng == op.eng) and not eop.dma and not op.dma
            if ov:
                if same:
                    if ew or is_write:
                        deps.append(eop)
                        deps.extend(more)
                elif conflict_w or ecw:
                    deps.append(eop)
                    deps.extend(more)
            cover = ov and p0 <= ep0 and ep1 <= p1 and self._cov(ivs, eivs)
            if cover and is_write:
                continue
            if cover and same and psum:
                op.opreds.append(eop)
                continue
            if cover and same and (not ew) and (not is_write):
                mine.append(eop)
                mine.extend(more)
                continue
            new.append(e)
        new.append((ivs, p0, p1, op, is_write, conflict_w, mine))
        self.acc[name] = new

    def add(self, eng, fn, reads, writes, dma=False, is_out=False, cost=100.0, nbytes=0):
        op = Op()
        op.eng, op.fn, op.dma, op.sig = eng, fn, dma, False
        op.id = len(self.allops)
        op.sem = None
        op.val = 0
        op.cost = cost
        op.nbytes = nbytes
        op.is_out = is_out
        op.opreds = []
        deps = []
        for ap in reads:
            if ap is not None and not isinstance(ap, (int, float)):
                self._access(op, ap, False, deps)
        for ap in writes:
            if ap is not None:
                self._access(op, ap, True, deps)
        seen = set()
        preds = []
        for a in deps:
            if a.id not in seen:
                seen.add(a.id)
                preds.append(a)
        op.preds = preds
        self.allops.append(op)
        return op

    def _schedule(self):
        import heapq
        ops = self.allops
        for op in ops:
            op.succs = []
        for op in ops:
            allp = {a.id: a for a in op.preds}
            for a in op.opreds:
                allp[a.id] = a
            op.npred = len(allp)
            for a in allp.values():
                a.succs.append(op)
        for op in reversed(ops):
            m = 0.0
            for s_ in op.succs:
                if s_.prio > m:
                    m = s_.prio
            lat = op.cost + (2000.0 + op.nbytes / 360.0 if op.dma else 0.0)
            op.prio = m + lat
        LAT = 250.0
        ready = {e: [] for e in self.ENGS}
        for op in ops:
            if op.npred == 0:
                heapq.heappush(ready[op.eng], (-op.prio, op.id, op))
        free = {e: 0.0 for e in self.ENGS}
        events = []
        pending = []
        for op in ops:
            op.rdy = 0.0
        order = {e: [] for e in self.ENGS}
        pipe_free = 0.0
        t = 0.0
        ndone = 0
        n = len(ops)
        while ndone < n:
            progressed = False
            for e in self.ENGS:
                if free[e] <= t and ready[e]:
                    _, _, op = heapq.heappop(ready[e])
                    op.start = t
                    if op.dma:
                        free[e] = t + op.cost
                        xs = max(t + op.cost, pipe_free)
                        pipe_free = xs + op.nbytes / 360.0
                        op.fin = pipe_free + 2000.0
                    else:
                        free[e] = t + op.cost
                        op.fin = free[e]
                    heapq.heappush(events, (op.fin, op.id, op))
                    order[e].append(op)
                    progressed = True
            cand = []
            if events:
                cand.append(events[0][0])
            if pending:
                cand.append(pending[0][0])
            for e in self.ENGS:
                if ready[e] and free[e] > t:
                    cand.append(free[e])
            if not progressed and not cand:
                raise RuntimeError("scheduler deadlock")
            if cand:
                nt = min(cand)
                if nt > t:
                    t = nt
            while events and events[0][0] <= t:
                _, _, op = heapq.heappop(events)
                ndone += 1
                for s_ in op.succs:
                    s_.npred -= 1
                    rt = op.fin if (s_.eng == op.eng and not op.dma) else op.fin + LAT
                    if rt > s_.rdy:
                        s_.rdy = rt
                    if s_.npred == 0:
                        if s_.rdy <= t:
                            heapq.heappush(ready[s_.eng], (-s_.prio, s_.id, s_))
                        else:
                            heapq.heappush(pending, (s_.rdy, s_.id, s_))
            while pending and pending[0][0] <= t:
                _, _, s_ = heapq.heappop(pending)
                heapq.heappush(ready[s_.eng], (-s_.prio, s_.id, s_))
        self.est_ns = t
        return order

    def _finalize(self):
        if self.SCHEDULE:
            order = self._schedule()
            glob = sorted(self.allops, key=lambda o: (o.start, o.id))
        else:
            order = {e: [o for o in self.allops if o.eng == e] for e in self.ENGS}
            glob = list(self.allops)
        self.ops = order
        for e in self.ENGS:
            for i, op in enumerate(order[e]):
                op.pos = i
        dma_last = {}
        dma_cnt = {}
        extra = {}
        for e in self.ENGS:
            pool = self.dma_sems.get(e)
            i = 0
            for op in order[e]:
                if not op.dma:
                    continue
                sem = pool[i % len(pool)]
                i += 1
                prev = dma_last.get(sem)
                if prev is not None:
                    extra[op.id] = prev
                dma_last[sem] = op
                dma_cnt[sem] = dma_cnt.get(sem, 0) + 16
                op.sem, op.val, op.sig = sem, dma_cnt[sem], True
                if op.is_out:
                    self.out_dmas.append(op)
        known = {e: {} for e in self.ENGS}
        for op in glob:
            kn = known[op.eng]
            deps = list(op.preds)
            if op.id in extra:
                deps.append(extra[op.id])
            deps.sort(key=lambda a: -a.pos)
            waits = []
            for a in deps:
                same = (a.eng == op.eng) and not a.dma and not op.dma
                if same and op.eng == "pe":
                    continue
                key = ("d", a.id) if a.dma else a.eng
                need = 1 if a.dma else a.pos
                if kn.get(key, -1) >= need:
                    continue
                waits.append(a)
                a.sig = True
                for k, v in a.vc.items():
                    if kn.get(k, -1) < v:
                        kn[k] = v
            op.waits = waits
            vc = dict(kn)
            if op.dma:
                vc[("d", op.id)] = 1
            else:
                vc[op.eng] = op.pos
            op.vc = vc

    def emit(self, block, sems_for_engine):
        self._finalize()
        for e in self.ENGS:
            pool = sems_for_engine[e]
            cnt, ep = 0, 0
            for op in self.ops[e]:
                if op.dma or not op.sig:
                    continue
                cnt += 1
                op.sem, op.val = pool[ep], cnt
                if cnt >= self.SEM_LIMIT:
                    cnt, ep = 0, ep + 1

        def run(engname, eng):
            for op in self.ops[engname]:
                best = {}
                for a in op.waits:
                    if best.get(a.sem, 0) < a.val:
                        best[a.sem] = a.val
                for sem, val in best.items():
                    eng.wait_ge(sem, val)
                ins = op.fn(eng)
                if op.sig:
                    ins.then_inc(op.sem, 16 if op.dma else 1)
            if engname == "sp":
                best = {}
                for a in self.out_dmas:
                    if best.get(a.sem, 0) < a.val:
                        best[a.sem] = a.val
                for sem, val in best.items():
                    eng.wait_ge(sem, val)

        @block.tensor
        def _(t):
            run("pe", t)

        @block.scalar
        def _(s):
            run("act", s)

        @block.vector
        def _(v):
            run("dve", v)

        @block.gpsimd
        def _(g):
            run("pool", g)

        @block.sync
        def _(sy):
            run("sp", sy)

    @staticmethod
    def _fs(ap):
        n = 1
        for s_ in ap.shape[1:]:
            n *= s_
        return n

    @staticmethod
    def _is_psum(ap):
        return str(ap.space) == "PSUM"

    def _vcost(self, eng, out, ins):
        n = self._fs(out)
        ps = any(self._is_psum(a) for a in ins if a is not None and not isinstance(a, (int, float))) or self._is_psum(out)
        if eng == "pool":
            return 150.0 + n * 1.9
        return (125.0 if ps else 65.0) + n * 1.04

    def mm(self, out, lhsT, rhs, start=True, stop=True, skip=False):
        n = self._fs(rhs)
        mult = 4.0 if rhs.dtype == F32 else 1.0
        return self.add("pe", lambda e: e.matmul(out, lhsT, rhs, start=start, stop=stop,
                                                 skip_group_check=skip), [lhsT, rhs], [out],
                        cost=mult * max(64, n) / 2.4 + 8.0)

    def tr(self, out, in_, ident):
        return self.add("pe", lambda e: e.transpose(out, in_, ident), [in_, ident], [out], cost=75.0)

    def act(self, out, in_, func, bias=0.0, scale=1.0, accum_out=None):
        rd = [in_]
        if not isinstance(bias, (int, float)):
            rd.append(bias)
        if not isinstance(scale, (int, float)):
            rd.append(scale)
        kw = {}
        if accum_out is not None:
            kw["accum_out"] = accum_out
        return self.add("act", lambda e: e.activation(out=out, in_=in_, func=func, bias=bias,
                                                      scale=scale, **kw), rd, [out, accum_out],
                        cost=200.0 + 0.8 * self._fs(in_))

    def tt(self, eng, out, in0, in1, op):
        return self.add(eng, lambda e: e.tensor_tensor(out=out, in0=in0, in1=in1, op=op),
                        [in0, in1], [out], cost=self._vcost(eng, out, [in0, in1]))

    def ts(self, eng, out, in0, s1, op0, s2=None, op1=None):
        rd = [in0]
        if not isinstance(s1, (int, float)):
            rd.append(s1)
        if s2 is not None and not isinstance(s2, (int, float)):
            rd.append(s2)
        c = self._vcost(eng, out, [in0])
        if op1 is None:
            return self.add(eng, lambda e: e.tensor_scalar(out=out, in0=in0, scalar1=s1, scalar2=None,
                                                           op0=op0), rd, [out], cost=c)
        return self.add(eng, lambda e: e.tensor_scalar(out=out, in0=in0, scalar1=s1, scalar2=s2,
                                                       op0=op0, op1=op1), rd, [out], cost=c)

    def stt(self, eng, out, in0, scalar, in1, op0, op1):
        rd = [in0, in1]
        if not isinstance(scalar, (int, float)):
            rd.append(scalar)
        return self.add(eng, lambda e: e.scalar_tensor_tensor(out=out, in0=in0, scalar=scalar, in1=in1,
                                                              op0=op0, op1=op1), rd, [out],
                        cost=self._vcost(eng, out, [in0, in1]))

    def copy(self, eng, out, in_):
        if eng == "act":
            return self.add("act", lambda e: e.activation(out=out, in_=in_, func=AF.Copy), [in_], [out],
                            cost=200.0 + 0.8 * self._fs(in_))
        return self.add(eng, lambda e: e.tensor_copy(out=out, in_=in_), [in_], [out],
                        cost=self._vcost(eng, out, [in_]))

    def memset(self, eng, ap, val):
        return self.add(eng, lambda e: e.memset(ap, val), [], [ap], cost=60.0 + 0.3 * self._fs(ap))

    def reduce(self, eng, out, in_, op=ALU.add):
        return self.add(eng, lambda e: e.tensor_reduce(out=out, in_=in_, axis=AX.X, op=op), [in_], [out],
                        cost=65.0 + 1.04 * self._fs(in_))

    def recip(self, out, in_):
        return self.add("dve", lambda e: e.reciprocal(out=out, in_=in_), [in_], [out],
                        cost=self._vcost("dve", out, [in_]))

    def affsel(self, out, in_, pattern, cmp, fill, base, cm):
        return self.add("pool", lambda e: e.affine_select(out=out, in_=in_, pattern=pattern, compare_op=cmp,
                                                          fill=fill, base=base, channel_multiplier=cm),
                        [in_], [out], cost=150.0 + 1.0 * self._fs(out))

    def dma(self, q, out, in_, is_out=False, slow=False):
        if slow:
            fn = lambda e: e.dma_start(out=out, in_=in_, allow_slow_non_contiguous=True)
        else:
            fn = lambda e: e.dma_start(out=out, in_=in_)
        nb = self._fs(out) * mybir.dt.size(out.dtype) * out.shape[0]
        return self.add(q, fn, [in_], [out], dma=True, is_out=is_out,
                        cost=(1000.0 if q == "pool" else 60.0), nbytes=nb)


def _view(base, off, shape, dt):
    esz = mybir.dt.size(dt)
    n = 1
    for s in shape[1:]:
        n *= s
    v = base[0:shape[0], off:off + n * esz].bitcast(dt)
    if len(shape) == 2:
        return v
    names = [f"d{i}" for i in range(1, len(shape))]
    pat = "p (" + " ".join(names) + ") -> p " + " ".join(names)
    kw = {names[i]: shape[i + 1] for i in range(len(names) - 1)}
    return v.rearrange(pat, **kw)


def _bc(ap, axis, n):
    shp = list(ap.shape)
    shp.insert(axis, n)
    return ap.unsqueeze(axis).broadcast_to(shp)


def build_program():
    nc = bass.Bass("TRN2", target_bir_lowering=False)

    def din(name, shape, dt=F32):
        return nc.dram_tensor(name, list(shape), dt, kind="ExternalInput").ap()

    x = din("x", [S, D])
    mem = din("mem", [MEM, D])
    positions = din("positions", [1, S], I32)
    norm_mix_w = din("norm_mix_w", [1, D])
    w_in = din("w_in", [D, IN_COLS])
    b_forget = din("b_forget", [1, 8])
    diff_q_norm_w = din("diff_q_norm_w", [1, 64])
    diff_k_norm_w = din("diff_k_norm_w", [1, 64])
    lambda_q1 = din("lambda_q1", [1, 64])
    lambda_k1 = din("lambda_k1", [1, 64])
    lambda_q2 = din("lambda_q2", [1, 64])
    lambda_k2 = din("lambda_k2", [1, 64])
    diff_subln_w = din("diff_subln_w", [1, 128])
    fox_q_norm_w = din("fox_q_norm_w", [1, 64])
    fox_k_norm_w = din("fox_k_norm_w", [1, 64])
    w_out = din("w_out", [D, D])
    norm_mem_q_w = din("norm_mem_q_w", [1, D])
    norm_mem_kv_w = din("norm_mem_kv_w", [1, D])
    w_mem_q = din("w_mem_q", [D, D])
    w_mem_kv = din("w_mem_kv", [D, 2 * D])
    mem_q_norm_w = din("mem_q_norm_w", [1, 256])
    mem_k_norm_w = din("mem_k_norm_w", [1, 256])
    w_mem_o = din("w_mem_o", [D, D])
    norm_mlp_w = din("norm_mlp_w", [1, D])
    w_up = din("w_up", [D, 4 * D])
    w_down = din("w_down", [4 * D, D])
    y = nc.dram_tensor("y", [S, D], F32, kind="ExternalOutput").ap()

    from contextlib import ExitStack
    with ExitStack() as es:
        def sb(name, shape, dt):
            return es.enter_context(nc.sbuf_tensor(name, list(shape), dt))

        RX = sb("RX", [128, 65536], U8)
        R2 = sb("R2", [128, 131072], U8)
        PS = es.enter_context(nc.psum_tensor("PS", [128, 4096], F32))
        ident = sb("ident", [128, 128], BF16)
        tri = sb("tri", [128, 128], F32)
        ones = sb("ones", [128, 128], F32)
        wcol_mix = sb("wcol_mix", [128, 8], F32)
        wcol_memq = sb("wcol_memq", [128, 8], F32)
        wcol_memkv = sb("wcol_memkv", [128, 8], F32)
        wcol_mlp = sb("wcol_mlp", [128, 8], F32)
        wq_adj = sb("wq_adj", [128, 1], F32)
        wk_adj = sb("wk_adj", [128, 1], F32)
        wfq = sb("wfq", [128, 1], F32)
        wfk = sb("wfk", [128, 1], F32)
        w16q = sb("w16q", [128, 16], F32)
        w16k = sb("w16k", [128, 16], F32)
        subln = sb("subln", [128, 128], F32)
        wcol_mq = sb("wcol_mq", [128, 2], F32)
        wcol_mk = sb("wcol_mk", [128, 2], F32)
        bfg = sb("bfg", [128, 8], F32)
        lamv = sb("lamv", [128, 4, 64], F32)
        lamt = sb("lamt", [128, 8], F32)
        posi = sb("posi", [128, 16], I32)
        posf = sb("posf", [128, 16], F32)
        invf = sb("invf", [128, 8], F32)
        ang = sb("ang", [128, 16, 8], F32)
        angk = sb("angk", [128, 16, 8], F32)
        angi = sb("angi", [128, 16, 8], I32)
        angr = sb("angr", [128, 16, 8], F32)
        angc = sb("angc", [128, 16, 8], F32)
        cosT = sb("cosT", [128, 16, 8], F32)
        sinT = sb("sinT", [128, 16, 8], F32)
        st1 = sb("st1", [128, 64], F32)
        st2 = sb("st2", [128, 64], F32)
        st3 = sb("st3", [128, 64], F32)
        nls = sb("nls", [128, 16, 8], F32)
        Tb = sb("Tb", [128, 16, 8], F32)
        Pinc = sb("Pinc", [128, 16, 8], F32)
        gcol = sb("gcol", [128, 16, 8], F32)
        gtmp = sb("gtmp", [128, 16, 8], F32)
        biasT = sb("biasT", [128, 16, 8, 8], F32)
        zt = sb("zt", [128, 8], F32)
        et = sb("et", [128, 8], F32)

        n_eng_sems = 1
        sem_ctx = {}
        for e in Prog.ENGS:
            sem_ctx[e] = [es.enter_context(nc.semaphore(f"s_{e}{i}")) for i in range(n_eng_sems)]
        P = Prog(nc)
        P.dma_sems["sp"] = [es.enter_context(nc.semaphore(f"d_sp{i}")) for i in range(12)]
        P.dma_sems["pool"] = [es.enter_context(nc.semaphore(f"d_pool{i}")) for i in range(12)]

        def bank(b, dt=F32):
            v = PS[:, b * 512:(b + 1) * 512]
            return v if dt == F32 else v.bitcast(dt)

        P.memset("pool", ident[:], 1.0)
        P.affsel(ident[:], ident[:], [[-1, 128]], ALU.is_equal, 0.0, 0, 1)
        P.memset("pool", tri[:], 1.0)
        P.affsel(tri[:], tri[:], [[1, 128]], ALU.is_ge, 0.0, 0, -1)
        P.memset("pool", ones[:], 1.0)

        identf = sb("identf", [128, 128], F32)
        P.memset("pool", identf[:], 1.0)
        P.affsel(identf[:], identf[:], [[-1, 128]], ALU.is_equal, 0.0, 0, 1)
        w8 = _view(RX, 0, [8, 6, 128], F32)
        w128 = _view(RX, 3072, [1, 4, 128], F32)
        posi16 = _view(RX, 5120, [16, 128], I32)
        posf16 = _view(RX, 5632, [16, 128], F32)
        pcol = [0]

        def col_load(dst, src, nk, slot):
            P.dma("sp", w8[0:nk, slot, :], src.rearrange("o (k q) -> (o k) q", q=128))
            c0 = pcol[0]
            pcol[0] += nk
            P.mm(bank(7)[:, c0:c0 + nk], w8[0:nk, slot, :], identf[0:nk, 0:nk], True, True)
            P.copy("dve", dst[:], bank(7)[:, c0:c0 + nk])

        def col64(dst, src, slot):
            P.dma("sp", w128[0:1, slot, 0:64], src)
            P.dma("sp", w128[0:1, slot, 64:128], src)
            c0 = pcol[0]
            pcol[0] += 1
            P.mm(bank(7)[:, c0:c0 + 1], w128[0:1, slot, :], ones[0:1, 0:1], True, True)
            P.copy("dve", dst[:, 0:1], bank(7)[:, c0:c0 + 1])

        col_load(wcol_mix, norm_mix_w, 8, 0)
        col_load(wcol_memq, norm_mem_q_w, 8, 1)
        col_load(wcol_memkv, norm_mem_kv_w, 8, 2)
        col_load(wcol_mlp, norm_mlp_w, 8, 3)
        col_load(wcol_mq, mem_q_norm_w, 2, 4)
        col_load(wcol_mk, mem_k_norm_w, 2, 5)
        for i, (dst, src) in enumerate(((wq_adj, diff_q_norm_w), (wk_adj, diff_k_norm_w), (wfq, fox_q_norm_w),
                                        (wfk, fox_k_norm_w))):
            col64(dst, src, i)
        for dst in (wq_adj, wk_adj):
            P.memset("pool", dst[0:16, :], 1.0)
            P.memset("pool", dst[64:80, :], 1.0)
        P.dma("sp", w16q[:], diff_q_norm_w[:, 0:16].partition_broadcast(128))
        P.dma("sp", w16k[:], diff_k_norm_w[:, 0:16].partition_broadcast(128))
        P.dma("sp", subln[:], diff_subln_w.partition_broadcast(128))
        P.dma("sp", bfg[:], b_forget.partition_broadcast(128))
        for i, src in enumerate((lambda_q1, lambda_k1, lambda_q2, lambda_k2)):
            P.dma("sp", lamv[:, i, :], src.partition_broadcast(128))
        P.dma("sp", posi16, positions.rearrange("o (t q) -> (o t) q", q=128))
        P.copy("dve", posf16, posi16)
        P.mm(bank(7)[:, 64:80], posf16, identf[0:16, 0:16], True, True)

        P.tt("dve", lamv[:, 0, :], lamv[:, 0, :], lamv[:, 1, :], ALU.mult)
        P.tt("dve", lamv[:, 2, :], lamv[:, 2, :], lamv[:, 3, :], ALU.mult)
        P.reduce("dve", lamt[:, 0:1], lamv[:, 0, :])
        P.reduce("dve", lamt[:, 1:2], lamv[:, 2, :])
        P.act(lamt[:, 2:4], lamt[:, 0:2], AF.Exp)
        P.tt("dve", lamt[:, 4:5], lamt[:, 3:4], lamt[:, 2:3], ALU.subtract)
        P.ts("dve", lamt[:, 4:5], lamt[:, 4:5], -LAM_INIT, ALU.add)
        P.ts("dve", subln[:], subln[:], 1.0 - LAM_INIT, ALU.mult)

        inv64 = 500000.0 ** (-np.arange(0, 16, 2, dtype=np.float64) / 16.0)
        inv_hi = inv64.astype(np.float32)
        inv_lo = (inv64 - inv_hi.astype(np.float64)).astype(np.float32)
        invf_lo = zt
        ang2 = gtmp
        for j in range(8):
            P.memset("pool", invf[:, j:j + 1], float(inv_hi[j]))
            P.memset("pool", invf_lo[:, j:j + 1], float(inv_lo[j]))
        P.copy("dve", posf[:], bank(7)[:, 64:80])
        P.tt("dve", ang[:], _bc(posf[:], 2, 8), _bc(invf[:], 1, 16), ALU.mult)
        P.tt("dve", ang2[:], _bc(posf[:], 2, 8), _bc(invf_lo[:], 1, 16), ALU.mult)
        C1 = 6.28125
        C2 = float(np.float32(2 * math.pi - C1))
        C3 = float(np.float32(2 * math.pi - C1 - C2))
        PI_SAFE = 3.1415925
        P.tt("dve", angk[:], ang[:], ang2[:], ALU.add)
        P.ts("dve", angk[:], angk[:], float(np.float32(1.0 / (2 * math.pi))), ALU.mult)
        P.copy("dve", angi[:], angk[:])
        P.copy("dve", angk[:], angi[:])
        P.stt("dve", angr[:], angk[:], -C1, ang[:], ALU.mult, ALU.add)
        P.tt("dve", angr[:], angr[:], ang2[:], ALU.add)
        P.stt("dve", angr[:], angk[:], -C2, angr[:], ALU.mult, ALU.add)
        P.stt("dve", angr[:], angk[:], -C3, angr[:], ALU.mult, ALU.add)
        P.ts("dve", angc[:], angr[:], math.pi / 2, ALU.is_gt)
        P.ts("dve", angk[:], angr[:], math.pi / 2, ALU.add)
        P.stt("dve", angc[:], angc[:], -2 * math.pi, angk[:], ALU.mult, ALU.add)
        P.ts("dve", angr[:], angr[:], -PI_SAFE, ALU.max, PI_SAFE, ALU.min)
        P.ts("dve", angc[:], angc[:], -PI_SAFE, ALU.max, PI_SAFE, ALU.min)
        P.act(sinT[:], angr[:], AF.Sin)
        P.act(cosT[:], angc[:], AF.Sin)

        QTd = _view(RX, 0, [128, 4, S], BF16)
        KTd = _view(RX, 16384, [128, 4, S], BF16)
        QTf = _view(RX, 32768, [128, 4, S], BF16)
        KTf = _view(RX, 49152, [128, 4, S], BF16)
        xres = _view(RX, 0, [128, NT, D], F32)

        w_in_sb = _view(R2, 0, [128, KC, IN_COLS], BF16)
        Vd = _view(R2, 50176, [128, NT, 4, 130], BF16)
        Vf = _view(R2, 66816, [128, NT, 8, 66], BF16)
        o1 = 83712
        XT = [_view(R2, o1 + i * 4096, [128, D], F32) for i in range(2)]
        XN = [_view(R2, o1 + 8192 + i * 2048, [128, D], BF16) for i in range(2)]
        HTt = [_view(R2, o1 + 12288 + i * 2048, [128, KC, 128], BF16) for i in range(2)]
        SQ = [_view(R2, o1 + 16384 + i * 2048, [128, 512], F32) for i in range(2)]
        QN = [_view(R2, o1 + 20480 + i * 4096, [128, 2048], BF16) for i in range(2)]
        ropeA = _view(R2, o1 + 28672, [128, 16, 16], F32)
        ropeT = [_view(R2, o1 + 29696 + i * 512, [128, 16, 8], F32) for i in range(4)]

        prev_grp = []
        for (c0, c1) in ((0, 1024), (1536, 2560), (1024, 1536), (2560, IN_COLS)):
            grp = []
            for kc in range(KC):
                o = P.dma("pool", w_in_sb[:, kc, c0:c1], w_in[kc * 128:(kc + 1) * 128, c0:c1])
                o.preds.extend(prev_grp)
                grp.append(o)
            prev_grp = grp
        P.memset("pool", Vd[:, :, :, 128:130], 1.0)
        P.memset("pool", Vf[:, :, :, 64:66], 1.0)

        rs_rr = [0]

        def rms_stats(src_ss, n, dst_rs, ncol):
            o = 16 * (rs_rr[0] % 4)
            rs_rr[0] += 1
            P.act(st3[:, o:o + ncol], src_ss, AF.Ln, bias=EPS, scale=1.0 / n)
            P.act(dst_rs, st3[:, o:o + ncol], AF.Exp, scale=-0.5)

        nrm_rr = [0]

        def norm_prep(src, xn):
            k = 4 * (nrm_rr[0] % 2)
            nrm_rr[0] += 1
            P.memset("dve", st1[:, k:k + 1], 0.0)
            P.act(xn, src, AF.Square, accum_out=st1[:, k:k + 1])
            P.act(st1[:, k + 2:k + 3], st1[:, k:k + 1], AF.Ln, bias=EPS, scale=1.0 / D)
            P.act(st1[:, k + 1:k + 2], st1[:, k + 2:k + 3], AF.Exp, scale=-0.5)
            P.ts("dve", xn, src, st1[:, k + 1:k + 2], ALU.mult)

        def norm_tr(xn, wcol, dst, pbank):
            pb = bank(pbank, BF16)
            for kc in range(KC):
                P.tr(pb[:, kc * 128:(kc + 1) * 128], xn[:, kc * 128:(kc + 1) * 128], ident[:])
            P.tt("dve", dst, pb.rearrange("p (k t) -> p k t", t=128), _bc(wcol[:], 2, 128), ALU.mult)

        def norm_transpose(src, xn, wcol, dst, pbank):
            norm_prep(src, xn)
            pb = bank(pbank, BF16)
            for kc in range(KC):
                P.tr(pb[:, kc * 128:(kc + 1) * 128], xn[:, kc * 128:(kc + 1) * 128], ident[:])
            P.tt("dve", dst, pb.rearrange("p (k t) -> p k t", t=128), _bc(wcol[:], 2, 128), ALU.mult)

        pb7 = bank(7, BF16)

        def stA0(t):
            P.dma("sp", XT[t % 2], x[t * 128:(t + 1) * 128, :])
            norm_prep(XT[t % 2], XN[t % 2])

        def stA1(t):
            norm_tr(XN[t % 2], wcol_mix, HTt[t % 2], 6)

        def proj(t, b, c0, n=512, col0=0):
            htt = HTt[t % 2]
            for kc in range(KC):
                P.mm(bank(b)[:, col0:col0 + n], htt[:, kc, :], w_in_sb[:, kc, c0:c0 + n],
                     start=(kc == 0), stop=(kc == KC - 1))

        def stB1(t):
            proj(t, 0, 0)
            proj(t, 1, 512)

        def stC1(t):
            qn = QN[t % 2]
            for i, b in enumerate((0, 1)):
                P.act(SQ[i], bank(b), AF.Square)
                P.reduce("dve", st2[:, i * 8:(i + 1) * 8], SQ[i].rearrange("p (g d) -> p g d", d=64))
            rms_stats(st2[:, 0:16], 64.0, st2[:, 16:32], 16)
            for i, b in enumerate((0, 1)):
                P.tt("dve", qn[:, i * 512:(i + 1) * 512].rearrange("p (g d) -> p g d", d=64),
                     bank(b).rearrange("p (g d) -> p g d", d=64),
                     _bc(st2[:, 16 + i * 8:24 + i * 8], 2, 64), ALU.mult)
                P.tt("dve", ropeA[:, i * 8:(i + 1) * 8, :],
                     bank(b).rearrange("p (g d) -> p g d", d=64)[:, :, 0:16],
                     _bc(st2[:, 16 + i * 8:24 + i * 8], 2, 16), ALU.mult)
            P.tt("pool", ropeA[:, 0:8, :], ropeA[:, 0:8, :], _bc(w16q[:], 1, 8), ALU.mult)
            P.tt("pool", ropeA[:, 8:16, :], ropeA[:, 8:16, :], _bc(w16k[:], 1, 8), ALU.mult)
            cs = _bc(cosT[:, t, :], 1, 16)
            sn = _bc(sinT[:, t, :], 1, 16)
            qv = qn[:, 0:1024].rearrange("p (g d) -> p g d", d=64)
            P.tt("pool", ropeT[0], ropeA[:, :, 0:8], cs, ALU.mult)
            P.tt("pool", ropeT[1], ropeA[:, :, 8:16], sn, ALU.mult)
            P.tt("pool", ropeT[2], ropeA[:, :, 8:16], cs, ALU.mult)
            P.tt("pool", ropeT[3], ropeA[:, :, 0:8], sn, ALU.mult)
            P.tt("pool", qv[:, :, 0:8], ropeT[0], ropeT[1], ALU.subtract)
            P.tt("pool", qv[:, :, 8:16], ropeT[2], ropeT[3], ALU.add)

        def stB2(t):
            proj(t, 2, 1536)
            proj(t, 3, 2048)

        def stC2(t):
            qn = QN[t % 2]
            for i, b in enumerate((2, 3)):
                P.act(SQ[i], bank(b), AF.Square)
                P.reduce("dve", st2[:, 32 + i * 8:40 + i * 8], SQ[i].rearrange("p (g d) -> p g d", d=64))
            rms_stats(st2[:, 32:48], 64.0, st2[:, 48:64], 16)
            for i, b in enumerate((2, 3)):
                P.tt("dve", qn[:, 1024 + i * 512:1536 + i * 512].rearrange("p (g d) -> p g d", d=64),
                     bank(b).rearrange("p (g d) -> p g d", d=64),
                     _bc(st2[:, 48 + i * 8:56 + i * 8], 2, 64), ALU.mult)

        def stD(t):
            qn = QN[t % 2]
            tcols = slice(t * 128, (t + 1) * 128)
            for (c0, dst, wc) in ((0, QTd, wq_adj), (512, KTd, wk_adj), (1024, QTf, wfq), (1536, KTf, wfk)):
                for j in range(4):
                    P.tr(pb7[:, j * 128:(j + 1) * 128], qn[:, c0 + j * 128:c0 + (j + 1) * 128], ident[:])
                P.ts("dve", dst[:, :, tcols], pb7[:, 0:512].rearrange("p (h t) -> p h t", t=128),
                     wc[:, 0:1], ALU.mult)

        def stB3(t):
            proj(t, 4, 1024)
            proj(t, 5, 2560)
            proj(t, 7, 3072, n=8, col0=256)

        def stC3(t):
            P.copy("act", Vd[:, t, :, 0:128], bank(4).rearrange("p (h d) -> p h d", d=128))
            P.copy("act", Vf[:, t, :, 0:64], bank(5).rearrange("p (h d) -> p h d", d=64))
            P.tt("dve", zt[:], bank(7)[:, 256:264], bfg[:], ALU.add)
            P.act(et[:], zt[:], AF.Exp, scale=-1.0)
            P.act(nls[:, t, :], et[:], AF.Ln, bias=1.0)

        stA0(0)
        stA1(0)
        for t in range(NT):
            if t + 1 < NT:
                stA0(t + 1)
            stB1(t)
            if t + 1 < NT:
                stA1(t + 1)
            stC1(t)
            stB2(t)
            stC2(t)
            if t >= 1:
                stD(t - 1)
            stB3(t)
            stC3(t)
        stD(NT - 1)

        nls2 = nls[:].rearrange("p t h -> p (t h)")
        P.mm(bank(7)[:, 0:128], tri[:], nls2, True, True)
        P.mm(bank(6)[:, 0:128], ones[:], nls2, True, True)
        P.copy("dve", Tb[:].rearrange("p t h -> p (t h)"), bank(6)[:, 0:128])
        P.copy("dve", Pinc[:, 0, :], Tb[:, 0, :])
        for j in range(1, NT):
            P.tt("dve", Pinc[:, j, :], Pinc[:, j - 1, :], Tb[:, j, :], ALU.add)
        P.tt("dve", gtmp[:], Pinc[:], Tb[:], ALU.subtract)
        P.tt("dve", gcol[:].rearrange("p t h -> p (t h)"), bank(7)[:, 0:128],
             gtmp[:].rearrange("p t h -> p (t h)"), ALU.add)
        Gmid = Pinc[:].rearrange("p (c four) h -> p c four h", four=4)[:, :, 1, :]
        for kb in range(NT):
            P.tt("dve", biasT[:, kb, 0:4, :], _bc(gcol[:, kb, :], 1, 4), Gmid, ALU.subtract)

        mixedT = _view(R2, 0, [128, KC, S], BF16)
        MIXTOK = [_view(R2, 32768, [128, 4, D], BF16), _view(R2, 122880, [128, 4, D], BF16)]
        PT = [_view(R2, 40960 + i * 1024, [128, 512], BF16) for i in range(3)]
        w_out_sb = _view(R2, 83712, [128, KC, D], BF16)
        e0 = 83712 + 16384
        XT4 = [_view(R2, e0 + i * 4096, [128, D], F32) for i in range(2)]
        QDP = _view(R2, e0, [128, 4, 2, 512], BF16)
        QFP = _view(R2, e0 + 8192, [128, 4, 2, 512], BF16)
        e1 = e0 + 16384
        EPB = [[_view(R2, e1 + k * 2048, [128, 4, 128], F32) for k in range(3)] for i in range(2)]
        assert e1 + 3 * 2048 <= 122880
        P.memset("pool", QDP[:], 0.0)
        P.memset("pool", QFP[:], 0.0)

        def mk_qpad(c, fox):
            def fn():
                for m in range(2):
                    rows = slice(m * 64, (m + 1) * 64)
                    if fox:
                        P.copy("dve", QFP[rows, :, m, :], QTf[rows, :, c * 512:(c + 1) * 512])
                    else:
                        P.copy("dve", QDP[rows, :, m, :], QTd[rows, :, c * 512:(c + 1) * 512])
            return fn

        mk_qpad(0, False)()
        mk_qpad(0, True)()

        for kc in range(KC):
            P.dma("pool", w_out_sb[:, kc, :], w_out[kc * 128:(kc + 1) * 128, :])

        steps = []
        deferred = {}

        def defer(idx, fn):
            deferred.setdefault(idx, []).append(fn)

        def mk_diff_step(i, c, h, m, kb, accv):
            rows = slice(m * 64, (m + 1) * 64)
            qlo = max(kb, 4 * c)
            j0 = qlo - 4 * c
            ncol = (4 - j0) * 128
            sb_ = 4 + i % 3
            pt = PT[i % 3]

            def st_fn():
                P.mm(bank(sb_)[:, 0:ncol], KTd[:, h, kb * 128:(kb + 1) * 128],
                     QDP[:, h, m, j0 * 128:512], True, True)

            def rest_fn():
                P.act(pt[:, 0:ncol], bank(sb_)[:, 0:ncol], AF.Exp, scale=0.125)
                if kb >= 4 * c:
                    P.memset("pool", pt[64:128, 0:64], 0.0)
                for j in range(j0, 4):
                    P.mm(accv[:, j, 0:129], pt[:, (j - j0) * 128:(j - j0 + 1) * 128],
                         Vd[:, kb, h, 0:129], start=(kb == 0 and j in (0, 2)),
                         stop=(kb == 4 * c + j), skip=True)
            return st_fn, rest_fn

        def mk_fox_step(i, c, h, kb, accv):
            pr = h // 2
            qlo = max(kb, 4 * c)
            j0 = qlo - 4 * c
            ncol = (4 - j0) * 128
            sb_ = 4 + i % 3
            pt = PT[i % 3]

            def st_fn():
                P.mm(bank(sb_)[:, 0:ncol], KTf[:, pr, kb * 128:(kb + 1) * 128],
                     QFP[:, pr, h % 2, j0 * 128:512], True, True)

            def rest_fn():
                P.act(pt[:, 0:ncol], bank(sb_)[:, 0:ncol], AF.Exp, scale=0.125,
                      bias=biasT[:, kb, c, h:h + 1])
                if kb >= 4 * c:
                    P.affsel(pt[:, 0:128], pt[:, 0:128], [[1, 128]], ALU.is_ge, 0.0, 0, -1)
                for j in range(j0, 4):
                    P.mm(accv[:, j, 0:65], pt[:, (j - j0) * 128:(j - j0 + 1) * 128],
                         Vf[:, kb, h, 0:65], start=(kb == 0 and j == 0),
                         stop=(kb == 4 * c + j), skip=True)
            return st_fn, rest_fn

        def mk_diff_ep(c, h, accs, par):
            epT, epU, epS = EPB[par]
            mixtok = MIXTOK[c % 2]
            so = 8 + 16 * (h % 2)

            def ep0():
                P.recip(st1[:, so:so + 4], accs[0][:, :, 128])
                P.tt("dve", epU[:], accs[0][:, :, 0:128], _bc(st1[:, so:so + 4], 2, 128), ALU.mult)

            def ep1():
                P.recip(st1[:, so + 4:so + 8], accs[1][:, :, 128])
                P.ts("dve", st1[:, so + 4:so + 8], st1[:, so + 4:so + 8], lamt[:, 4:5], ALU.mult)
                P.tt("dve", epT[:], accs[1][:, :, 0:128], _bc(st1[:, so + 4:so + 8], 2, 128), ALU.mult)
                P.tt("dve", epU[:], epU[:], epT[:], ALU.add)
                P.tt("pool", epS[:], epU[:], epU[:], ALU.mult)
                P.reduce("dve", st1[:, so + 8:so + 12], epS[:])

            def ep2():
                rms_stats(st1[:, so + 8:so + 12], 128.0, st1[:, so + 12:so + 16], 4)
                P.tt("dve", epU[:], epU[:], _bc(st1[:, so + 12:so + 16], 2, 128), ALU.mult)
                P.tt("pool", mixtok[:, :, h * 128:(h + 1) * 128], epU[:], _bc(subln[:], 1, 4), ALU.mult)
            return ep0, ep1, ep2

        def mk_fox_ep(c, h, accv, par):
            mixtok = MIXTOK[c % 2]
            so = 40 + 4 * par

            def ep():
                P.recip(st1[:, so:so + 4], accv[:, :, 64])
                P.tt("dve", mixtok[:, :, 512 + h * 64:512 + (h + 1) * 64],
                     accv[:, :, 0:64], _bc(st1[:, so:so + 4], 2, 64), ALU.mult)
            return ep

        def mk_mix_tr(c):
            mixtok = MIXTOK[c % 2]

            def fn():
                for tl in range(4):
                    for kc in range(KC):
                        P.tr(pb7[:, kc * 128:(kc + 1) * 128], mixtok[:, tl, kc * 128:(kc + 1) * 128], ident[:])
                    tg = c * 4 + tl
                    P.copy("dve", mixedT[:, :, tg * 128:(tg + 1) * 128], pb7.rearrange("p (k t) -> p k t", t=128))
            return fn

        nfox = 0
        for c in range(4):
            for h in range(4):
                accs = [PS[:, m * 1024:(m + 1) * 1024].rearrange("p (q n) -> p q n", n=256) for m in range(2)]
                ep0, ep1, ep2 = mk_diff_ep(c, h, accs, 0)
                for m in range(2):
                    for kb in range(4 * c + 4):
                        steps.append(mk_diff_step(len(steps), c, h, m, kb, accs[m]))
                    if m == 0:
                        defer(len(steps) - 1, ep0)
                defer(len(steps) - 1, ep1)
                defer(len(steps) - 1, ep2)
            if c + 1 < 4:
                defer(len(steps) - 1, mk_qpad(c + 1, False))
            for h in range(8):
                accv = bank(nfox % 4).rearrange("p (q n) -> p q n", n=128)
                for kb in range(4 * c + 4):
                    steps.append(mk_fox_step(len(steps), c, h, kb, accv))
                defer(len(steps) - 1, mk_fox_ep(c, h, accv, nfox % 2))
                nfox += 1
            if c + 1 < 4:
                defer(len(steps) - 1, mk_qpad(c + 1, True))
            defer(len(steps) - 1 + 8, mk_mix_tr(c))

        LA = 2
        nst = len(steps)
        endi = max(nst, max(deferred) + 1)
        for i in range(endi + LA):
            if i < nst:
                steps[i][0]()
            j = i - LA
            if j >= 0:
                if j < nst:
                    steps[j][1]()
                for fn in deferred.get(j, []):
                    fn()

        o5 = 83712
        w_kv_sb = _view(R2, 50176, [128, KC, 2 * D], BF16)
        w_mq_sb = _view(R2, 32768, [128, KC, D], BF16)
        w_mo_sb = _view(R2, 0, [128, KC, D], BF16)
        for kc in range(KC):
            P.dma("pool", w_kv_sb[:, kc, :], w_mem_kv[kc * 128:(kc + 1) * 128, :])
        for kc in range(KC):
            P.dma("pool", w_mq_sb[:, kc, :], w_mem_q[kc * 128:(kc + 1) * 128, :])

        for t in range(NT):
            xt = XT4[t % 2]
            P.dma("sp", xt, x[t * 128:(t + 1) * 128, :])
            for hf in range(2):
                b = 2 * (t % 2) + hf
                for kc in range(KC):
                    P.mm(bank(b), mixedT[:, kc, t * 128:(t + 1) * 128], w_out_sb[:, kc, hf * 512:(hf + 1) * 512],
                         start=(kc == 0), stop=(kc == KC - 1))
                P.tt("dve", xres[:, t, hf * 512:(hf + 1) * 512], bank(b), xt[:, hf * 512:(hf + 1) * 512], ALU.add)

        for kc in range(KC):
            P.dma("pool", w_mo_sb[:, kc, :], w_mem_o[kc * 128:(kc + 1) * 128, :])
        hmT = _view(R2, o5, [128, KC, MEM], BF16)
        mkT = _view(R2, o5 + 4096, [128, 8, MEM], BF16)
        mv = _view(R2, o5 + 8192, [128, 2, 4, 258], BF16)
        o5b = o5 + 8192 + 4128
        MT = [_view(R2, o5b + i * 4096, [128, D], F32) for i in range(2)]
        SQ5 = _view(R2, o5b + 8192, [128, D], F32)
        PT5 = [_view(R2, o5b + 12288 + i * 1024, [128, 512], BF16) for i in range(2)]
        motok = _view(R2, o5b + 14336, [128, 4, D], BF16)
        XN5 = [_view(R2, o5b + 22528 + i * 2048, [128, D], BF16) for i in range(2)]
        MQN = [_view(R2, o5b + 26624 + i * 2048, [128, D], BF16) for i in range(2)]
        MQT = [_view(R2, 16384 + i * 8192, [128, 8, 512], BF16) for i in range(2)]
        moT = _view(R2, o5b, [128, KC, 512], BF16)
        assert o5b + 30720 <= 131072

        P.memset("pool", mv[:, :, :, 256:258], 1.0)

        def head_norm(src_banks, sq, dst_bf, stcol):
            for i, b in enumerate(src_banks):
                P.act(sq[:, i * 512:(i + 1) * 512], bank(b), AF.Square)
            P.reduce("dve", st2[:, stcol:stcol + 4], sq.rearrange("p (g d) -> p g d", d=256))
            rms_stats(st2[:, stcol:stcol + 4], 256.0, st2[:, stcol + 4:stcol + 8], 4)
            for i, b in enumerate(src_banks):
                P.tt("dve", dst_bf[:, i * 512:(i + 1) * 512].rearrange("p (g d) -> p g d", d=256),
                     bank(b).rearrange("p (g d) -> p g d", d=256),
                     _bc(st2[:, stcol + 4 + 2 * i:stcol + 6 + 2 * i], 2, 256), ALU.mult)

        for mt in range(2):
            mtile = MT[mt]
            P.dma("sp", mtile, mem[mt * 128:(mt + 1) * 128, :])
            norm_transpose(mtile, XN5[mt], wcol_memkv, hmT[:, :, mt * 128:(mt + 1) * 128], 6)
        for mt in range(2):
            for g in range(4):
                for kc in range(KC):
                    P.mm(bank(g), hmT[:, kc, mt * 128:(mt + 1) * 128], w_kv_sb[:, kc, g * 512:(g + 1) * 512],
                         start=(kc == 0), stop=(kc == KC - 1))
            head_norm((0, 1), SQ5, MQN[mt], 0)
            P.copy("act", mv[:, mt, 0:2, 0:256], bank(2).rearrange("p (h d) -> p h d", d=256))
            P.copy("act", mv[:, mt, 2:4, 0:256], bank(3).rearrange("p (h d) -> p h d", d=256))
            pb7 = bank(7, BF16)
            for j in range(8):
                P.tr(pb7[:, j * 128:(j + 1) * 128], MQN[mt][:, j * 128:(j + 1) * 128], ident[:])
            pv = pb7.rearrange("p (h f t) -> p h f t", f=2, t=128)
            dv_ = mkT[:, :, mt * 128:(mt + 1) * 128].rearrange("p (h f) t -> p h f t", f=2)
            for f in range(2):
                P.ts("dve", dv_[:, :, f, :], pv[:, :, f, :], wcol_mk[:, f:f + 1], ALU.mult)

        SQH = [SQ5[:, 0:512], SQ5[:, 512:1024]]
        XN5b = [_view(R2, 50176 + i * 2048, [128, D], BF16) for i in range(3)]
        HQTb = [_view(R2, 50176 + 6144 + i * 2048, [128, KC, 128], BF16) for i in range(3)]
        MQNb = [_view(R2, 50176 + 12288 + i * 2048, [128, D], BF16) for i in range(3)]
        QF = [_view(R2, 50176 + 18432 + i * 2048, [128, 512], F32) for i in range(3)]
        MOTOK = [motok, _view(R2, 50176 + 24576, [128, 4, D], BF16)]
        MOT = [moT, _view(R2, o5b + 22528, [128, KC, 512], BF16)]
        accv5 = PS[:, 4 * 512:6 * 512].rearrange("p (q n) -> p q n", n=256)
        def q_stage(c):
            mq = MQT[c % 2]
            for tl in range(4):
                t = c * 4 + tl
                xn, hq, mqn = XN5b[t % 3], HQTb[t % 3], MQNb[t % 3]
                norm_prep(xres[:, t, :], xn)
                norm_tr(xn, wcol_memq, hq, 6)
                for g in range(2):
                    qf = QF[(2 * t + g) % 3]
                    for kc in range(KC):
                        P.mm(bank(g), hq[:, kc, :], w_mq_sb[:, kc, g * 512:(g + 1) * 512],
                             start=(kc == 0), stop=(kc == KC - 1))
                    s0 = 16 * (t % 2) + 4 * g
                    P.memset("dve", st2[:, s0:s0 + 2], 0.0)
                    for hh in range(2):
                        P.act(SQH[g][:, hh * 256:(hh + 1) * 256], bank(g)[:, hh * 256:(hh + 1) * 256], AF.Square,
                              accum_out=st2[:, s0 + hh:s0 + hh + 1])
                    P.copy("act", qf, bank(g))
                    rms_stats(st2[:, s0:s0 + 2], 256.0, st2[:, s0 + 2:s0 + 4], 2)
                    P.tt("pool", mqn[:, g * 512:(g + 1) * 512].rearrange("p (g d) -> p g d", d=256),
                         qf.rearrange("p (g d) -> p g d", d=256),
                         _bc(st2[:, s0 + 2:s0 + 4], 2, 256), ALU.mult)
                pbq = bank(7, BF16)
                for j in range(8):
                    P.tr(pbq[:, j * 128:(j + 1) * 128], mqn[:, j * 128:(j + 1) * 128], ident[:])
                pv = pbq.rearrange("p (h f t) -> p h f t", f=2, t=128)
                dv_ = mq[:, :, tl * 128:(tl + 1) * 128].rearrange("p (h f) t -> p h f t", f=2)
                for f in range(2):
                    P.ts("dve", dv_[:, :, f, :], pv[:, :, f, :], wcol_mq[:, f:f + 1], ALU.mult)
        def heads_stage(c):
            mq = MQT[c % 2]
            motok = MOTOK[c % 2]
            moT = MOT[c % 2]
            for h in range(4):
                sumv = bank(2)[:, 8 * h:8 * h + 4]
                for mt in range(2):
                    pt = PT5[mt]
                    for f in range(2):
                        P.mm(bank(7), mkT[:, 2 * h + f, mt * 128:(mt + 1) * 128], mq[:, 2 * h + f, :],
                             start=(f == 0), stop=(f == 1))
                    P.act(pt[:], bank(7), AF.Exp, scale=1.0 / 16.0)
                    for tl in range(4):
                        P.mm(accv5[:, tl, :], pt[:, tl * 128:(tl + 1) * 128], mv[:, mt, h, 0:256],
                             start=(mt == 0 and tl in (0, 2)), stop=(mt == 1), skip=True)
                        P.mm(sumv[:, tl:tl + 1], pt[:, tl * 128:(tl + 1) * 128], mv[:, mt, h, 256:257],
                             start=(mt == 0 and tl == 0 and h == 0), stop=(mt == 1), skip=True)
                so = 32 + 4 * (h % 2)
                P.recip(st1[:, so:so + 4], sumv)
                P.tt("dve", motok[:, :, h * 256:(h + 1) * 256], accv5, _bc(st1[:, so:so + 4], 2, 256), ALU.mult)
            pb6 = bank(6, BF16)
            for tl in range(4):
                for kc in range(KC):
                    P.tr(pb6[:, kc * 128:(kc + 1) * 128], motok[:, tl, kc * 128:(kc + 1) * 128], ident[:])
                P.copy("act", moT[:, :, tl * 128:(tl + 1) * 128], pb6.rearrange("p (k t) -> p k t", t=128))
            for tl in range(4):
                t = c * 4 + tl
                for hf in range(2):
                    for kc in range(KC):
                        P.mm(bank(3), moT[:, kc, tl * 128:(tl + 1) * 128], w_mo_sb[:, kc, hf * 512:(hf + 1) * 512],
                             start=(kc == 0), stop=(kc == KC - 1))
                    P.tt("dve", xres[:, t, hf * 512:(hf + 1) * 512], bank(3),
                         xres[:, t, hf * 512:(hf + 1) * 512], ALU.add)

        q_stage(0)
        for c in range(4):
            if c + 1 < 4:
                q_stage(c + 1)
            heads_stage(c)

        hT = _view(R2, 0, [128, KC, S], BF16)
        WU = [_view(R2, 32768 + i * 16384, [128, KC, 1024], BF16) for i in range(2)]
        WD = [_view(R2, 65536 + i * 16384, [128, 8, 1024], BF16) for i in range(2)]
        AT = [_view(R2, 98304 + i * 8192, [128, 8, 512], BF16) for i in range(2)]
        OUTS = [_view(R2, 114688 + i * 4096, [128, D], F32) for i in range(2)]
        RL = [_view(R2, 122880 + i * 2048, [128, 512], F32) for i in range(2)]
        XN6 = [_view(R2, 126976 + i * 2048, [128, D], BF16) for i in range(2)]

        def load_mlp_w(qf):
            for kc in range(KC):
                P.dma("pool", WU[qf % 2][:, kc, :], w_up[kc * 128:(kc + 1) * 128, qf * 1024:(qf + 1) * 1024])
            for fc in range(8):
                r0 = qf * 1024 + fc * 128
                P.dma("pool", WD[qf % 2][:, fc, :], w_down[r0:r0 + 128, :])

        load_mlp_w(0)
        for t in range(NT):
            norm_transpose(xres[:, t, :], XN6[t % 2], wcol_mlp, hT[:, :, t * 128:(t + 1) * 128], 6 + t % 2)
        load_mlp_w(1)
        cnt = 0
        for qf in range(4):
            wu, wd = WU[qf % 2], WD[qf % 2]
            if qf in (1, 2):
                pass
            for c in range(4):
                at = AT[c % 2]
                for fcl in range(8):
                    b = 4 + cnt % 2
                    rl = RL[cnt % 2]
                    cnt += 1
                    for kc in range(KC):
                        P.mm(bank(b), wu[:, kc, fcl * 128:(fcl + 1) * 128], hT[:, kc, c * 512:(c + 1) * 512],
                             start=(kc == 0), stop=(kc == KC - 1))
                    P.act(rl[:], bank(b), AF.Relu)
                    P.tt("dve", at[:, fcl, :], rl[:], rl[:], ALU.mult)
                for tl in range(4):
                    t = c * 4 + tl
                    for hf in range(2):
                        b = 2 * (tl % 2) + hf
                        for fcl in range(8):
                            P.mm(bank(b), at[:, fcl, tl * 128:(tl + 1) * 128], wd[:, fcl, hf * 512:(hf + 1) * 512],
                                 start=(fcl == 0), stop=(fcl == 7))
                        if qf < 3:
                            P.tt("dve", xres[:, t, hf * 512:(hf + 1) * 512], bank(b),
                                 xres[:, t, hf * 512:(hf + 1) * 512], ALU.add)
                        else:
                            P.tt("dve", OUTS[t % 2][:, hf * 512:(hf + 1) * 512], bank(b),
                                 xres[:, t, hf * 512:(hf + 1) * 512], ALU.add)
                    if qf == 3:
                        P.dma("sp", y[t * 128:(t + 1) * 128, :], OUTS[t % 2], is_out=True)
            if qf + 2 < 4:
                load_mlp_w(qf + 2)

        with nc.Block() as block:
            P.emit(block, sem_ctx)
    return nc


_NC_CACHE = {}


def kernel(**inputs):
    if "nc" not in _NC_CACHE:
        _NC_CACHE["nc"] = build_program()
    nc = _NC_CACHE["nc"]
    f32 = lambda a: np.ascontiguousarray(np.asarray(a, dtype=np.float32))
    shared = {}
    for name in ("norm_mix_w", "w_in", "b_forget", "diff_q_norm_w", "diff_k_norm_w", "lambda_q1", "lambda_k1",
                 "lambda_q2", "lambda_k2", "diff_subln_w", "fox_q_norm_w", "fox_k_norm_w", "w_out",
                 "norm_mem_q_w", "norm_mem_kv_w", "w_mem_q", "w_mem_kv", "mem_q_norm_w", "mem_k_norm_w",
                 "w_mem_o", "norm_mlp_w", "w_up", "w_down"):
        a = f32(inputs[name])
        if name in ("w_in", "w_out", "w_mem_q", "w_mem_kv", "w_mem_o", "w_up", "w_down"):
            shared[name] = np.ascontiguousarray(a[0])
        else:
            shared[name] = np.ascontiguousarray(a.reshape(1, -1))
    x = f32(inputs["x"])
    mem = f32(inputs["mem"])
    pos = np.ascontiguousarray(np.asarray(inputs["positions"], dtype=np.int32))
    in_maps = []
    for b in range(N_CORES):
        m = dict(shared)
        m["x"] = np.ascontiguousarray(x[b])
        m["mem"] = np.ascontiguousarray(mem[b])
        m["positions"] = np.ascontiguousarray(pos[b].reshape(1, S))
        in_maps.append(m)
    res = run_bass_kernel_spmd(nc, in_maps, core_ids=list(range(N_CORES)))
    out = np.stack([np.asarray(r["y"], dtype=np.float32) for r in res.results], axis=0)
    return out
```

```python
import math
import numpy as np
import concourse.bass as bass
import concourse.mybir as mybir
from concourse.bass_utils import run_bass_kernel_spmd

F32 = mybir.dt.float32
BF16 = mybir.dt.bfloat16
I32 = mybir.dt.int32
U8 = mybir.dt.uint8
AF = mybir.ActivationFunctionType
ALU = mybir.AluOpType
AX = mybir.AxisListType

S = 2048
D = 1024
NT = 16
KC = 8
MEM = 256
IN_COLS = 3080
EPS = 1e-6
LAM_INIT = 0.8 - 0.6 * math.exp(0.0)
N_CORES = 8


class Op:
    __slots__ = ("eng", "fn", "pos", "dma", "sig", "sem", "val", "vc", "waits", "id", "preds", "cost",
                 "nbytes", "is_out", "prio", "start", "fin", "succs", "npred", "opreds", "rdy")


class Prog:
    ENGS = ("pe", "act", "dve", "pool", "sp")
    SEM_LIMIT = 30000
    SCHEDULE = True

    def __init__(self, nc):
        self.nc = nc
        self.allops = []
        self.ops = {e: [] for e in self.ENGS}
        self.acc = {}
        self.dma_sems = {}
        self.out_dmas = []

    @staticmethod
    def _region(ap):
        sp = str(ap.space)
        if sp not in ("SB", "PSUM"):
            return None
        aps = ap.ap
        esz = mybir.dt.size(ap.dtype)
        pstride, npart = aps[0]
        off = ap.offset
        if pstride == 0:
            p0, f0 = 0, off
        else:
            p0, f0 = off // pstride, off % pstride
        dims = sorted([(abs(s_), c) for s_, c in aps[1:] if c > 1])
        ivs = [(0, 1)]
        for s_, c in dims:
            if s_ == 0:
                continue
            span = ivs[-1][1] - ivs[0][0]
            if s_ <= span or len(ivs) * c > 64:
                ivs = [(ivs[0][0], ivs[-1][1] + (c - 1) * s_)]
            else:
                ivs = [(lo + i * s_, hi + i * s_) for i in range(c) for lo, hi in ivs]
                ivs.sort()
        out = []
        for lo, hi in ivs:
            lo, hi = (f0 + lo) * esz, (f0 + hi) * esz
            if sp == "PSUM":
                lo = (lo // 2048) * 2048
                hi = ((hi + 2047) // 2048) * 2048
            if out and lo <= out[-1][1]:
                if hi > out[-1][1]:
                    out[-1] = (out[-1][0], hi)
            else:
                out.append((lo, hi))
        if sp == "PSUM":
            return (ap.tensor.name, True, 0, 128, tuple(out))
        return (ap.tensor.name, False, p0, p0 + npart, tuple(out))

    @staticmethod
    def _ov(a, b):
        if a[0][0] >= b[-1][1] or b[0][0] >= a[-1][1]:
            return False
        for lo, hi in a:
            for lo2, hi2 in b:
                if lo < hi2 and lo2 < hi:
                    return True
        return False

    @staticmethod
    def _cov(new, old):
        for lo, hi in old:
            ok = False
            for lo2, hi2 in new:
                if lo2 <= lo and hi <= hi2:
                    ok = True
                    break
            if not ok:
                return False
        return True

    def _access(self, op, ap, is_write, deps):
        r = self._region(ap)
        if r is None:
            return
        name, psum, p0, p1, ivs = r
        lst = self.acc.get(name)
        if lst is None:
            lst = []
            self.acc[name] = lst
        conflict_w = is_write or psum
        new = []
        mine = []
        for e in lst:
            eivs, ep0, ep1, eop, ew, ecw, more = e
            if eop is op:
                new.append(e)
                continue
            pov = ep0 < p1 and p0 < ep1
            ov = pov and self._ov(ivs, eivs)
            same = (eop.eng == op.eng) and not eop.dma and not op.dma
            if ov:
                if same:
                    if ew or is_write:
                        deps.append(eop)
                        deps.extend(more)
                elif conflict_w or ecw:
                    deps.append(eop)
                    deps.extend(more)
            cover = ov and p0 <= ep0 and ep1 <= p1 and self._cov(ivs, eivs)
            if cover and is_write:
                continue
            if cover and same and psum:
                op.opreds.append(eop)
                continue
            if cover and same and (not ew) and (not is_write):
                mine.append(eop)
                mine.extend(more)
                continue
            new.append(e)
        new.append((ivs, p0, p1, op, is_write, conflict_w, mine))
        self.acc[name] = new

    def add(self, eng, fn, reads, writes, dma=False, is_out=False, cost=100.0, nbytes=0):
        op = Op()
        op.eng, op.fn, op.dma, op.sig = eng, fn, dma, False
        op.id = len(self.allops)
        op.sem = None
        op.val = 0
        op.cost = cost
        op.nbytes = nbytes
        op.is_out = is_out
        op.opreds = []
        deps = []
        for ap in reads:
            if ap is not None and not isinstance(ap, (int, float)):
                self._access(op, ap, False, deps)
        for ap in writes:
            if ap is not None:
                self._access(op, ap, True, deps)
        seen = set()
        preds = []
        for a in deps:
            if a.id not in seen:
                seen.add(a.id)
                preds.append(a)
        op.preds = preds
        self.allops.append(op)
        return op

    def _schedule(self):
        import heapq
        ops = self.allops
        for op in ops:
            op.succs = []
        for op in ops:
            allp = {a.id: a for a in op.preds}
            for a in op.opreds:
                allp[a.id] = a
            op.npred = len(allp)
            for a in allp.values():
                a.succs.append(op)
        for op in reversed(ops):
            m = 0.0
            for s_ in op.succs:
                if s_.prio > m:
                    m = s_.prio
            lat = op.cost + (2000.0 + op.nbytes / 360.0 if op.dma else 0.0)
            op.prio = m + lat
        LAT = 250.0
        ready = {e: [] for e in self.ENGS}
        for op in ops:
            if op.npred == 0:
                heapq.heappush(ready[op.eng], (-op.prio, op.id, op))
        free = {e: 0.0 for e in self.ENGS}
        events = []
        pending = []
        for op in ops:
            op.rdy = 0.0
        order = {e: [] for e in self.ENGS}
        pipe_free = 0.0
        t = 0.0
        ndone = 0
        n = len(ops)
        while ndone < n:
            progressed = False
            for e in self.ENGS:
                if free[e] <= t and ready[e]:
                    _, _, op = heapq.heappop(ready[e])
                    op.start = t
                    if op.dma:
                        free[e] = t + op.cost
                        xs = max(t + op.cost, pipe_free)
                        pipe_free = xs + op.nbytes / 360.0
                        op.fin = pipe_free + 2000.0
                    else:
                        free[e] = t + op.cost
                        op.fin = free[e]
                    heapq.heappush(events, (op.fin, op.id, op))
                    order[e].append(op)
                    progressed = True
            cand = []
            if events:
                cand.append(events[0][0])
            if pending:
                cand.append(pending[0][0])
            for e in self.ENGS:
                if ready[e] and free[e] > t:
                    cand.append(free[e])
            if not progressed and not cand:
                raise RuntimeError("scheduler deadlock")
            if cand:
                nt = min(cand)
                if nt > t:
                    t = nt
            while events and events[0][0] <= t:
                _, _, op = heapq.heappop(events)
                ndone += 1
                for s_ in op.succs:
                    s_.npred -= 1
                    rt = op.fin if (s_.eng == op.eng and not op.dma) else op.fin + LAT
                    if rt > s_.rdy:
                        s_.rdy = rt
                    if s_.npred == 0:
                        if s_.rdy <= t:
                            heapq.heappush(ready[s_.eng], (-s_.prio, s_.id, s_))
                        else:
                            heapq.heappush(pending, (s_.rdy, s_.id, s_))
            while pending and pending[0][0] <= t:
                _, _, s_ = heapq.heappop(pending)
                heapq.heappush(ready[s_.eng], (-s_.prio, s_.id, s_))
        self.est_ns = t
        return order

    def _finalize(self):
        if self.SCHEDULE:
            order = self._schedule()
            glob = sorted(self.allops, key=lambda o: (o.start, o.id))
        else:
            order = {e: [o for o in self.allops if o.eng == e] for e in self.ENGS}
            glob = list(self.allops)
        self.ops = order
        for e in self.ENGS:
            for i, op in enumerate(order[e]):
                op.pos = i
        dma_last = {}
        dma_cnt = {}
        extra = {}
        for e in self.ENGS:
            pool = self.dma_sems.get(e)
            i = 0
            for op in order[e]:
                if not op.dma:
                    continue
                sem = pool[i % len(pool)]
                i += 1
                prev = dma_last.get(sem)
                if prev is not None:
                    extra[op.id] = prev
                dma_last[sem] = op
                dma_cnt[sem] = dma_cnt.get(sem, 0) + 16
                op.sem, op.val, op.sig = sem, dma_cnt[sem], True
                if op.is_out:
                    self.out_dmas.append(op)
        known = {e: {} for e in self.ENGS}
        for op in glob:
            kn = known[op.eng]
            deps = list(op.preds)
            if op.id in extra:
                deps.append(extra[op.id])
            deps.sort(key=lambda a: -a.pos)
            waits = []
            for a in deps:
                same = (a.eng == op.eng) and not a.dma and not op.dma
                if same and op.eng == "pe":
                    continue
                key = ("d", a.id) if a.dma else a.eng
                need = 1 if a.dma else a.pos
                if kn.get(key, -1) >= need:
                    continue
                waits.append(a)
                a.sig = True
                for k, v in a.vc.items():
                    if kn.get(k, -1) < v:
                        kn[k] = v
            op.waits = waits
            vc = dict(kn)
            if op.dma:
                vc[("d", op.id)] = 1
            else:
                vc[op.eng] = op.pos
            op.vc = vc

    def emit(self, block, sems_for_engine):
        self._finalize()
        for e in self.ENGS:
            pool = sems_for_engine[e]
            cnt, ep = 0, 0
            for op in self.ops[e]:
                if op.dma or not op.sig:
                    continue
                cnt += 1
                op.sem, op.val = pool[ep], cnt
                if cnt >= self.SEM_LIMIT:
                    cnt, ep = 0, ep + 1

        def run(engname, eng):
            for op in self.ops[engname]:
                best = {}
                for a in op.waits:
                    if best.get(a.sem, 0) < a.val:
                        best[a.sem] = a.val
                for sem, val in best.items():
                    eng.wait_ge(sem, val)
                ins = op.fn(eng)
                if op.sig:
                    ins.then_inc(op.sem, 16 if op.dma else 1)
            if engname == "sp":
                best = {}
                for a in self.out_dmas:
                    if best.get(a.sem, 0) < a.val:
                        best[a.sem] = a.val
                for sem, val in best.items():
                    eng.wait_ge(sem, val)

        @block.tensor
        def _(t):
            run("pe", t)

        @block.scalar
        def _(s):
            run("act", s)

        @block.vector
        def _(v):
            run("dve", v)

        @block.gpsimd
        def _(g):
            run("pool", g)

        @block.sync
        def _(sy):
            run("sp", sy)

    @staticmethod
    def _fs(ap):
        n = 1
        for s_ in ap.shape[1:]:
            n *= s_
        return n

    @staticmethod
    def _is_psum(ap):
        return str(ap.space) == "PSUM"

    def _vcost(self, eng, out, ins):
        n = self._fs(out)
        ps = any(self._is_psum(a) for a in ins if a is not None and not isinstance(a, (int, float))) or self._is_psum(out)
        if eng == "pool":
            return 150.0 + n * 1.9
        return (125.0 if ps else 65.0) + n * 1.04

    def mm(self, out, lhsT, rhs, start=True, stop=True, skip=False):
        n = self._fs(rhs)
        mult = 4.0 if rhs.dtype == F32 else 1.0
        return self.add("pe", lambda e: e.matmul(out, lhsT, rhs, start=start, stop=stop,
                                                 skip_group_check=skip), [lhsT, rhs], [out],
                        cost=mult * max(64, n) / 2.4 + 8.0)

    def tr(self, out, in_, ident):
        return self.add("pe", lambda e: e.transpose(out, in_, ident), [in_, ident], [out], cost=75.0)

    def act(self, out, in_, func, bias=0.0, scale=1.0, accum_out=None):
        rd = [in_]
        if not isinstance(bias, (int, float)):
            rd.append(bias)
        if not isinstance(scale, (int, float)):
            rd.append(scale)
        kw = {}
        if accum_out is not None:
            kw["accum_out"] = accum_out
        return self.add("act", lambda e: e.activation(out=out, in_=in_, func=func, bias=bias,
                                                      scale=scale, **kw), rd, [out, accum_out],
                        cost=200.0 + 0.8 * self._fs(in_))

    def tt(self, eng, out, in0, in1, op):
        return self.add(eng, lambda e: e.tensor_tensor(out=out, in0=in0, in1=in1, op=op),
                        [in0, in1], [out], cost=self._vcost(eng, out, [in0, in1]))

    def ts(self, eng, out, in0, s1, op0, s2=None, op1=None):
        rd = [in0]
        if not isinstance(s1, (int, float)):
            rd.append(s1)
        if s2 is not None and not isinstance(s2, (int, float)):
            rd.append(s2)
        c = self._vcost(eng, out, [in0])
        if op1 is None:
            return self.add(eng, lambda e: e.tensor_scalar(out=out, in0=in0, scalar1=s1, scalar2=None,
                                                           op0=op0), rd, [out], cost=c)
        return self.add(eng, lambda e: e.tensor_scalar(out=out, in0=in0, scalar1=s1, scalar2=s2,
                                                       op0=op0, op1=op1), rd, [out], cost=c)

    def stt(self, eng, out, in0, scalar, in1, op0, op1):
        rd = [in0, in1]
        if not isinstance(scalar, (int, float)):
            rd.append(scalar)
        return self.add(eng, lambda e: e.scalar_tensor_tensor(out=out, in0=in0, scalar=scalar, in1=in1,
                                                              op0=op0, op1=op1), rd, [out],
                        cost=self._vcost(eng, out, [in0, in1]))

    def copy(self, eng, out, in_):
        if eng == "act":
            return self.add("act", lambda e: e.activation(out=out, in_=in_, func=AF.Copy), [in_], [out],
                            cost=200.0 + 0.8 * self._fs(in_))
        return self.add(eng, lambda e: e.tensor_copy(out=out, in_=in_), [in_], [out],
                        cost=self._vcost(eng, out, [in_]))

    def memset(self, eng, ap, val):
        return self.add(eng, lambda e: e.memset(ap, val), [], [ap], cost=60.0 + 0.3 * self._fs(ap))

    def reduce(self, eng, out, in_, op=ALU.add):
        return self.add(eng, lambda e: e.tensor_reduce(out=out, in_=in_, axis=AX.X, op=op), [in_], [out],
                        cost=65.0 + 1.04 * self._fs(in_))

    def recip(self, out, in_):
        return self.add("dve", lambda e: e.reciprocal(out=out, in_=in_), [in_], [out],
                        cost=self._vcost("dve", out, [in_]))

    def affsel(self, out, in_, pattern, cmp, fill, base, cm):
        return self.add("pool", lambda e: e.affine_select(out=out, in_=in_, pattern=pattern, compare_op=cmp,
                                                          fill=fill, base=base, channel_multiplier=cm),
                        [in_], [out], cost=150.0 + 1.0 * self._fs(out))

    def dma(self, q, out, in_, is_out=False, slow=False):
        if slow:
            fn = lambda e: e.dma_start(out=out, in_=in_, allow_slow_non_contiguous=True)
        else:
            fn = lambda e: e.dma_start(out=out, in_=in_)
        nb = self._fs(out) * mybir.dt.size(out.dtype) * out.shape[0]
        return self.add(q, fn, [in_], [out], dma=True, is_out=is_out,
                        cost=(1000.0 if q == "pool" else 60.0), nbytes=nb)


def _view(base, off, shape, dt):
    esz = mybir.dt.size(dt)
    n = 1
    for s in shape[1:]:
        n *= s
    v = base[0:shape[0], off:off + n * esz].bitcast(dt)
    if len(shape) == 2:
        return v
    names = [f"d{i}" for i in range(1, len(shape))]
    pat = "p (" + " ".join(names) + ") -> p " + " ".join(names)
    kw = {names[i]: shape[i + 1] for i in range(len(names) - 1)}
    return v.rearrange(pat, **kw)


def _bc(ap, axis, n):
    shp = list(ap.shape)
    shp.insert(axis, n)
    return ap.unsqueeze(axis).broadcast_to(shp)


def build_program():
    nc = bass.Bass("TRN2", target_bir_lowering=False)

    def din(name, shape, dt=F32):
        return nc.dram_tensor(name, list(shape), dt, kind="ExternalInput").ap()

    x = din("x", [S, D])
    mem = din("mem", [MEM, D])
    positions = din("positions", [1, S], I32)
    norm_mix_w = din("norm_mix_w", [1, D])
    w_in = din("w_in", [D, IN_COLS])
    b_forget = din("b_forget", [1, 8])
    diff_q_norm_w = din("diff_q_norm_w", [1, 64])
    diff_k_norm_w = din("diff_k_norm_w", [1, 64])
    lambda_q1 = din("lambda_q1", [1, 64])
    lambda_k1 = din("lambda_k1", [1, 64])
    lambda_q2 = din("lambda_q2", [1, 64])
    lambda_k2 = din("lambda_k2", [1, 64])
    diff_subln_w = din("diff_subln_w", [1, 128])
    fox_q_norm_w = din("fox_q_norm_w", [1, 64])
    fox_k_norm_w = din("fox_k_norm_w", [1, 64])
    w_out = din("w_out", [D, D])
    norm_mem_q_w = din("norm_mem_q_w", [1, D])
    norm_mem_kv_w = din("norm_mem_kv_w", [1, D])
    w_mem_q = din("w_mem_q", [D, D])
    w_mem_kv = din("w_mem_kv", [D, 2 * D])
    mem_q_norm_w = din("mem_q_norm_w", [1, 256])
    mem_k_norm_w = din("mem_k_norm_w", [1, 256])
    w_mem_o = din("w_mem_o", [D, D])
    norm_mlp_w = din("norm_mlp_w", [1, D])
    w_up = din("w_up", [D, 4 * D])
    w_down = din("w_down", [4 * D, D])
    y = nc.dram_tensor("y", [S, D], F32, kind="ExternalOutput").ap()

    from contextlib import ExitStack
    with ExitStack() as es:
        def sb(name, shape, dt):
            return es.enter_context(nc.sbuf_tensor(name, list(shape), dt))

        RX = sb("RX", [128, 65536], U8)
        R2 = sb("R2", [128, 131072], U8)
        PS = es.enter_context(nc.psum_tensor("PS", [128, 4096], F32))
        ident = sb("ident", [128, 128], BF16)
        tri = sb("tri", [128, 128], F32)
        ones = sb("ones", [128, 128], F32)
        wcol_mix = sb("wcol_mix", [128, 8], F32)
        wcol_memq = sb("wcol_memq", [128, 8], F32)
        wcol_memkv = sb("wcol_memkv", [128, 8], F32)
        wcol_mlp = sb("wcol_mlp", [128, 8], F32)
        wq_adj = sb("wq_adj", [128, 1], F32)
        wk_adj = sb("wk_adj", [128, 1], F32)
        wfq = sb("wfq", [128, 1], F32)
        wfk = sb("wfk", [128, 1], F32)
        w16q = sb("w16q", [128, 16], F32)
        w16k = sb("w16k", [128, 16], F32)
        subln = sb("subln", [128, 128], F32)
        wcol_mq = sb("wcol_mq", [128, 2], F32)
        wcol_mk = sb("wcol_mk", [128, 2], F32)
        bfg = sb("bfg", [128, 8], F32)
        lamv = sb("lamv", [128, 4, 64], F32)
        lamt = sb("lamt", [128, 8], F32)
        posi = sb("posi", [128, 16], I32)
        posf = sb("posf", [128, 16], F32)
        invf = sb("invf", [128, 8], F32)
        ang = sb("ang", [128, 16, 8], F32)
        angk = sb("angk", [128, 16, 8], F32)
        angi = sb("angi", [128, 16, 8], I32)
        angr = sb("angr", [128, 16, 8], F32)
        angc = sb("angc", [128, 16, 8], F32)
        cosT = sb("cosT", [128, 16, 8], F32)
        sinT = sb("sinT", [128, 16, 8], F32)
        st1 = sb("st1", [128, 64], F32)
        st2 = sb("st2", [128, 64], F32)
        st3 = sb("st3", [128, 64], F32)
        nls = sb("nls", [128, 16, 8], F32)
        Tb = sb("Tb", [128, 16, 8], F32)
        Pinc = sb("Pinc", [128, 16, 8], F32)
        gcol = sb("gcol", [128, 16, 8], F32)
        gtmp = sb("gtmp", [128, 16, 8], F32)
        biasT = sb("biasT", [128, 16, 8, 8], F32)
        zt = sb("zt", [128, 8], F32)
        et = sb("et", [128, 8], F32)

        n_eng_sems = 1
        sem_ctx = {}
        for e in Prog.ENGS:
            sem_ctx[e] = [es.enter_context(nc.semaphore(f"s_{e}{i}")) for i in range(n_eng_sems)]
        P = Prog(nc)
        P.dma_sems["sp"] = [es.enter_context(nc.semaphore(f"d_sp{i}")) for i in range(12)]
        P.dma_sems["pool"] = [es.enter_context(nc.semaphore(f"d_pool{i}")) for i in range(12)]

        def bank(b, dt=F32):
            v = PS[:, b * 512:(b + 1) * 512]
            return v if dt == F32 else v.bitcast(dt)

        P.memset("pool", ident[:], 1.0)
        P.affsel(ident[:], ident[:], [[-1, 128]], ALU.is_equal, 0.0, 0, 1)
        P.memset("pool", tri[:], 1.0)
        P.affsel(tri[:], tri[:], [[1, 128]], ALU.is_ge, 0.0, 0, -1)
        P.memset("pool", ones[:], 1.0)

        identf = sb("identf", [128, 128], F32)
        P.memset("pool", identf[:], 1.0)
        P.affsel(identf[:], identf[:], [[-1, 128]], ALU.is_equal, 0.0, 0, 1)
        w8 = _view(RX, 0, [8, 6, 128], F32)
        w128 = _view(RX, 3072, [1, 4, 128], F32)
        posi16 = _view(RX, 5120, [16, 128], I32)
        posf16 = _view(RX, 5632, [16, 128], F32)
        pcol = [0]

        def col_load(dst, src, nk, slot):
            P.dma("sp", w8[0:nk, slot, :], src.rearrange("o (k q) -> (o k) q", q=128))
            c0 = pcol[0]
            pcol[0] += nk
            P.mm(bank(7)[:, c0:c0 + nk], w8[0:nk, slot, :], identf[0:nk, 0:nk], True, True)
            P.copy("dve", dst[:], bank(7)[:, c0:c0 + nk])

        def col64(dst, src, slot):
            P.dma("sp", w128[0:1, slot, 0:64], src)
            P.dma("sp", w128[0:1, slot, 64:128], src)
            c0 = pcol[0]
            pcol[0] += 1
            P.mm(bank(7)[:, c0:c0 + 1], w128[0:1, slot, :], ones[0:1, 0:1], True, True)
            P.copy("dve", dst[:, 0:1], bank(7)[:, c0:c0 + 1])

        col_load(wcol_mix, norm_mix_w, 8, 0)
        col_load(wcol_memq, norm_mem_q_w, 8, 1)
        col_load(wcol_memkv, norm_mem_kv_w, 8, 2)
        col_load(wcol_mlp, norm_mlp_w, 8, 3)
        col_load(wcol_mq, mem_q_norm_w, 2, 4)
        col_load(wcol_mk, mem_k_norm_w, 2, 5)
        for i, (dst, src) in enumerate(((wq_adj, diff_q_norm_w), (wk_adj, diff_k_norm_w), (wfq, fox_q_norm_w),
                                        (wfk, fox_k_norm_w))):
            col64(dst, src, i)
        for dst in (wq_adj, wk_adj):
            P.memset("pool", dst[0:16, :], 1.0)
            P.memset("pool", dst[64:80, :], 1.0)
        P.dma("sp", w16q[:], diff_q_norm_w[:, 0:16].partition_broadcast(128))
        P.dma("sp", w16k[:], diff_k_norm_w[:, 0:16].partition_broadcast(128))
        P.dma("sp", subln[:], diff_subln_w.partition_broadcast(128))
        P.dma("sp", bfg[:], b_forget.partition_broadcast(128))
        for i, src in enumerate((lambda_q1, lambda_k1, lambda_q2, lambda_k2)):
            P.dma("sp", lamv[:, i, :], src.partition_broadcast(128))
        P.dma("sp", posi16, positions.rearrange("o (t q) -> (o t) q", q=128))
        P.copy("dve", posf16, posi16)
        P.mm(bank(7)[:, 64:80], posf16, identf[0:16, 0:16], True, True)

        P.tt("dve", lamv[:, 0, :], lamv[:, 0, :], lamv[:, 1, :], ALU.mult)
        P.tt("dve", lamv[:, 2, :], lamv[:, 2, :], lamv[:, 3, :], ALU.mult)
        P.reduce("dve", lamt[:, 0:1], lamv[:, 0, :])
        P.reduce("dve", lamt[:, 1:2], lamv[:, 2, :])
        P.act(lamt[:, 2:4], lamt[:, 0:2], AF.Exp)
        P.tt("dve", lamt[:, 4:5], lamt[:, 3:4], lamt[:, 2:3], ALU.subtract)
        P.ts("dve", lamt[:, 4:5], lamt[:, 4:5], -LAM_INIT, ALU.add)
        P.ts("dve", subln[:], subln[:], 1.0 - LAM_INIT, ALU.mult)

        inv64 = 500000.0 ** (-np.arange(0, 16, 2, dtype=np.float64) / 16.0)
        inv_hi = inv64.astype(np.float32)
        inv_lo = (inv64 - inv_hi.astype(np.float64)).astype(np.float32)
        invf_lo = zt
        ang2 = gtmp
        for j in range(8):
            P.memset("pool", invf[:, j:j + 1], float(inv_hi[j]))
            P.memset("pool", invf_lo[:, j:j + 1], float(inv_lo[j]))
        P.copy("dve", posf[:], bank(7)[:, 64:80])
        P.tt("dve", ang[:], _bc(posf[:], 2, 8), _bc(invf[:], 1, 16), ALU.mult)
        P.tt("dve", ang2[:], _bc(posf[:], 2, 8), _bc(invf_lo[:], 1, 16), ALU.mult)
        C1 = 6.28125
        C2 = float(np.float32(2 * math.pi - C1))
        C3 = float(np.float32(2 * math.pi - C1 - C2))
        PI_SAFE = 3.1415925
        P.tt("dve", angk[:], ang[:], ang2[:], ALU.add)
        P.ts("dve", angk[:], angk[:], float(np.float32(1.0 / (2 * math.pi))), ALU.mult)
        P.copy("dve", angi[:], angk[:])
        P.copy("dve", angk[:], angi[:])
        P.stt("dve", angr[:], angk[:], -C1, ang[:], ALU.mult, ALU.add)
        P.tt("dve", angr[:], angr[:], ang2[:], ALU.add)
        P.stt("dve", angr[:], angk[:], -C2, angr[:], ALU.mult, ALU.add)
        P.stt("dve", angr[:], angk[:], -C3, angr[:], ALU.mult, ALU.add)
        P.ts("dve", angc[:], angr[:], math.pi / 2, ALU.is_gt)
        P.ts("dve", angk[:], angr[:], math.pi / 2, ALU.add)
        P.stt("dve", angc[:], angc[:], -2 * math.pi, angk[:], ALU.mult, ALU.add)
        P.ts("dve", angr[:], angr[:], -PI_SAFE, ALU.max, PI_SAFE, ALU.min)
        P.ts("dve", angc[:], angc[:], -PI_SAFE, ALU.max, PI_SAFE, ALU.min)
        P.act(sinT[:], angr[:], AF.Sin)
        P.act(cosT[:], angc[:], AF.Sin)

        QTd = _view(RX, 0, [128, 4, S], BF16)
        KTd = _view(RX, 16384, [128, 4, S], BF16)
        QTf = _view(RX, 32768, [128, 4, S], BF16)
        KTf = _view(RX, 49152, [128, 4, S], BF16)
        xres = _view(RX, 0, [128, NT, D], F32)

        w_in_sb = _view(R2, 0, [128, KC, IN_COLS], BF16)
        Vd = _view(R2, 50176, [128, NT, 4, 130], BF16)
        Vf = _view(R2, 66816, [128, NT, 8, 66], BF16)
        o1 = 83712
        XT = [_view(R2, o1 + i * 4096, [128, D], F32) for i in range(2)]
        XN = [_view(R2, o1 + 8192 + i * 2048, [128, D], BF16) for i in range(2)]
        HTt = [_view(R2, o1 + 12288 + i * 2048, [128, KC, 128], BF16) for i in range(2)]
        SQ = [_view(R2, o1 + 16384 + i * 2048, [128, 512], F32) for i in range(2)]
        QN = [_view(R2, o1 + 20480 + i * 4096, [128, 2048], BF16) for i in range(2)]
        ropeA = _view(R2, o1 + 28672, [128, 16, 16], F32)
        ropeT = [_view(R2, o1 + 29696 + i * 512, [128, 16, 8], F32) for i in range(4)]

        prev_grp = []
        for (c0, c1) in ((0, 1024), (1536, 2560), (1024, 1536), (2560, IN_COLS)):
            grp = []
            for kc in range(KC):
                o = P.dma("pool", w_in_sb[:, kc, c0:c1], w_in[kc * 128:(kc + 1) * 128, c0:c1])
                o.preds.extend(prev_grp)
                grp.append(o)
            prev_grp = grp
        P.memset("pool", Vd[:, :, :, 128:130], 1.0)
        P.memset("pool", Vf[:, :, :, 64:66], 1.0)

        rs_rr = [0]

        def rms_stats(src_ss, n, dst_rs, ncol):
            o = 16 * (rs_rr[0] % 4)
            rs_rr[0] += 1
            P.act(st3[:, o:o + ncol], src_ss, AF.Ln, bias=EPS, scale=1.0 / n)
            P.act(dst_rs, st3[:, o:o + ncol], AF.Exp, scale=-0.5)

        nrm_rr = [0]

        def norm_prep(src, xn):
            k = 4 * (nrm_rr[0] % 2)
            nrm_rr[0] += 1
            P.memset("dve", st1[:, k:k + 1], 0.0)
            P.act(xn, src, AF.Square, accum_out=st1[:, k:k + 1])
            P.act(st1[:, k + 2:k + 3], st1[:, k:k + 1], AF.Ln, bias=EPS, scale=1.0 / D)
            P.act(st1[:, k + 1:k + 2], st1[:, k + 2:k + 3], AF.Exp, scale=-0.5)
            P.ts("dve", xn, src, st1[:, k + 1:k + 2], ALU.mult)

        def norm_tr(xn, wcol, dst, pbank):
            pb = bank(pbank, BF16)
            for kc in range(KC):
                P.tr(pb[:, kc * 128:(kc + 1) * 128], xn[:, kc * 128:(kc + 1) * 128], ident[:])
            P.tt("dve", dst, pb.rearrange("p (k t) -> p k t", t=128), _bc(wcol[:], 2, 128), ALU.mult)

        def norm_transpose(src, xn, wcol, dst, pbank):
            norm_prep(src, xn)
            pb = bank(pbank, BF16)
            for kc in range(KC):
                P.tr(pb[:, kc * 128:(kc + 1) * 128], xn[:, kc * 128:(kc + 1) * 128], ident[:])
            P.tt("dve", dst, pb.rearrange("p (k t) -> p k t", t=128), _bc(wcol[:], 2, 128), ALU.mult)

        pb7 = bank(7, BF16)

        def stA0(t):
            P.dma("sp", XT[t % 2], x[t * 128:(t + 1) * 128, :])
            norm_prep(XT[t % 2], XN[t % 2])

        def stA1(t):
            norm_tr(XN[t % 2], wcol_mix, HTt[t % 2], 6)

        def proj(t, b, c0, n=512, col0=0):
            htt = HTt[t % 2]
            for kc in range(KC):
                P.mm(bank(b)[:, col0:col0 + n], htt[:, kc, :], w_in_sb[:, kc, c0:c0 + n],
                     start=(kc == 0), stop=(kc == KC - 1))

        def stB1(t):
            proj(t, 0, 0)
            proj(t, 1, 512)

        def stC1(t):
            qn = QN[t % 2]
            for i, b in enumerate((0, 1)):
                P.act(SQ[i], bank(b), AF.Square)
                P.reduce("dve", st2[:, i * 8:(i + 1) * 8], SQ[i].rearrange("p (g d) -> p g d", d=64))
            rms_stats(st2[:, 0:16], 64.0, st2[:, 16:32], 16)
            for i, b in enumerate((0, 1)):
                P.tt("dve", qn[:, i * 512:(i + 1) * 512].rearrange("p (g d) -> p g d", d=64),
                     bank(b).rearrange("p (g d) -> p g d", d=64),
                     _bc(st2[:, 16 + i * 8:24 + i * 8], 2, 64), ALU.mult)
                P.tt("dve", ropeA[:, i * 8:(i + 1) * 8, :],
                     bank(b).rearrange("p (g d) -> p g d", d=64)[:, :, 0:16],
                     _bc(st2[:, 16 + i * 8:24 + i * 8], 2, 16), ALU.mult)
            P.tt("pool", ropeA[:, 0:8, :], ropeA[:, 0:8, :], _bc(w16q[:], 1, 8), ALU.mult)
            P.tt("pool", ropeA[:, 8:16, :], ropeA[:, 8:16, :], _bc(w16k[:], 1, 8), ALU.mult)
            cs = _bc(cosT[:, t, :], 1, 16)
            sn = _bc(sinT[:, t, :], 1, 16)
            qv = qn[:, 0:1024].rearrange("p (g d) -> p g d", d=64)
            P.tt("pool", ropeT[0], ropeA[:, :, 0:8], cs, ALU.mult)
            P.tt("pool", ropeT[1], ropeA[:, :, 8:16], sn, ALU.mult)
            P.tt("pool", ropeT[2], ropeA[:, :, 8:16], cs, ALU.mult)
            P.tt("pool", ropeT[3], ropeA[:, :, 0:8], sn, ALU.mult)
            P.tt("pool", qv[:, :, 0:8], ropeT[0], ropeT[1], ALU.subtract)
            P.tt("pool", qv[:, :, 8:16], ropeT[2], ropeT[3], ALU.add)

        def stB2(t):
            proj(t, 2, 1536)
            proj(t, 3, 2048)

        def stC2(t):
            qn = QN[t % 2]
            for i, b in enumerate((2, 3)):
                P.act(SQ[i], bank(b), AF.Square)
                P.reduce("dve", st2[:, 32 + i * 8:40 + i * 8], SQ[i].rearrange("p (g d) -> p g d", d=64))
            rms_stats(st2[:, 32:48], 64.0, st2[:, 48:64], 16)
            for i, b in enumerate((2, 3)):
                P.tt("dve", qn[:, 1024 + i * 512:1536 + i * 512].rearrange("p (g d) -> p g d", d=64),
                     bank(b).rearrange("p (g d) -> p g d", d=64),
                     _bc(st2[:, 48 + i * 8:56 + i * 8], 2, 64), ALU.mult)

        def stD(t):
            qn = QN[t % 2]
            tcols = slice(t * 128, (t + 1) * 128)
            for (c0, dst, wc) in ((0, QTd, wq_adj), (512, KTd, wk_adj), (1024, QTf, wfq), (1536, KTf, wfk)):
                for j in range(4):
                    P.tr(pb7[:, j * 128:(j + 1) * 128], qn[:, c0 + j * 128:c0 + (j + 1) * 128], ident[:])
                P.ts("dve", dst[:, :, tcols], pb7[:, 0:512].rearrange("p (h t) -> p h t", t=128),
                     wc[:, 0:1], ALU.mult)

        def stB3(t):
            proj(t, 4, 1024)
            proj(t, 5, 2560)
            proj(t, 7, 3072, n=8, col0=256)

        def stC3(t):
            P.copy("act", Vd[:, t, :, 0:128], bank(4).rearrange("p (h d) -> p h d", d=128))
            P.copy("act", Vf[:, t, :, 0:64], bank(5).rearrange("p (h d) -> p h d", d=64))
            P.tt("dve", zt[:], bank(7)[:, 256:264], bfg[:], ALU.add)
            P.act(et[:], zt[:], AF.Exp, scale=-1.0)
            P.act(nls[:, t, :], et[:], AF.Ln, bias=1.0)

        stA0(0)
        stA1(0)
        for t in range(NT):
            if t + 1 < NT:
                stA0(t + 1)
            stB1(t)
            if t + 1 < NT:
                stA1(t + 1)
            stC1(t)
            stB2(t)
            stC2(t)
            if t >= 1:
                stD(t - 1)
            stB3(t)
            stC3(t)
        stD(NT - 1)

        nls2 = nls[:].rearrange("p t h -> p (t h)")
        P.mm(bank(7)[:, 0:128], tri[:], nls2, True, True)
        P.mm(bank(6)[:, 0:128], ones[:], nls2, True, True)
        P.copy("dve", Tb[:].rearrange("p t h -> p (t h)"), bank(6)[:, 0:128])
        P.copy("dve", Pinc[:, 0, :], Tb[:, 0, :])
        for j in range(1, NT):
            P.tt("dve", Pinc[:, j, :], Pinc[:, j - 1, :], Tb[:, j, :], ALU.add)
        P.tt("dve", gtmp[:], Pinc[:], Tb[:], ALU.subtract)
        P.tt("dve", gcol[:].rearrange("p t h -> p (t h)"), bank(7)[:, 0:128],
             gtmp[:].rearrange("p t h -> p (t h)"), ALU.add)
        Gmid = Pinc[:].rearrange("p (c four) h -> p c four h", four=4)[:, :, 1, :]
        for kb in range(NT):
            P.tt("dve", biasT[:, kb, 0:4, :], _bc(gcol[:, kb, :], 1, 4), Gmid, ALU.subtract)

        mixedT = _view(R2, 0, [128, KC, S], BF16)
        MIXTOK = [_view(R2, 32768, [128, 4, D], BF16), _view(R2, 122880, [128, 4, D], BF16)]
        PT = [_view(R2, 40960 + i * 1024, [128, 512], BF16) for i in range(3)]
        w_out_sb = _view(R2, 83712, [128, KC, D], BF16)
        e0 = 83712 + 16384
        XT4 = [_view(R2, e0 + i * 4096, [128, D], F32) for i in range(2)]
        QDP = _view(R2, e0, [128, 4, 2, 512], BF16)
        QFP = _view(R2, e0 + 8192, [128, 4, 2, 512], BF16)
        e1 = e0 + 16384
        EPB = [[_view(R2, e1 + k * 2048, [128, 4, 128], F32) for k in range(3)] for i in range(2)]
        assert e1 + 3 * 2048 <= 122880
        P.memset("pool", QDP[:], 0.0)
        P.memset("pool", QFP[:], 0.0)

        def mk_qpad(c, fox):
            def fn():
                for m in range(2):
                    rows = slice(m * 64, (m + 1) * 64)
                    if fox:
                        P.copy("dve", QFP[rows, :, m, :], QTf[rows, :, c * 512:(c + 1) * 512])
                    else:
                        P.copy("dve", QDP[rows, :, m, :], QTd[rows, :, c * 512:(c + 1) * 512])
            return fn

        mk_qpad(0, False)()
        mk_qpad(0, True)()

        for kc in range(KC):
            P.dma("pool", w_out_sb[:, kc, :], w_out[kc * 128:(kc + 1) * 128, :])

        steps = []
        deferred = {}

        def defer(idx, fn):
            deferred.setdefault(idx, []).append(fn)

        def mk_diff_step(i, c, h, m, kb, accv):
            rows = slice(m * 64, (m + 1) * 64)
            qlo = max(kb, 4 * c)
            j0 = qlo - 4 * c
            ncol = (4 - j0) * 128
            sb_ = 4 + i % 3
            pt = PT[i % 3]

            def st_fn():
                P.mm(bank(sb_)[:, 0:ncol], KTd[:, h, kb * 128:(kb + 1) * 128],
                     QDP[:, h, m, j0 * 128:512], True, True)

            def rest_fn():
                P.act(pt[:, 0:ncol], bank(sb_)[:, 0:ncol], AF.Exp, scale=0.125)
                if kb >= 4 * c:
                    P.memset("pool", pt[64:128, 0:64], 0.0)
                for j in range(j0, 4):
                    P.mm(accv[:, j, 0:129], pt[:, (j - j0) * 128:(j - j0 + 1) * 128],
                         Vd[:, kb, h, 0:129], start=(kb == 0 and j in (0, 2)),
                         stop=(kb == 4 * c + j), skip=True)
            return st_fn, rest_fn

        def mk_fox_step(i, c, h, kb, accv):
            pr = h // 2
            qlo = max(kb, 4 * c)
            j0 = qlo - 4 * c
            ncol = (4 - j0) * 128
            sb_ = 4 + i % 3
            pt = PT[i % 3]

            def st_fn():
                P.mm(bank(sb_)[:, 0:ncol], KTf[:, pr, kb * 128:(kb + 1) * 128],
                     QFP[:, pr, h % 2, j0 * 128:512], True, True)

            def rest_fn():
                P.act(pt[:, 0:ncol], bank(sb_)[:, 0:ncol], AF.Exp, scale=0.125,
                      bias=biasT[:, kb, c, h:h + 1])
                if kb >= 4 * c:
                    P.affsel(pt[:, 0:128], pt[:, 0:128], [[1, 128]], ALU.is_ge, 0.0, 0, -1)
                for j in range(j0, 4):
                    P.mm(accv[:, j, 0:65], pt[:, (j - j0) * 128:(j - j0 + 1) * 128],
                         Vf[:, kb, h, 0:65], start=(kb == 0 and j == 0),
                         stop=(kb == 4 * c + j), skip=True)
            return st_fn, rest_fn

        def mk_diff_ep(c, h, accs, par):
            epT, epU, epS = EPB[par]
            mixtok = MIXTOK[c % 2]
            so = 8 + 16 * (h % 2)

            def ep0():
                P.recip(st1[:, so:so + 4], accs[0][:, :, 128])
                P.tt("dve", epU[:], accs[0][:, :, 0:128], _bc(st1[:, so:so + 4], 2, 128), ALU.mult)

            def ep1():
                P.recip(st1[:, so + 4:so + 8], accs[1][:, :, 128])
                P.ts("dve", st1[:, so + 4:so + 8], st1[:, so + 4:so + 8], lamt[:, 4:5], ALU.mult)
                P.tt("dve", epT[:], accs[1][:, :, 0:128], _bc(st1[:, so + 4:so + 8], 2, 128), ALU.mult)
                P.tt("dve", epU[:], epU[:], epT[:], ALU.add)
                P.tt("pool", epS[:], epU[:], epU[:], ALU.mult)
                P.reduce("dve", st1[:, so + 8:so + 12], epS[:])

            def ep2():
                rms_stats(st1[:, so + 8:so + 12], 128.0, st1[:, so + 12:so + 16], 4)
                P.tt("dve", epU[:], epU[:], _bc(st1[:, so + 12:so + 16], 2, 128), ALU.mult)
                P.tt("pool", mixtok[:, :, h * 128:(h + 1) * 128], epU[:], _bc(subln[:], 1, 4), ALU.mult)
            return ep0, ep1, ep2

        def mk_fox_ep(c, h, accv, par):
            mixtok = MIXTOK[c % 2]
            so = 40 + 4 * par

            def ep():
                P.recip(st1[:, so:so + 4], accv[:, :, 64])
                P.tt("dve", mixtok[:, :, 512 + h * 64:512 + (h + 1) * 64],
                     accv[:, :, 0:64], _bc(st1[:, so:so + 4], 2, 64), ALU.mult)
            return ep

        def mk_mix_tr(c):
            mixtok = MIXTOK[c % 2]

            def fn():
                for tl in range(4):
                    for kc in range(KC):
                        P.tr(pb7[:, kc * 128:(kc + 1) * 128], mixtok[:, tl, kc * 128:(kc + 1) * 128], ident[:])
                    tg = c * 4 + tl
                    P.copy("dve", mixedT[:, :, tg * 128:(tg + 1) * 128], pb7.rearrange("p (k t) -> p k t", t=128))
            return fn

        nfox = 0
        for c in range(4):
            for h in range(4):
                accs = [PS[:, m * 1024:(m + 1) * 1024].rearrange("p (q n) -> p q n", n=256) for m in range(2)]
                ep0, ep1, ep2 = mk_diff_ep(c, h, accs, 0)
                for m in range(2):
                    for kb in range(4 * c + 4):
                        steps.append(mk_diff_step(len(steps), c, h, m, kb, accs[m]))
                    if m == 0:
                        defer(len(steps) - 1, ep0)
                defer(len(steps) - 1, ep1)
                defer(len(steps) - 1, ep2)
            if c + 1 < 4:
                defer(len(steps) - 1, mk_qpad(c + 1, False))
            for h in range(8):
                accv = bank(nfox % 4).rearrange("p (q n) -> p q n", n=128)
                for kb in range(4 * c + 4):
                    steps.append(mk_fox_step(len(steps), c, h, kb, accv))
                defer(len(steps) - 1, mk_fox_ep(c, h, accv, nfox % 2))
                nfox += 1
            if c + 1 < 4:
                defer(len(steps) - 1, mk_qpad(c + 1, True))
            defer(len(steps) - 1 + 8, mk_mix_tr(c))

        LA = 2
        nst = len(steps)
        endi = max(nst, max(deferred) + 1)
        for i in range(endi + LA):
            if i < nst:
                steps[i][0]()
            j = i - LA
            if j >= 0:
                if j < nst:
                    steps[j][1]()
                for fn in deferred.get(j, []):
                    fn()

        o5 = 83712
        w_kv_sb = _view(R2, 50176, [128, KC, 2 * D], BF16)
        w_mq_sb = _view(R2, 32768, [128, KC, D], BF16)
        w_mo_sb = _view(R2, 0, [128, KC, D], BF16)
        for kc in range(KC):
            P.dma("pool", w_kv_sb[:, kc, :], w_mem_kv[kc * 128:(kc + 1) * 128, :])
        for kc in range(KC):
            P.dma("pool", w_mq_sb[:, kc, :], w_mem_q[kc * 128:(kc + 1) * 128, :])

        for t in range(NT):
            xt = XT4[t % 2]
            P.dma("sp", xt, x[t * 128:(t + 1) * 128, :])
            for hf in range(2):
                b = 2 * (t % 2) + hf
                for kc in range(KC):
                    P.mm(bank(b), mixedT[:, kc, t * 128:(t + 1) * 128], w_out_sb[:, kc, hf * 512:(hf + 1) * 512],
                         start=(kc == 0), stop=(kc == KC - 1))
                P.tt("dve", xres[:, t, hf * 512:(hf + 1) * 512], bank(b), xt[:, hf * 512:(hf + 1) * 512], ALU.add)

        for kc in range(KC):
            P.dma("pool", w_mo_sb[:, kc, :], w_mem_o[kc * 128:(kc + 1) * 128, :])
        hmT = _view(R2, o5, [128, KC, MEM], BF16)
        mkT = _view(R2, o5 + 4096, [128, 8, MEM], BF16)
        mv = _view(R2, o5 + 8192, [128, 2, 4, 258], BF16)
        o5b = o5 + 8192 + 4128
        MT = [_view(R2, o5b + i * 4096, [128, D], F32) for i in range(2)]
        SQ5 = _view(R2, o5b + 8192, [128, D], F32)
        PT5 = [_view(R2, o5b + 12288 + i * 1024, [128, 512], BF16) for i in range(2)]
        motok = _view(R2, o5b + 14336, [128, 4, D], BF16)
        XN5 = [_view(R2, o5b + 22528 + i * 2048, [128, D], BF16) for i in range(2)]
        MQN = [_view(R2, o5b + 26624 + i * 2048, [128, D], BF16) for i in range(2)]
        MQT = [_view(R2, 16384 + i * 8192, [128, 8, 512], BF16) for i in range(2)]
        moT = _view(R2, o5b, [128, KC, 512], BF16)
        assert o5b + 30720 <= 131072

        P.memset("pool", mv[:, :, :, 256:258], 1.0)

        def head_norm(src_banks, sq, dst_bf, stcol):
            for i, b in enumerate(src_banks):
                P.act(sq[:, i * 512:(i + 1) * 512], bank(b), AF.Square)
            P.reduce("dve", st2[:, stcol:stcol + 4], sq.rearrange("p (g d) -> p g d", d=256))
            rms_stats(st2[:, stcol:stcol + 4], 256.0, st2[:, stcol + 4:stcol + 8], 4)
            for i, b in enumerate(src_banks):
                P.tt("dve", dst_bf[:, i * 512:(i + 1) * 512].rearrange("p (g d) -> p g d", d=256),
                     bank(b).rearrange("p (g d) -> p g d", d=256),
                     _bc(st2[:, stcol + 4 + 2 * i:stcol + 6 + 2 * i], 2, 256), ALU.mult)

        for mt in range(2):
            mtile = MT[mt]
            P.dma("sp", mtile, mem[mt * 128:(mt + 1) * 128, :])
            norm_transpose(mtile, XN5[mt], wcol_memkv, hmT[:, :, mt * 128:(mt + 1) * 128], 6)
        for mt in range(2):
            for g in range(4):
                for kc in range(KC):
                    P.mm(bank(g), hmT[:, kc, mt * 128:(mt + 1) * 128], w_kv_sb[:, kc, g * 512:(g + 1) * 512],
                         start=(kc == 0), stop=(kc == KC - 1))
            head_norm((0, 1), SQ5, MQN[mt], 0)
            P.copy("act", mv[:, mt, 0:2, 0:256], bank(2).rearrange("p (h d) -> p h d", d=256))
            P.copy("act", mv[:, mt, 2:4, 0:256], bank(3).rearrange("p (h d) -> p h d", d=256))
            pb7 = bank(7, BF16)
            for j in range(8):
                P.tr(pb7[:, j * 128:(j + 1) * 128], MQN[mt][:, j * 128:(j + 1) * 128], ident[:])
            pv = pb7.rearrange("p (h f t) -> p h f t", f=2, t=128)
            dv_ = mkT[:, :, mt * 128:(mt + 1) * 128].rearrange("p (h f) t -> p h f t", f=2)
            for f in range(2):
                P.ts("dve", dv_[:, :, f, :], pv[:, :, f, :], wcol_mk[:, f:f + 1], ALU.mult)

        SQH = [SQ5[:, 0:512], SQ5[:, 512:1024]]
        XN5b = [_view(R2, 50176 + i * 2048, [128, D], BF16) for i in range(3)]
        HQTb = [_view(R2, 50176 + 6144 + i * 2048, [128, KC, 128], BF16) for i in range(3)]
        MQNb = [_view(R2, 50176 + 12288 + i * 2048, [128, D], BF16) for i in range(3)]
        QF = [_view(R2, 50176 + 18432 + i * 2048, [128, 512], F32) for i in range(3)]
        MOTOK = [motok, _view(R2, 50176 + 24576, [128, 4, D], BF16)]
        MOT = [moT, _view(R2, o5b + 22528, [128, KC, 512], BF16)]
        accv5 = PS[:, 4 * 512:6 * 512].rearrange("p (q n) -> p q n", n=256)
        def q_stage(c):
            mq = MQT[c % 2]
            for tl in range(4):
                t = c * 4 + tl
                xn, hq, mqn = XN5b[t % 3], HQTb[t % 3], MQNb[t % 3]
                norm_prep(xres[:, t, :], xn)
                norm_tr(xn, wcol_memq, hq, 6)
                for g in range(2):
                    qf = QF[(2 * t + g) % 3]
                    for kc in range(KC):
                        P.mm(bank(g), hq[:, kc, :], w_mq_sb[:, kc, g * 512:(g + 1) * 512],
                             start=(kc == 0), stop=(kc == KC - 1))
                    s0 = 16 * (t % 2) + 4 * g
                    P.memset("dve", st2[:, s0:s0 + 2], 0.0)
                    for hh in range(2):
                        P.act(SQH[g][:, hh * 256:(hh + 1) * 256], bank(g)[:, hh * 256:(hh + 1) * 256], AF.Square,
                              accum_out=st2[:, s0 + hh:s0 + hh + 1])
                    P.copy("act", qf, bank(g))
                    rms_stats(st2[:, s0:s0 + 2], 256.0, st2[:, s0 + 2:s0 + 4], 2)
                    P.tt("dve", mqn[:, g * 512:(g + 1) * 512].rearrange("p (g d) -> p g d", d=256),
                         qf.rearrange("p (g d) -> p g d", d=256),
                         _bc(st2[:, s0 + 2:s0 + 4], 2, 256), ALU.mult)
                pbq = bank(7, BF16)
                for j in range(8):
                    P.tr(pbq[:, j * 128:(j + 1) * 128], mqn[:, j * 128:(j + 1) * 128], ident[:])
                pv = pbq.rearrange("p (h f t) -> p h f t", f=2, t=128)
                dv_ = mq[:, :, tl * 128:(tl + 1) * 128].rearrange("p (h f) t -> p h f t", f=2)
                for f in range(2):
                    P.ts("dve", dv_[:, :, f, :], pv[:, :, f, :], wcol_mq[:, f:f + 1], ALU.mult)
        def heads_stage(c):
            mq = MQT[c % 2]
            motok = MOTOK[c % 2]
            moT = MOT[c % 2]
            for h in range(4):
                sumv = bank(2)[:, 8 * h:8 * h + 4]
                for mt in range(2):
                    pt = PT5[mt]
                    for f in range(2):
                        P.mm(bank(7), mkT[:, 2 * h + f, mt * 128:(mt + 1) * 128], mq[:, 2 * h + f, :],
                             start=(f == 0), stop=(f == 1))
                    P.act(pt[:], bank(7), AF.Exp, scale=1.0 / 16.0)
                    for tl in range(4):
                        P.mm(accv5[:, tl, :], pt[:, tl * 128:(tl + 1) * 128], mv[:, mt, h, 0:256],
                             start=(mt == 0 and tl in (0, 2)), stop=(mt == 1), skip=True)
                        P.mm(sumv[:, tl:tl + 1], pt[:, tl * 128:(tl + 1) * 128], mv[:, mt, h, 256:257],
                             start=(mt == 0 and tl == 0 and h == 0), stop=(mt == 1), skip=True)
                so = 32 + 4 * (h % 2)
                P.recip(st1[:, so:so + 4], sumv)
                P.tt("dve", motok[:, :, h * 256:(h + 1) * 256], accv5, _bc(st1[:, so:so + 4], 2, 256), ALU.mult)
            pb6 = bank(6, BF16)
            for tl in range(4):
                for kc in range(KC):
                    P.tr(pb6[:, kc * 128:(kc + 1) * 128], motok[:, tl, kc * 128:(kc + 1) * 128], ident[:])
                P.copy("act", moT[:, :, tl * 128:(tl + 1) * 128], pb6.rearrange("p (k t) -> p k t", t=128))
            for tl in range(4):
                t = c * 4 + tl
                for hf in range(2):
                    for kc in range(KC):
                        P.mm(bank(3), moT[:, kc, tl * 128:(tl + 1) * 128], w_mo_sb[:, kc, hf * 512:(hf + 1) * 512],
                             start=(kc == 0), stop=(kc == KC - 1))
                    P.tt("dve", xres[:, t, hf * 512:(hf + 1) * 512], bank(3),
                         xres[:, t, hf * 512:(hf + 1) * 512], ALU.add)

        q_stage(0)
        for c in range(4):
            if c + 1 < 4:
                q_stage(c + 1)
            heads_stage(c)

        hT = _view(R2, 0, [128, KC, S], BF16)
        WU = [_view(R2, 32768 + i * 16384, [128, KC, 1024], BF16) for i in range(2)]
        WD = [_view(R2, 65536 + i * 16384, [128, 8, 1024], BF16) for i in range(2)]
        AT = [_view(R2, 98304 + i * 8192, [128, 8, 512], BF16) for i in range(2)]
        OUTS = [_view(R2, 114688 + i * 4096, [128, D], F32) for i in range(2)]
        RL = [_view(R2, 122880 + i * 2048, [128, 512], F32) for i in range(2)]
        XN6 = [_view(R2, 126976 + i * 2048, [128, D], BF16) for i in range(2)]

        def load_mlp_w(qf):
            for kc in range(KC):
                P.dma("pool", WU[qf % 2][:, kc, :], w_up[kc * 128:(kc + 1) * 128, qf * 1024:(qf + 1) * 1024])
            for fc in range(8):
                r0 = qf * 1024 + fc * 128
                P.dma("pool", WD[qf % 2][:, fc, :], w_down[r0:r0 + 128, :])

        load_mlp_w(0)
        for t in range(NT):
            norm_transpose(xres[:, t, :], XN6[t % 2], wcol_mlp, hT[:, :, t * 128:(t + 1) * 128], 6 + t % 2)
        load_mlp_w(1)
        cnt = 0
        for qf in range(4):
            wu, wd = WU[qf % 2], WD[qf % 2]
            if qf in (1, 2):
                pass
            for c in range(4):
                at = AT[c % 2]
                for fcl in range(8):
                    b = 4 + cnt % 2
                    rl = RL[cnt % 2]
                    cnt += 1
                    for kc in range(KC):
                        P.mm(bank(b), wu[:, kc, fcl * 128:(fcl + 1) * 128], hT[:, kc, c * 512:(c + 1) * 512],
                             start=(kc == 0), stop=(kc == KC - 1))
                    P.act(rl[:], bank(b), AF.Relu)
                    P.tt("dve", at[:, fcl, :], rl[:], rl[:], ALU.mult)
                for tl in range(4):
                    t = c * 4 + tl
                    for hf in range(2):
                        b = 2 * (tl % 2) + hf
                        for fcl in range(8):
                            P.mm(bank(b), at[:, fcl, tl * 128:(tl + 1) * 128], wd[:, fcl, hf * 512:(hf + 1) * 512],
                                 start=(fcl == 0), stop=(fcl == 7))
                        if qf < 3:
                            P.tt("dve", xres[:, t, hf * 512:(hf + 1) * 512], bank(b),
                                 xres[:, t, hf * 512:(hf + 1) * 512], ALU.add)
                        else:
                            P.tt("dve", OUTS[t % 2][:, hf * 512:(hf + 1) * 512], bank(b),
                                 xres[:, t, hf * 512:(hf + 1) * 512], ALU.add)
                    if qf == 3:
                        P.dma("sp", y[t * 128:(t + 1) * 128, :], OUTS[t % 2], is_out=True)
            if qf + 2 < 4:
                load_mlp_w(qf + 2)

        with nc.Block() as block:
            P.emit(block, sem_ctx)
    return nc


_NC_CACHE = {}


def kernel(**inputs):
    if "nc" not in _NC_CACHE:
        _NC_CACHE["nc"] = build_program()
    nc = _NC_CACHE["nc"]
    f32 = lambda a: np.ascontiguousarray(np.asarray(a, dtype=np.float32))
    shared = {}
    for name in ("norm_mix_w", "w_in", "b_forget", "diff_q_norm_w", "diff_k_norm_w", "lambda_q1", "lambda_k1",
                 "lambda_q2", "lambda_k2", "diff_subln_w", "fox_q_norm_w", "fox_k_norm_w", "w_out",
                 "norm_mem_q_w", "norm_mem_kv_w", "w_mem_q", "w_mem_kv", "mem_q_norm_w", "mem_k_norm_w",
                 "w_mem_o", "norm_mlp_w", "w_up", "w_down"):
        a = f32(inputs[name])
        if name in ("w_in", "w_out", "w_mem_q", "w_mem_kv", "w_mem_o", "w_up", "w_down"):
            shared[name] = np.ascontiguousarray(a[0])
        else:
            shared[name] = np.ascontiguousarray(a.reshape(1, -1))
    x = f32(inputs["x"])
    mem = f32(inputs["mem"])
    pos = np.ascontiguousarray(np.asarray(inputs["positions"], dtype=np.int32))
    in_maps = []
    for b in range(N_CORES):
        m = dict(shared)
        m["x"] = np.ascontiguousarray(x[b])
        m["mem"] = np.ascontiguousarray(mem[b])
        m["positions"] = np.ascontiguousarray(pos[b].reshape(1, S))
        in_maps.append(m)
    res = run_bass_kernel_spmd(nc, in_maps, core_ids=list(range(N_CORES)))
    out = np.stack([np.asarray(r["y"], dtype=np.float32) for r in res.results], axis=0)
    return out
```
